# Optimizing a Trainium2 kernel written in Bass

```python
import math
import jax, jax.numpy as jnp
from jax import lax
import numpy as np

D_MODEL = 2048
BATCH = 4
SEQ = 8192
DEPTH = 2

PLE_DIM = 256
D_FF = 5632
MIX_WIDTH = D_MODEL
GROUP_WIDTH = MIX_WIDTH // 4

A_HEADS = 4
A_HEAD_DIM = GROUP_WIDTH // (2 * A_HEADS)
A_QBLOCK = 128
N_BUCKETS = 32
MAX_DISTANCE = 128

B_HEADS = 4
B_HEAD_DIM = GROUP_WIDTH // B_HEADS
B_CHUNK = 128
ROPE_BASE = 10000.0

C_GROUPS = 4
C_CHUNK = 128
C_GROUP_DIM = GROUP_WIDTH // C_GROUPS

D_HEADS = 4
D_EXPAND = 128
D_HEAD_DIM = GROUP_WIDTH // D_HEADS
D_CHUNK = 64

A_COLS = 3 * GROUP_WIDTH
B_COLS = 4 * GROUP_WIDTH
C_COLS = 2 * GROUP_WIDTH
D_COLS = 4 * GROUP_WIDTH
IN_COLS = A_COLS + B_COLS + C_COLS + D_COLS

ALPHA = (2 * DEPTH) ** 0.25
BETA = (8 * DEPTH) ** -0.25
LN_EPS = 1e-5
MASK_VALUE = -1e30
LB_FLOOR = 1e-30

kernel_name = "hybrid_parallel_heads_diffattn_retnet_gmlp_hgrn2"


def layer_norm(x, g, b):
    xf = x.astype(jnp.float32)
    mu = jnp.mean(xf, axis=-1, keepdims=True)
    var = jnp.mean(jnp.square(xf - mu), axis=-1, keepdims=True)
    return ((xf - mu) * lax.rsqrt(var + LN_EPS) * g.astype(jnp.float32) + b.astype(jnp.float32)).astype(x.dtype)


def head_norm(x):
    xf = x.astype(jnp.float32)
    mu = jnp.mean(xf, axis=-1, keepdims=True)
    var = jnp.mean(jnp.square(xf - mu), axis=-1, keepdims=True)
    return ((xf - mu) * lax.rsqrt(var + LN_EPS)).astype(x.dtype)


def rms_norm(x, g):
    xf = x.astype(jnp.float32)
    return (xf * lax.rsqrt(jnp.mean(xf * xf, axis=-1, keepdims=True) + LN_EPS) * g.astype(jnp.float32)).astype(x.dtype)


def swiglu(x, w_in, w_out):
    gate, up = jnp.split(x @ w_in, 2, axis=-1)
    return (jax.nn.silu(gate) * up) @ w_out


def t5_bucket(n):
    n = jnp.maximum(n, 0)
    max_exact = N_BUCKETS // 2
    nf = jnp.maximum(n, 1).astype(jnp.float32)
    large = max_exact + (jnp.log(nf / max_exact) / math.log(MAX_DISTANCE / max_exact)
                         * (N_BUCKETS - max_exact)).astype(jnp.int32)
    large = jnp.minimum(large, N_BUCKETS - 1)
    return jnp.where(n < max_exact, n, large)


def rotary(x, positions):
    d = x.shape[-1]
    inv = ROPE_BASE ** (-jnp.linspace(0.0, 1.0, d // 2, dtype=jnp.float32))
    ang = positions.astype(jnp.float32)[..., None] * inv
    cos, sin = jnp.cos(ang)[:, :, None, :], jnp.sin(ang)[:, :, None, :]
    xf = x.astype(jnp.float32)
    x1, x2 = xf[..., 0::2], xf[..., 1::2]
    out = jnp.stack([x1 * cos - x2 * sin, x1 * sin + x2 * cos], axis=-1)
    return out.reshape(x.shape).astype(x.dtype)


def diff_attention(q, k, v, positions, rel_bias, lam, lam_init, norm_g):
    bsz, seq = q.shape[:2]
    nb = seq // A_QBLOCK
    scale = A_HEAD_DIM ** -0.5
    qb = jnp.moveaxis(q.reshape(bsz, nb, A_QBLOCK, A_HEADS, 2, A_HEAD_DIM), 1, 0)
    pb = jnp.moveaxis(positions.reshape(bsz, nb, A_QBLOCK), 1, 0)
    starts = jnp.arange(nb, dtype=jnp.int32) * A_QBLOCK
    k_idx = jnp.arange(seq, dtype=jnp.int32)

    def block(args):
        qi, pi, s0 = args
        logits = jnp.einsum('bqhcd,bkhcd->bhcqk', qi, k).astype(jnp.float32) * scale
        rel = pi[:, :, None] - positions[:, None, :]
        bias = jnp.moveaxis(rel_bias.astype(jnp.float32)[t5_bucket(rel)], -1, 1)
        q_idx = s0 + jnp.arange(A_QBLOCK, dtype=jnp.int32)
        causal = q_idx[:, None] >= k_idx[None, :]
        logits = jnp.where(causal, logits + bias[:, :, None], MASK_VALUE)
        probs = jax.nn.softmax(logits, axis=-1)
        w = probs[:, :, 0] - lam * probs[:, :, 1]
        return jnp.einsum('bhqk,bkhd->bqhd', w.astype(v.dtype), v)

    o = lax.map(block, (qb, pb, starts))
    o = jnp.moveaxis(o, 0, 1).reshape(bsz, seq, A_HEADS, 2 * A_HEAD_DIM)
    o = rms_norm(o, norm_g) * (1.0 - lam_init)
    return o.reshape(bsz, seq, GROUP_WIDTH)


def retention(q, k, v, g):
    bsz, seq = q.shape[:2]
    n = seq // B_CHUNK
    log_g = jnp.log(1.0 - 2.0 ** (-5.0 - jnp.arange(B_HEADS, dtype=jnp.float32)))
    j = jnp.arange(B_CHUNK, dtype=jnp.float32)
    diff = j[:, None] - j[None, :]
    decay_mask = jnp.where(diff >= 0, jnp.exp(log_g[:, None, None] * jnp.maximum(diff, 0.0)), 0.0)
    q_dec = jnp.exp(log_g[None, :] * (j[:, None] + 1.0))
    k_dec = jnp.exp(log_g[:, None] * (B_CHUNK - 1.0 - j[None, :]))
    chunk_dec = jnp.exp(log_g * B_CHUNK)
    k = k * (B_HEAD_DIM ** -0.5)

    def to_chunks(t):
        return jnp.moveaxis(t.astype(jnp.float32).reshape(bsz, n, B_CHUNK, B_HEADS, -1), 1, 0)

    def step(state, inp):
        qc, kc, vc = inp
        scores = jnp.einsum('bthd,bshd->bhts', qc, kc) * decay_mask
        o = jnp.einsum('bhts,bshd->bthd', scores, vc)
        o = o + jnp.einsum('bthd,bhde->bthe', qc, state) * q_dec[None, :, :, None]
        state = state * chunk_dec[None, :, None, None] + jnp.einsum('bshd,bshe,hs->bhde', kc, vc, k_dec)
        return state, o

    s0 = jnp.zeros((bsz, B_HEADS, B_HEAD_DIM, B_HEAD_DIM), jnp.float32)
    _, o = lax.scan(step, s0, (to_chunks(q), to_chunks(k), to_chunks(v)))
    o = jnp.moveaxis(o, 0, 1).reshape(bsz, seq, B_HEADS, B_HEAD_DIM)
    o = head_norm(o).reshape(bsz, seq, GROUP_WIDTH).astype(v.dtype)
    return o * jax.nn.silu(g)


def spatial_gating(u, v, w_s, b_s, ln_g, ln_b):
    bsz, seq = u.shape[:2]
    n = seq // C_CHUNK
    v = layer_norm(v, ln_g, ln_b)
    vc = v.reshape(bsz, n, C_CHUNK, C_GROUPS, C_GROUP_DIM)
    mask = jnp.tril(jnp.ones((C_CHUNK, C_CHUNK), dtype=w_s.dtype))
    w = w_s * mask[None]
    mixed = jnp.einsum('gts,bnsgc->bntgc', w, vc) + b_s.T[:, :, None]
    return u * mixed.reshape(bsz, seq, GROUP_WIDTH)


def hgrn2(q, f_raw, i_in, g, lb, norm_g):
    bsz, seq = q.shape[:2]
    n = seq // D_CHUNK
    lb = lb.reshape(D_HEADS, D_EXPAND).astype(jnp.float32)
    z = f_raw.astype(jnp.float32)
    log_lb = jnp.log(jnp.maximum(lb, LB_FLOOR))
    log_f = jnp.logaddexp(jax.nn.log_sigmoid(z), log_lb + jax.nn.log_sigmoid(-z))
    key = -jnp.expm1(log_f)
    causal = jnp.tril(jnp.ones((D_CHUNK, D_CHUNK), dtype=bool))

    def to_chunks(t):
        return jnp.moveaxis(t.astype(jnp.float32).reshape(bsz, n, D_CHUNK, D_HEADS, -1), 1, 0)

    def step(state, inp):
        qc, kc, vc, lfc = inp
        b = jnp.cumsum(lfc, axis=1)
        o_inter = jnp.einsum('bthk,bhkv->bthv', qc * jnp.exp(b), state)
        rel = jnp.where(causal[None, :, :, None, None], b[:, :, None] - b[:, None, :], MASK_VALUE)
        a = jnp.einsum('bthk,bshk,btshk->bhts', qc, kc, jnp.exp(rel))
        o_intra = jnp.einsum('bhts,bshv->bthv', a, vc)
        b_last = b[:, -1]
        state = state * jnp.exp(b_last)[..., None] + jnp.einsum(
            'bshk,bshv->bhkv', kc * jnp.exp(b_last[:, None] - b), vc)
        return state, o_inter + o_intra

    s0 = jnp.zeros((bsz, D_HEADS, D_EXPAND, D_HEAD_DIM), jnp.float32)
    _, o = lax.scan(step, s0, (to_chunks(q), to_chunks(key), to_chunks(i_in), to_chunks(log_f)))
    o = jnp.moveaxis(o, 0, 1).reshape(bsz, seq, GROUP_WIDTH).astype(i_in.dtype)
    return rms_norm(o, norm_g) * jax.nn.silu(g)


def token_mixing(x, positions, layer_idx, w_in, w_out, rel_bias, diff_lambda, diff_norm_g,
                 gmlp_ln_g, gmlp_ln_b, gmlp_w_s, gmlp_b_s, lower_bound, hgrn_norm_g):
    bsz, seq = x.shape[:2]
    h = x @ w_in
    a_part, b_part, c_part, d_part = jnp.split(
        h, [A_COLS, A_COLS + B_COLS, A_COLS + B_COLS + C_COLS], axis=-1)

    qa, ka, va = jnp.split(a_part, 3, axis=-1)
    qa = qa.reshape(bsz, seq, A_HEADS, 2, A_HEAD_DIM)
    ka = ka.reshape(bsz, seq, A_HEADS, 2, A_HEAD_DIM)
    va = va.reshape(bsz, seq, A_HEADS, 2 * A_HEAD_DIM)
    lam_init = 0.8 - 0.6 * math.exp(-0.3 * layer_idx)
    lq1, lk1, lq2, lk2 = [diff_lambda[j].astype(jnp.float32) for j in range(4)]
    lam = jnp.exp(jnp.sum(lq1 * lk1)) - jnp.exp(jnp.sum(lq2 * lk2)) + lam_init
    out_a = diff_attention(qa, ka, va, positions, rel_bias, lam, lam_init, diff_norm_g)

    qb, kb, vb, gb = jnp.split(b_part, 4, axis=-1)
    qb = rotary(qb.reshape(bsz, seq, B_HEADS, B_HEAD_DIM), positions)
    kb = rotary(kb.reshape(bsz, seq, B_HEADS, B_HEAD_DIM), positions)
    vb = vb.reshape(bsz, seq, B_HEADS, B_HEAD_DIM)
    out_b = retention(qb, kb, vb, gb)

    uc, vc = jnp.split(jax.nn.gelu(c_part, approximate=False), 2, axis=-1)
    out_c = spatial_gating(uc, vc, gmlp_w_s, gmlp_b_s, gmlp_ln_g, gmlp_ln_b)

    qd, fd, idd, gd = jnp.split(d_part, 4, axis=-1)
    out_d = hgrn2(qd.reshape(bsz, seq, D_HEADS, D_EXPAND), fd.reshape(bsz, seq, D_HEADS, D_EXPAND),
                  idd.reshape(bsz, seq, D_HEADS, D_HEAD_DIM), gd, lower_bound, hgrn_norm_g)

    return jnp.concatenate([out_a, out_b, out_c, out_d], axis=-1) @ w_out


def setup_inputs(seed: int = 0) -> dict:
    key = jax.random.key(seed)
    ks = jax.random.split(key, 24)
    f32 = jnp.float32

    def nrm(k, shape, fan_in, gain=1.0):
        return jax.random.normal(k, shape, f32) * (fan_in ** -0.5) * gain

    x = jax.random.normal(ks[0], (BATCH, SEQ, D_MODEL), f32)
    p = jax.random.normal(ks[1], (DEPTH, BATCH, SEQ, PLE_DIM), f32)
    offset = jax.random.randint(ks[2], (BATCH, 1), 0, 1024, dtype=jnp.int32)
    positions = offset + jnp.arange(SEQ, dtype=jnp.int32)[None, :]
    return {
        "x": x,
        "p": p,
        "positions": positions,
        "ffn1_w_in": nrm(ks[3], (DEPTH, D_MODEL, 2 * D_FF), D_MODEL),
        "ffn1_w_out": nrm(ks[4], (DEPTH, D_FF, D_MODEL), D_FF, BETA),
        "w_mix_in": nrm(ks[5], (DEPTH, D_MODEL, IN_COLS), D_MODEL),
        "w_mix_out": nrm(ks[6], (DEPTH, MIX_WIDTH, D_MODEL), MIX_WIDTH, BETA),
        "rel_bias": 0.1 * jax.random.normal(ks[7], (N_BUCKETS, A_HEADS), f32),
        "diff_lambda": 0.1 * jax.random.normal(ks[8], (DEPTH, 4, A_HEAD_DIM), f32),
        "diff_norm_g": 1.0 + 0.02 * jax.random.normal(ks[9], (DEPTH, 2 * A_HEAD_DIM), f32),
        "gmlp_ln_g": 1.0 + 0.02 * jax.random.normal(ks[10], (DEPTH, GROUP_WIDTH), f32),
        "gmlp_ln_b": 0.02 * jax.random.normal(ks[11], (DEPTH, GROUP_WIDTH), f32),
        "gmlp_w_s": nrm(ks[12], (DEPTH, C_GROUPS, C_CHUNK, C_CHUNK), C_CHUNK),
        "gmlp_b_s": 1.0 + 0.02 * jax.random.normal(ks[13], (DEPTH, C_GROUPS, C_CHUNK), f32),
        "hgrn_lb_logits": 0.5 * jax.random.normal(ks[14], (DEPTH, D_HEADS * D_EXPAND), f32),
        "hgrn_norm_g": 1.0 + 0.02 * jax.random.normal(ks[15], (DEPTH, GROUP_WIDTH), f32),
        "ffn2_w_in": nrm(ks[16], (DEPTH, D_MODEL, 2 * D_FF), D_MODEL),
        "ffn2_w_out": nrm(ks[17], (DEPTH, D_FF, D_MODEL), D_FF, BETA),
        "ple_w_gate": nrm(ks[18], (DEPTH, D_MODEL, D_MODEL), D_MODEL),
        "ple_w_proj": nrm(ks[19], (DEPTH, PLE_DIM, D_MODEL), PLE_DIM, BETA),
        "ln_g": 1.0 + 0.02 * jax.random.normal(ks[20], (DEPTH, 4, D_MODEL), f32),
        "ln_b": 0.02 * jax.random.normal(ks[21], (DEPTH, 4, D_MODEL), f32),
    }


def reference(x, p, positions, ffn1_w_in, ffn1_w_out, w_mix_in, w_mix_out, rel_bias, diff_lambda,
              diff_norm_g, gmlp_ln_g, gmlp_ln_b, gmlp_w_s, gmlp_b_s, hgrn_lb_logits, hgrn_norm_g,
              ffn2_w_in, ffn2_w_out, ple_w_gate, ple_w_proj, ln_g, ln_b):
    lb_soft = jax.nn.softmax(hgrn_lb_logits.astype(jnp.float32), axis=0)
    lower_bounds = jnp.cumsum(lb_soft, axis=0) - lb_soft[0]
    for i in range(DEPTH):
        x = layer_norm(ALPHA * x + 0.5 * swiglu(x, ffn1_w_in[i], ffn1_w_out[i]), ln_g[i, 0], ln_b[i, 0])
        mix = token_mixing(x, positions, i, w_mix_in[i], w_mix_out[i], rel_bias, diff_lambda[i],
                           diff_norm_g[i], gmlp_ln_g[i], gmlp_ln_b[i], gmlp_w_s[i], gmlp_b_s[i],
                           lower_bounds[i], hgrn_norm_g[i])
        x = layer_norm(ALPHA * x + mix, ln_g[i, 1], ln_b[i, 1])
        x = layer_norm(ALPHA * x + 0.5 * swiglu(x, ffn2_w_in[i], ffn2_w_out[i]), ln_g[i, 2], ln_b[i, 2])
        gate = jax.nn.sigmoid(x @ ple_w_gate[i])
        x = layer_norm(ALPHA * x + gate * (p[i] @ ple_w_proj[i]), ln_g[i, 3], ln_b[i, 3])
    return x
```

```python
import math
from contextlib import ExitStack
import numpy as np
import ml_dtypes
import concourse.bass as bass
import concourse.mybir as mybir
from concourse.bass_utils import run_bass_kernel_spmd

F32 = mybir.dt.float32
BF16 = mybir.dt.bfloat16
I32 = mybir.dt.int32
U32 = mybir.dt.uint32
AF = mybir.ActivationFunctionType
ALU = mybir.AluOpType

D = 2048
KC = 16
T = 512
PD = 256
DEPTH = 2
ALPHA = (2 * DEPTH) ** 0.25
EPS = 1e-5
EPS_LN = EPS / (ALPHA * ALPHA)
SLOT = 4096
NSLOT = 4
MIXCOLS = 6656


class Res:
    __slots__ = ("name", "w", "r")

    def __init__(self, name="r"):
        self.name = name
        self.w = None
        self.r = {}


class Sched:
    ENG = ("pe", "act", "dve", "pool", "sp")

    def __init__(self, nc, es, n_dma_sems=20):
        self.nc = nc
        self.sems = {}
        self.cnt = {}
        for e in self.ENG:
            self.sems[e] = es.enter_context(nc.semaphore("s_" + e))
            self.cnt[e] = 0
        self.dma_sems = []
        for i in range(n_dma_sems):
            k = "d%d" % i
            self.sems[k] = es.enter_context(nc.semaphore("s_" + k))
            self.cnt[k] = 0
            self.dma_sems.append(k)
        self.dma_rr = 0
        self.known = {e: {} for e in self.ENG}
        self.prog = {e: [] for e in self.ENG}
        self.n_wait = 0
        self.n_inst = 0

    def _wait(self, e, key, val):
        if val <= 0:
            return
        if e == "pe" and key == "pe":
            return
        kn = self.known[e]
        if kn.get(key, 0) >= val:
            return
        self.prog[e].append(("wait", key, val))
        kn[key] = val
        self.n_wait += 1

    def _deps(self, e, reads, writes):
        need = {}
        for R in reads:
            if R.w is not None:
                k, v = R.w
                if need.get(k, 0) < v:
                    need[k] = v
        for R in writes:
            if R.w is not None:
                k, v = R.w
                if need.get(k, 0) < v:
                    need[k] = v
            for k, v in R.r.items():
                if need.get(k, 0) < v:
                    need[k] = v
        for k, v in need.items():
            self._wait(e, k, v)

    def _mark(self, ev, reads, writes):
        k, v = ev
        for R in writes:
            R.w = ev
            R.r = {}
        for R in reads:
            if R.r.get(k, 0) < v:
                R.r[k] = v

    def op(self, e, name, reads=(), writes=(), inc=True, **kw):
        self._deps(e, reads, writes)
        if inc:
            self.cnt[e] += 1
            self.prog[e].append(("op", name, kw, e, 1))
            self._mark((e, self.cnt[e]), reads, writes)
        else:
            self.prog[e].append(("op", name, kw, None, 0))
            self._mark((e, self.cnt[e] + 1), reads, writes)
        self.n_inst += 1

    def dma(self, q, out, in_, reads=(), writes=(), **kw):
        k = self.dma_sems[self.dma_rr]
        self.dma_rr = (self.dma_rr + 1) % len(self.dma_sems)
        self._wait(q, k, self.cnt[k])
        self._deps(q, reads, writes)
        self.cnt[k] += 16
        kw = dict(kw)
        kw["out"] = out
        kw["in_"] = in_
        self.prog[q].append(("op", "dma_start", kw, k, 16))
        self._mark((k, self.cnt[k]), reads, writes)
        self.n_inst += 1

    def events(self, resources):
        ev = {}
        for R in resources:
            if R.w is not None:
                k, v = R.w
                if ev.get(k, 0) < v:
                    ev[k] = v
            for k, v in R.r.items():
                if ev.get(k, 0) < v:
                    ev[k] = v
        return ev

    def finish(self, e, resources):
        for k, v in self.events(resources).items():
            self._wait(e, k, v)

    def emit(self, block):
        sems = self.sems

        def run(eng, prog):
            for it in prog:
                if it[0] == "wait":
                    eng.wait_ge(sems[it[1]], it[2])
                else:
                    _, name, kw, sk, inc = it
                    inst = getattr(eng, name)(**kw)
                    if sk is not None:
                        inst.then_inc(sems[sk], inc)

        block.tensor(lambda e: run(e, self.prog["pe"]))
        block.scalar(lambda e: run(e, self.prog["act"]))
        block.vector(lambda e: run(e, self.prog["dve"]))
        block.gpsimd(lambda e: run(e, self.prog["pool"]))
        block.sync(lambda e: run(e, self.prog["sp"]))


def t5_bucket_np(n):
    n = np.maximum(n, 0)
    nf = np.maximum(n, 1).astype(np.float32)
    large = 16 + (np.log(nf / np.float32(16)) / np.float32(math.log(128 / 16)) * np.float32(16)).astype(np.int32)
    large = np.minimum(large, 31)
    return np.where(n < 16, n, large)


def host_consts():
    c = {}
    eye = np.eye(128, dtype=np.float32)
    c["c_ident"] = eye
    k = np.arange(128)[:, None]
    q = np.arange(128)[None, :]
    oh = np.zeros((128, 32, 256), np.float32)
    bd = t5_bucket_np(q - k)
    bs = t5_bucket_np(q - k + 128)
    for b in range(32):
        oh[:, b, 0:128] = (bd == b)
        oh[:, b, 128:256] = (bs == b)
    c["c_oh"] = oh
    c["c_negmask"] = np.where(k > q, np.float32(-1e30), np.float32(0)).astype(np.float32)
    c["c_causT"] = (q >= k).astype(np.float32)
    blk = ((k // 64) == (q // 64)) & (q >= k)
    c["c_hgmask"] = blk.astype(np.float32)
    c["c_tril"] = (k >= q).astype(np.float32)
    inv = (10000.0 ** (-np.linspace(0.0, 1.0, 64))).astype(np.float32)
    c["c_invd"] = np.repeat(inv, 2)[:, None].astype(np.float32)
    prot = np.zeros((128, 128), np.float32)
    for i in range(64):
        prot[2 * i + 1, 2 * i] = -1.0
        prot[2 * i, 2 * i + 1] = 1.0
    c["c_prot"] = prot
    g = 1.0 - 2.0 ** (-5.0 - np.arange(4, dtype=np.float64))
    j = np.arange(128, dtype=np.float64)
    gq = np.stack([g[h] ** (j + 1.0) for h in range(4)])
    gk = np.stack([g[h] ** (-(j + 1.0)) * (128.0 ** -0.5) for h in range(4)])
    gs = np.stack([np.full(128, g[h] ** 128.0) for h in range(4)])
    c["c_ret"] = np.concatenate([gq.reshape(1, 512), gk.reshape(1, 512), gs.reshape(1, 512)], 1).astype(np.float32)
    ud = np.zeros((128, 128), np.float32)
    ux = np.zeros((128, 4), np.float32)
    for s in range(128):
        cs, ls = s // 64, s % 64
        for t in range(128):
            ct, lt = t // 64, t % 64
            if cs != ct:
                continue
            if 31 < ls <= lt:
                ud[s, t] = 1.0
            elif lt < ls <= 31:
                ud[s, t] = -1.0
        if ls <= 31:
            ux[s, 2 * cs] = 1.0
        else:
            ux[s, 2 * cs + 1] = 1.0
    c["c_ud"] = ud
    c["c_ux"] = ux
    cm = np.zeros((128, 4), np.float32)
    cm[0:64, 0] = 1.0
    cm[64:128, 1] = 1.0
    cm[0, 2] = 1.0
    cm[1, 3] = 1.0
    c["c_colmask"] = cm
    return c


WMI_ORDER = list(range(26))


def unit_table(F):
    FC = F // 128
    FH = FC // 2
    units = []
    def ffn_units(tagi, tago):
        u = []
        for hf in range(2):
            for jj in range(FH):
                u.append((tagi, hf * FH + jj))
            for m in range(16):
                u.append((tago, hf * 16 + m))
        return u
    units += ffn_units("w1i", "w1o")
    for u in WMI_ORDER:
        units.append(("wmi", u))
    for u in range(8):
        units.append(("wmo", u))
    units += ffn_units("w2i", "w2o")
    for u in range(8):
        units.append(("wpg", u))
    units.append(("wpp", 0))
    return units


def build(S_len, F, stop=99, nlayers=DEPTH, dumpmix=False):
    NT = S_len // T
    FC = F // 128
    FH = FC // 2
    L = DEPTH
    nc = bass.Bass("TRN2", target_bir_lowering=False)

    def din(name, shape, dt=F32):
        return nc.dram_tensor(name, list(shape), dt, kind="ExternalInput").ap()

    x_d = din("x", [S_len, D])
    p_d = din("p", [L, S_len, PD])
    pos_d = din("pos", [1, S_len], I32)
    W = {
        "w1i": din("w1i", [L, D, 2 * F]), "w1o": din("w1o", [L, F, D]),
        "wmi": din("wmi", [L, D, MIXCOLS]), "wmo": din("wmo", [L, D, D]),
        "w2i": din("w2i", [L, D, 2 * F]), "w2o": din("w2o", [L, F, D]),
        "wpg": din("wpg", [L, D, D]), "wpp": din("wpp", [L, PD, D]),
    }
    relb_d = din("relb", [1, 128])
    dlam_d = din("dlam", [1, L * 256])
    par_d = din("par", [32, 128])
    gg_d = din("gg", [1, L * 512])
    gb_d = din("gb", [1, L * 512])
    gws_d = din("gws", [L, 4, 128, 128])
    gbs_d = din("gbs", [1, L * 512])
    lng_d = din("lng", [128, 128])
    lnb_d = din("lnb", [128, 128])
    cst = host_consts()
    C = {k: din(k, v.shape) for k, v in cst.items()}
    y_d = nc.dram_tensor("y", [S_len, D], F32, kind="ExternalOutput").ap()

    units = unit_table(F)
    debug_mode = (stop < 99 or nlayers < DEPTH)
    import os
    SKIP = os.environ.get('KSKIP', '').split(',')
    NU = len(units)
    uidx = {u: i for i, u in enumerate(units)}
    wsc = [nc.dram_tensor("wsc%d" % l_, [NU, 128, SLOT], BF16).ap() for l_ in range(L)]
    kcache = nc.dram_tensor("kcache", [L, NT, 4, 128, 1024], BF16).ap()
    vcache = nc.dram_tensor("vcache", [L, NT, 4, 128, 512], BF16).ap()

    with ExitStack() as es:
        S = Sched(nc, es)
        block = es.enter_context(nc.Block())

        def sb(name, shape, dt=F32, stack=es):
            return stack.enter_context(nc.sbuf_tensor("sb_" + name, list(shape), dt))

        X = sb("X", [128, KC, T])
        Xb = sb("Xb", [128, KC, T], BF16)
        mixT = sb("mixT", [128, KC, T], BF16)
        RX = [Res("X%d" % i) for i in range(KC)]
        RXb = [Res("Xb%d" % i) for i in range(KC)]
        Rmix = [Res("mix%d" % i) for i in range(KC)]
        ring = [sb("ring%d" % i, [128, SLOT], BF16) for i in range(NSLOT)]
        Rring = [Res("ring%d" % i) for i in range(NSLOT)]
        ident = sb("ident", [128, 128]); identb = sb("identb", [128, 128], BF16)
        ones = sb("ones", [128, 128]); onesb = sb("onesb", [128, 128], BF16)
        prot = sb("prot", [128, 128], BF16)
        tabA = sb("tabA", [128, 4, 256])
        chA = sb("chA", [128, 4])
        causT = sb("causT", [128, 128]); hgmask = sb("hgmask", [128, 128], U32)
        ud = sb("ud", [128, 128]); ux = sb("ux", [128, 4]); colmask = sb("colmask", [128, 4])
        invd = sb("invd", [128, 1])
        retc = sb("retc", [128, 1536])
        lnG = sb("lnG", [128, 128]); lnB = sb("lnB", [128, 128])
        par = sb("par", [128, 32])
        lbp = sb("lbp", [128, 8]); oml = sb("oml", [128, 8]); noml = sb("noml", [128, 8])
        gA = sb("gA", [128, 2]); nlam = sb("nlam", [128, 2])
        ggT = sb("ggT", [128, L * 512]); gbT = sb("gbT", [128, L * 512])
        wsT = sb("wsT", [128, L * 4, 128], BF16)
        bs2 = sb("bs2", [2, L * 512], BF16)
        S_ret = [sb("S_ret%d" % l, [128, 512]) for l in range(L)]
        Sb_ret = [sb("Sb_ret%d" % l, [128, 512], BF16) for l in range(L)]
        S_hg = [sb("S_hg%d" % l, [128, 512]) for l in range(L)]
        cosT = sb("cosT", [128, T]); sinT = sb("sinT", [128, T])
        Rc = Res("consts")
        RS_ret = [Res() for _ in range(L)]; RSb_ret = [Res() for _ in range(L)]; RS_hg = [Res() for _ in range(L)]
        Rcs = Res("cossin")
        Rkc = [[Res() for _ in range(NT)] for _ in range(L)]
        Rvc = [[Res() for _ in range(NT)] for _ in range(L)]
        Ry = Res("y")

        PSB = [es.enter_context(nc.psum_tensor("ps%d" % i, [128, 512], F32)) for i in range(8)]
        RPS = [Res("ps%d" % i) for i in range(8)]
        ps_state = {"rr": 0, "held": set()}

        def ps_get(hold=False):
            for _ in range(16):
                i = ps_state["rr"]
                ps_state["rr"] = (i + 1) % 8
                if i not in ps_state["held"]:
                    if hold:
                        ps_state["held"].add(i)
                    return i
            raise RuntimeError("no psum bank")

        def ps_release(i):
            ps_state["held"].discard(i)

        def mm(out, lhsT, rhs, start, stop, reads, writes, inc=None):
            S.op("pe", "matmul", out=out, lhsT=lhsT, rhs=rhs, start=start, stop=stop,
                 reads=reads, writes=writes, inc=(stop if inc is None else inc))

        def act(out, in_, func, reads, writes, **kw):
            S.op("act", "activation", out=out, in_=in_, func=func, reads=reads, writes=writes, **kw)

        def tt(e, out, in0, in1, op, reads, writes):
            S.op(e, "tensor_tensor", out=out, in0=in0, in1=in1, op=op, reads=reads, writes=writes)

        def ts(e, out, in0, s1, s2, op0, op1, reads, writes):
            if s2 is None:
                S.op(e, "tensor_scalar", out=out, in0=in0, scalar1=s1, scalar2=None, op0=op0, reads=reads, writes=writes)
            else:
                S.op(e, "tensor_scalar", out=out, in0=in0, scalar1=s1, scalar2=s2, op0=op0, op1=op1, reads=reads, writes=writes)

        def stt(e, out, in0, scalar, in1, op0, op1, reads, writes):
            S.op(e, "scalar_tensor_tensor", out=out, in0=in0, scalar=scalar, in1=in1, op0=op0, op1=op1,
                 reads=reads, writes=writes)

        def cp(e, out, in_, reads, writes):
            if e == "act":
                act(out, in_, AF.Copy, reads, writes)
            else:
                S.op(e, "tensor_copy", out=out, in_=in_, reads=reads, writes=writes)

        def bc(ap, shape):
            return ap.unsqueeze(1).broadcast_to(list(shape))

        def rstd_from(out, in_, scale, reads, writes, tmp, Rtmp):
            act(tmp, in_, AF.Ln, reads, [Rtmp], scale=scale, bias=eps_ap[:, 0:1])
            act(out, tmp, AF.Exp, [Rtmp], writes, scale=-0.5)

        class Phase:
            def __init__(self, prev_ev):
                self.es = ExitStack()
                self.res = []
                self.prev = prev_ev
                self.n = 0

            def tile(self, shape, dt=F32):
                phase_ctr[0] += 1
                t = sb("ph%d" % phase_ctr[0], shape, dt, stack=self.es)
                r = Res()
                r.r = dict(self.prev)
                self.res.append(r)
                return t, r

            def close(self):
                ev = S.events(self.res)
                for k, v in self.prev.items():
                    if ev.get(k, 0) < v:
                        ev[k] = v
                self.es.close()
                return ev

        phase_ctr = [0]
        eps_t = sb("eps_t", [128, 2])
        eps_ap = eps_t

        Rwsc = [[Res() for _ in range(NU)] for _ in range(L)]

        def wsrc(l, tag, idx):
            dst = wsc[l][uidx[(tag, idx)]]
            if tag in ("w1i", "w2i"):
                w = W[tag][l]
                d3 = dst.rearrange("p (kc n) -> p kc n", kc=KC)
                return [(d3[:, :, 0:128], w[:, idx * 128:(idx + 1) * 128].rearrange("(kc p) n -> p kc n", p=128)),
                        (d3[:, :, 128:256], w[:, F + idx * 128:F + (idx + 1) * 128].rearrange("(kc p) n -> p kc n", p=128))]
            if tag in ("w1o", "w2o"):
                hf, m = idx // 16, idx % 16
                w = W[tag][l]
                d3 = dst[:, 0:FH * 128].rearrange("p (fc n) -> p fc n", fc=FH)
                return [(d3, w[hf * FH * 128:(hf + 1) * FH * 128, m * 128:(m + 1) * 128].rearrange("(fc p) n -> p fc n", p=128))]
            if tag in ("wmi", "wmo", "wpg"):
                w = W[tag][l]
                d3 = dst.rearrange("p (kc n) -> p kc n", kc=KC)
                return [(d3, w[:, idx * 256:(idx + 1) * 256].rearrange("(kc p) n -> p kc n", p=128))]
            if tag == "wpp":
                w = W[tag][l]
                d3 = dst.rearrange("p (kc n) -> p kc n", kc=2)
                return [(d3, w.rearrange("(kc p) n -> p kc n", p=128))]
            raise KeyError(tag)

        pro_list = []
        for l in range(L):
            for (tag, idx) in units:
                pro_list.append((l, uidx[(tag, idx)], wsrc(l, tag, idx)))
        pro = {"ptr": 0}

        def pump_until(l, u, ahead=6):
            return

        def pump_all():
            target = len(pro_list) if 'prologue' not in SKIP else 0
            while pro["ptr"] < target:
                lj, uj, lst = pro_list[pro["ptr"]]
                for dv, sv in lst:
                    S.dma("pool", dv, sv, writes=[Rwsc[lj][uj]])
                pro["ptr"] += 1

        pump_all()

        stream_seq = []
        for t_ in range(NT):
            for l in range(L):
                for u in range(NU):
                    stream_seq.append((l, u))
        st = {"issued": 0, "next": 0}

        def w_next(l, tag, idx):
            u = uidx[(tag, idx)]
            i = st["next"]
            assert stream_seq[i] == (l, u), (stream_seq[i], (l, u, tag, idx))
            if debug_mode:
                k = st["issued"]
                pump_until(l, u)
                S.dma("sp", ring[k % NSLOT][:], wsc[l][u], reads=[Rwsc[l][u]], writes=[Rring[k % NSLOT]])
                st["issued"] += 1
                st["next"] += 1
                return ring[k % NSLOT], Rring[k % NSLOT]
            while st["issued"] < min(len(stream_seq), i + NSLOT):
                j = st["issued"]
                lj, uj = stream_seq[j]
                pump_until(lj, uj)
                S.dma("sp", ring[j % NSLOT][:], wsc[lj][uj], reads=[Rwsc[lj][uj]], writes=[Rring[j % NSLOT]])
                st["issued"] += 1
            st["next"] += 1
            return ring[i % NSLOT], Rring[i % NSLOT]

        ph = Phase({})
        stg, Rstg = ph.tile([128, 128])
        for name, dst in (("c_ident", ident), ("c_causT", causT), ("c_ud", ud)):
            S.dma("sp", dst[:], C[name], writes=[Rc])
        S.dma("sp", ux[:], C["c_ux"], writes=[Rc])
        S.dma("sp", colmask[:], C["c_colmask"], writes=[Rc])
        S.dma("sp", invd[:], C["c_invd"], writes=[Rc])
        S.dma("sp", retc[:], C["c_ret"].partition_broadcast(128), writes=[Rc])
        S.dma("sp", ggT[:], gg_d.partition_broadcast(128), writes=[Rc])
        S.dma("sp", gbT[:], gb_d.partition_broadcast(128), writes=[Rc])
        S.op("pool", "memset", ap=ones[:], constant=1.0, writes=[Rc])
        S.op("pool", "memset", ap=onesb[:], constant=1.0, writes=[Rc])
        S.op("pool", "memset", ap=eps_t[:, 0:1], constant=EPS, writes=[Rc])
        S.op("pool", "memset", ap=eps_t[:, 1:2], constant=EPS_LN, writes=[Rc])
        for l in range(L):
            S.op("pool", "memset", ap=S_ret[l][:], constant=0.0, writes=[RS_ret[l]])
            S.op("pool", "memset", ap=Sb_ret[l][:], constant=0.0, writes=[RSb_ret[l]])
            S.op("pool", "memset", ap=S_hg[l][:], constant=0.0, writes=[RS_hg[l]])
        cp("dve", identb[:], ident[:], [Rc], [Rc])
        S.dma("sp", stg[:], C["c_prot"], writes=[Rstg])
        cp("dve", prot[:], stg[:], [Rstg], [Rc])
        S.dma("sp", stg[:], C["c_hgmask"], writes=[Rstg])
        cp("dve", hgmask[:], stg[:], [Rstg], [Rc])
        if 'lnp' not in SKIP:
            for src, dst in ((lng_d, lnG), (lnb_d, lnB)):
                S.dma("sp", stg[:], src, writes=[Rstg])
                b = ps_get()
                S.op("pe", "transpose", out=PSB[b][:, 0:128], in_=stg[:], identity=ident[:], reads=[Rstg, Rc], writes=[RPS[b]])
                cp("dve", dst[:], PSB[b][:, 0:128], [RPS[b]], [Rc])
        if 'par' not in SKIP:
            S.op("pool", "memset", ap=stg[:], constant=0.0, writes=[Rstg])
            S.dma("sp", stg[0:32, :], par_d, writes=[Rstg])
            b = ps_get()
            S.op("pe", "transpose", out=PSB[b][:, 0:128], in_=stg[:], identity=ident[:], reads=[Rstg, Rc], writes=[RPS[b]])
            cp("dve", par[:], PSB[b][:, 0:32], [RPS[b]], [Rc])
            S.op("pool", "memset", ap=lbp[:], constant=0.0, writes=[Rc])
            tmp8, Rtmp8 = ph.tile([128, 8])
            tt("dve", tmp8[:, 0:4], par[:, 14:18], par[:, 10:14], ALU.subtract, [Rc], [Rtmp8])
            act(lbp[:, 4:8], tmp8[:, 0:4], AF.Sigmoid, [Rtmp8], [Rc])
            ts("dve", lbp[:], lbp[:], 1e-30, None, ALU.max, None, [Rc], [Rc])
            ts("dve", oml[:], lbp[:], -1.0, 1.0, ALU.mult, ALU.add, [Rc], [Rc])
            ts("dve", noml[:], oml[:], -1.0, None, ALU.mult, None, [Rc], [Rc])
        if 'lam' not in SKIP:
            dl, Rdl = ph.tile([128, L * 256])
            S.dma("sp", dl[:], dlam_d.partition_broadcast(128), writes=[Rdl])
            pr, Rpr = ph.tile([128, 64])
            sm, Rsm = ph.tile([128, 4])
            for l in range(L):
                lam_init = 0.8 - 0.6 * math.exp(-0.3 * l)
                for i in range(2):
                    a0 = l * 256 + i * 128
                    tt("dve", pr[:], dl[:, a0:a0 + 64], dl[:, a0 + 64:a0 + 128], ALU.mult, [Rdl], [Rpr])
                    S.op("dve", "reduce_sum", out=sm[:, i:i + 1], in_=pr[:], axis=mybir.AxisListType.X, reads=[Rpr], writes=[Rsm])
                act(sm[:, 2:4], sm[:, 0:2], AF.Exp, [Rsm], [Rsm])
                tt("dve", nlam[:, l:l + 1], sm[:, 3:4], sm[:, 2:3], ALU.subtract, [Rsm], [Rc])
                ts("dve", nlam[:, l:l + 1], nlam[:, l:l + 1], -lam_init, None, ALU.add, None, [Rc], [Rc])
                ts("dve", gA[:, l:l + 1], par[:, l:l + 1], 1.0 - lam_init, None, ALU.mult, None, [Rc], [Rc])
        if 'tab' not in SKIP:
            rbB, RrbB = ph.tile([128, 128])
            S.dma("sp", rbB[:], relb_d.partition_broadcast(128), writes=[RrbB])
            cp("dve", chA[:], rbB[:, 124:128], [RrbB], [Rc])
            S.op("pool", "memset", ap=tabA[:], constant=0.0, writes=[Rc])
            ohb, Rohb = ph.tile([128, 8, 256])
            for g8 in range(4):
                S.dma("sp", ohb[:], C["c_oh"][:, g8 * 8:(g8 + 1) * 8, :], writes=[Rohb])
                for bb in range(8):
                    bk = g8 * 8 + bb
                    for h in range(4):
                        stt("dve", tabA[:, h, :], ohb[:, bb, :], rbB[:, bk * 4 + h:bk * 4 + h + 1], tabA[:, h, :],
                            ALU.mult, ALU.add, [Rohb, RrbB, Rc], [Rc])
            S.dma("sp", stg[:], C["c_negmask"], writes=[Rstg])
            for h in range(4):
                tt("dve", tabA[:, h, 0:128], tabA[:, h, 0:128], stg[:], ALU.add, [Rc, Rstg], [Rc])
        if 'gws' not in SKIP:
            tril, Rtril = ph.tile([128, 128])
            S.dma("sp", tril[:], C["c_tril"], writes=[Rtril])
            wst, Rwst = ph.tile([128, 128])
            for l in range(L):
                for g in range(4):
                    S.dma("sp", wst[:], gws_d[l, g], writes=[Rwst])
                    tt("dve", wst[:], wst[:], tril[:], ALU.mult, [Rwst, Rtril], [Rwst])
                    b = ps_get()
                    S.op("pe", "transpose", out=PSB[b][:, 0:128], in_=wst[:], identity=ident[:], reads=[Rwst, Rc], writes=[RPS[b]])
                    cp("dve", wsT[:, l * 4 + g, :], PSB[b][:, 0:128], [RPS[b]], [Rc])
        if 'bs2' not in SKIP:
            b2f, Rb2f = ph.tile([2, L * 512])
            b2h, Rb2h = ph.tile([2, L * 512], BF16)
            b2g, Rb2g = ph.tile([2, L * 512])
            S.dma("sp", b2f[:], gbs_d.partition_broadcast(2), writes=[Rb2f])
            cp("dve", b2h[:], b2f[:], [Rb2f], [Rb2h])
            cp("dve", b2g[:], b2h[:], [Rb2h], [Rb2g])
            tt("dve", b2f[:], b2f[:], b2g[:], ALU.subtract, [Rb2f, Rb2g], [Rb2f])
            ts("dve", b2g[:], b2g[:], colmask[0:2, 2:3], None, ALU.mult, None, [Rb2g, Rc], [Rb2g])
            stt("dve", bs2[:], b2f[:], colmask[0:2, 3:4], b2g[:], ALU.mult, ALU.add, [Rb2f, Rb2g, Rc], [Rc])
        prev_ev = ph.close()

        def ln_begin():
            b1 = ps_get(hold=True)
            b2 = ps_get(hold=True)
            return {"b1": b1, "b2": b2, "n": 0}

        def resid_chunk(lnst, m, Yap, Yreads, coef, sq2, Rsq2, extra_in1=None):
            stt("dve", X[:, m, :], Yap, coef, X[:, m, :], ALU.mult, ALU.add, Yreads + [RX[m]], [RX[m]])
            if lnst is not None:
                i = lnst["n"]
                sq, Rsq = sq2[i % 2], Rsq2[i % 2]
                act(sq[:], X[:, m, :], AF.Square, [RX[m]], [Rsq])
                cp("pool", Xb[:, m, :], X[:, m, :], [RX[m]], [RXb[m]])
                mm(PSB[lnst["b1"]][:], onesb[:], Xb[:, m, :], i == 0, i == KC - 1, [Rc, RXb[m]], [RPS[lnst["b1"]]])
                mm(PSB[lnst["b2"]][:], onesb[:], sq[:], i == 0, i == KC - 1, [Rc, Rsq], [RPS[lnst["b2"]]])
                lnst["n"] += 1

        def ln_finish(lnst, l, i, P):
            b1, b2 = lnst["b1"], lnst["b2"]
            mean, Rmean = P.tile([128, T])
            rstd, Rrstd = P.tile([128, T])
            t1, Rt1 = P.tile([128, T])
            ts("dve", mean[:], PSB[b1][:], 1.0 / D, None, ALU.mult, None, [RPS[b1]], [Rmean])
            tt("pool", t1[:], mean[:], mean[:], ALU.mult, [Rmean], [Rt1])
            stt("dve", t1[:], PSB[b2][:], 1.0 / D, t1[:], ALU.mult, ALU.subtract, [RPS[b2], Rt1], [Rt1])
            act(t1[:], t1[:], AF.Ln, [Rt1], [Rt1], bias=eps_ap[:, 1:2])
            act(rstd[:], t1[:], AF.Exp, [Rt1], [Rrstd], scale=-0.5)
            ps_release(b1)
            ps_release(b2)
            for m in range(KC):
                col = (l * 4 + i) * KC + m
                e = "dve" if m % 2 == 0 else "pool"
                tt(e, X[:, m, :], X[:, m, :], mean[:], ALU.subtract, [RX[m], Rmean], [RX[m]])
                tt(e, X[:, m, :], X[:, m, :], rstd[:], ALU.mult, [RX[m], Rrstd], [RX[m]])
                act(X[:, m, :], X[:, m, :], AF.Identity, [RX[m], Rc], [RX[m]], scale=lnG[:, col:col + 1], bias=lnB[:, col:col + 1])
                cp("pool" if m % 2 == 0 else "dve", Xb[:, m, :], X[:, m, :], [RX[m]], [RXb[m]])

        def ffn(l, tagi, tago, lni, prev):
            P = Phase(prev)
            G, RG_ = P.tile([128, FH, T], BF16)
            RG = [Res() for _ in range(FH)]
            for r in RG:
                r.r = dict(prev)
            P.res.extend(RG)
            sg2 = [P.tile([128, T]) for _ in range(2)]
            sq2 = [P.tile([128, T], BF16) for _ in range(2)]
            lnst = None
            for hf in range(2):
                for jj in range(FH):
                    j = hf * FH + jj
                    slot, Rs = w_next(l, tagi, j)
                    s3 = slot[:].rearrange("p (kc n) -> p kc n", kc=KC)
                    bg = ps_get(); bu = ps_get()
                    for kc in range(KC):
                        mm(PSB[bg][:], s3[:, kc, 0:128], Xb[:, kc, :], kc == 0, kc == KC - 1, [Rs, RXb[kc]], [RPS[bg]])
                    for kc in range(KC):
                        mm(PSB[bu][:], s3[:, kc, 128:256], Xb[:, kc, :], kc == 0, kc == KC - 1, [Rs, RXb[kc]], [RPS[bu]])
                    sg, Rsg = sg2[jj % 2]
                    act(sg[:], PSB[bg][:], AF.Silu, [RPS[bg]], [Rsg])
                    tt("dve", G[:, jj, :], sg[:], PSB[bu][:], ALU.mult, [Rsg, RPS[bu]], [RG[jj]])
                if hf == 1:
                    lnst = ln_begin()
                for m in range(KC):
                    slot, Rs = w_next(l, tago, hf * 16 + m)
                    s3 = slot[:, 0:FH * 128].rearrange("p (fc n) -> p fc n", fc=FH)
                    by = ps_get()
                    for fc in range(FH):
                        mm(PSB[by][:], s3[:, fc, :], G[:, fc, :], fc == 0, fc == FH - 1, [Rs, RG[fc]], [RPS[by]])
                    resid_chunk(lnst, m, PSB[by][:], [RPS[by]], 0.5 / ALPHA,
                                [s[0] for s in sq2], [s[1] for s in sq2])
            ln_finish(lnst, l, lni, P)
            return P.close()

        def proj_fm(l, u):
            slot, Rs = w_next(l, "wmi", u)
            s3 = slot[:].rearrange("p (kc n) -> p kc n", kc=KC)
            for j in range(2):
                b = ps_get()
                for kc in range(KC):
                    mm(PSB[b][:], s3[:, kc, j * 128:(j + 1) * 128], Xb[:, kc, :], kc == 0, kc == KC - 1, [Rs, RXb[kc]], [RPS[b]])
                yield j, b

        def proj_tm(l, u):
            slot, Rs = w_next(l, "wmi", u)
            s3 = slot[:].rearrange("p (kc n) -> p kc n", kc=KC)
            for sub in range(4):
                b = ps_get()
                for kc in range(KC):
                    mm(PSB[b][:, 0:256], Xb[:, kc, sub * 128:(sub + 1) * 128], s3[:, kc, :], kc == 0, kc == KC - 1,
                       [Rs, RXb[kc]], [RPS[b]])
                yield sub, b

        SCALE_A = 64 ** -0.5

        def mixer_A(l, t, prev):
            P = Phase(prev)
            qT = [P.tile([128, T], BF16) for _ in range(4)]
            kpad = [P.tile([128, 2, T], BF16) for _ in range(4)]
            vtok, Rvtok = P.tile([128, 4, 512], BF16)
            kbuf = [P.tile([128, 2, T], BF16) for _ in range(3)]
            vbuf = [P.tile([128, 4, 128], BF16) for _ in range(3)]
            PT = [P.tile([128, T], BF16) for _ in range(3)]
            tmpd = [P.tile([128, 128]) for _ in range(2)]
            r0, Rr0 = P.tile([128, T]); t0, Rt0 = P.tile([128, T])
            r1, Rr1 = P.tile([128, T]); t1, Rt1 = P.tile([128, T])
            sqb, Rsqb = P.tile([128, T], BF16)
            for h in range(4):
                S.op("pool", "memset", ap=kpad[h][0][64:128, 0, :], constant=0.0, writes=[kpad[h][1]])
                S.op("pool", "memset", ap=kpad[h][0][0:64, 1, :], constant=0.0, writes=[kpad[h][1]])
            for u in (0, 1):
                for j, b in proj_fm(l, u):
                    h = u * 2 + j
                    cp("act", qT[h][0][:], PSB[b][:], [RPS[b]], [qT[h][1]])
            for u in (2, 3):
                for j, b in proj_fm(l, u):
                    h = (u - 2) * 2 + j
                    cp("act", kpad[h][0][0:64, 0, :], PSB[b][0:64, :], [RPS[b]], [kpad[h][1]])
                    cp("dve", kpad[h][0][64:128, 1, :], PSB[b][64:128, :], [RPS[b]], [kpad[h][1]])
            for u in (4, 5):
                for sub, b in proj_tm(l, u):
                    cp("act" if sub % 2 else "dve", vtok[:, sub, (u - 4) * 256:(u - 3) * 256], PSB[b][:, 0:256], [RPS[b]], [Rvtok])
            if t < NT - 1:
                for h in range(4):
                    S.dma("pool", kcache[l, t, h], kpad[h][0][:].rearrange("p c n -> p (c n)"), reads=[kpad[h][1]], writes=[Rkc[l][t]])
                for h in range(4):
                    S.dma("pool", vcache[l, t, h].rearrange("p (s n) -> p s n", s=4), vtok[:, :, h * 128:(h + 1) * 128], reads=[Rvtok], writes=[Rvc[l][t]])
            nb = 0
            npt = 0
            for h in range(4):
                acc = [ps_get(hold=True) for _ in range(4)]
                for kt in range(t + 1):
                    if kt < t:
                        kb_t, Rkb = kbuf[nb % 3]
                        vb_t, Rvb = vbuf[nb % 3]
                        nb += 1
                        S.dma("act", kb_t[:].rearrange("p c n -> p (c n)"), kcache[l, kt, h], reads=[Rkc[l][kt]], writes=[Rkb])
                        S.dma("act", vb_t[:].rearrange("p s n -> p (s n)"), vcache[l, kt, h], reads=[Rvc[l][kt]], writes=[Rvb])
                        kview, vview = kb_t, vb_t[:]
                        Rv_ = Rvb
                    else:
                        kview, Rkb = kpad[h]
                        vview = vtok[:, :, h * 128:(h + 1) * 128]
                        Rv_ = Rvtok
                    for kb in range(4):
                        q0 = kb * 128 if kt == t else 0
                        for c in range(2):
                            b = ps_get()
                            mm(PSB[b][:, q0:T], kview[:, c, kb * 128:(kb + 1) * 128], qT[h][0][:, q0:T], True, True,
                               [Rkb, qT[h][1]], [RPS[b]])
                            pt, Rpt = PT[npt % 3]
                            npt += 1
                            far0 = None
                            for qb in range(q0 // 128, 4):
                                rel = (4 * t + qb) - (4 * kt + kb)
                                if rel >= 2:
                                    if far0 is None:
                                        far0 = qb
                                    continue
                                td, Rtd = tmpd[(npt + qb) % 2]
                                stt("dve", td[:], PSB[b][:, qb * 128:(qb + 1) * 128], SCALE_A,
                                    tabA[:, h, rel * 128:(rel + 1) * 128], ALU.mult, ALU.add, [RPS[b], Rc], [Rtd])
                                act(pt[:, qb * 128:(qb + 1) * 128], td[:], AF.Exp, [Rtd], [Rpt])
                            if far0 is not None:
                                act(pt[:, far0 * 128:T], PSB[b][:, far0 * 128:T], AF.Exp, [RPS[b], Rc], [Rpt],
                                    scale=SCALE_A, bias=chA[:, h:h + 1])
                            first = (kt == 0 and kb == 0)
                            last = (kt == t and kb == 3)
                            mm(PSB[acc[c]][:, q0:T], vview[:, kb, :], pt[:, q0:T], first, last, [Rv_, Rpt], [RPS[acc[c]]], inc=False)
                            mm(PSB[acc[2 + c]][:, q0:T], onesb[:], pt[:, q0:T], first, last, [Rc, Rpt], [RPS[acc[2 + c]]], inc=True)
                S.op("dve", "reciprocal", out=r0[:], in_=PSB[acc[2]][:], reads=[RPS[acc[2]]], writes=[Rr0])
                tt("dve", t0[:], PSB[acc[0]][:], r0[:], ALU.mult, [RPS[acc[0]], Rr0], [Rt0])
                S.op("dve", "reciprocal", out=r1[:], in_=PSB[acc[3]][:], reads=[RPS[acc[3]]], writes=[Rr1])
                tt("dve", t1[:], PSB[acc[1]][:], r1[:], ALU.mult, [RPS[acc[1]], Rr1], [Rt1])
                for a in acc:
                    ps_release(a)
                stt("dve", t0[:], t1[:], nlam[:, l:l + 1], t0[:], ALU.mult, ALU.add, [Rt1, Rt0, Rc], [Rt0])
                act(sqb[:], t0[:], AF.Square, [Rt0], [Rsqb])
                b = ps_get()
                mm(PSB[b][:], onesb[:], sqb[:], True, True, [Rc, Rsqb], [RPS[b]])
                rstd_from(r0[:], PSB[b][:], 1.0 / 128, [RPS[b]], [Rr0], r1[:], Rr1)
                stt("dve", mixT[:, h, :], t0[:], gA[:, l:l + 1], r0[:], ALU.mult, ALU.mult, [Rt0, Rr0, Rc], [Rmix[h]])
            return P.close()

        def rope_tables(t, prev):
            P = Phase(prev)
            pi_t, Rpi = P.tile([128, T], I32)
            ang, Rang = P.tile([128, T])
            w1, Rw1 = P.tile([128, T])
            ki, Rki = P.tile([128, T], I32)
            S.dma("pool", pi_t[:], pos_d[:, t * T:(t + 1) * T].partition_broadcast(128), writes=[Rpi])
            cp("dve", ang[:], pi_t[:], [Rpi], [Rang])
            ts("dve", ang[:], ang[:], invd[:, 0:1], None, ALU.mult, None, [Rang, Rc], [Rang])
            for which, dst in ((0, sinT), (1, cosT)):
                src = ang
                if which == 1:
                    ts("dve", w1[:], ang[:], math.pi / 2, None, ALU.add, None, [Rang], [Rw1])
                    src = w1
                kf, Rkf = P.tile([128, T])
                ts("dve", kf[:], src[:], 1.0 / (2 * math.pi), None, ALU.mult, None, [Rang, Rw1], [Rkf])
                cp("dve", ki[:], kf[:], [Rkf], [Rki])
                cp("dve", kf[:], ki[:], [Rki], [Rkf])
                stt("dve", kf[:], kf[:], -2 * math.pi, src[:], ALU.mult, ALU.add, [Rkf, Rang, Rw1], [Rkf])
                ts("dve", kf[:], kf[:], 3.141592, -3.141592, ALU.min, ALU.max, [Rkf], [Rkf])
                act(dst[:], kf[:], AF.Sin, [Rkf], [Rcs])
            return P.close()

        def mixer_B(l, t, prev):
            P = Phase(prev)
            qt = [P.tile([128, T], BF16) for _ in range(4)]
            kt_ = [P.tile([128, T], BF16) for _ in range(4)]
            gs = [P.tile([128, T], BF16) for _ in range(4)]
            vtok, Rvtok = P.tile([128, 4, 512], BF16)
            P1 = Phase(prev)
            raw2 = [P1.tile([128, T], BF16) for _ in range(2)]
            a1, Ra1 = P1.tile([128, T]); a2, Ra2 = P1.tile([128, T])
            nraw = 0
            for (u0, dstl, goff) in ((6, qt, 0), (8, kt_, 512)):
                for u in (u0, u0 + 1):
                    for j, b in proj_fm(l, u):
                        h = (u - u0) * 2 + j
                        raw, Rraw = raw2[nraw % 2]
                        nraw += 1
                        cp("act", raw[:], PSB[b][:], [RPS[b]], [Rraw])
                        b2 = ps_get()
                        mm(PSB[b2][:], prot[:], raw[:], True, True, [Rc, Rraw], [RPS[b2]])
                        tt("dve", a1[:], raw[:], cosT[:], ALU.mult, [Rraw, Rcs], [Ra1])
                        tt("dve", a2[:], PSB[b2][:], sinT[:], ALU.mult, [RPS[b2], Rcs], [Ra2])
                        tt("pool", a1[:], a1[:], a2[:], ALU.add, [Ra1, Ra2], [Ra1])
                        tt("dve", dstl[h][0][:].rearrange("p (c n) -> p c n", c=4), a1[:].rearrange("p (c n) -> p c n", c=4),
                           bc(retc[:, goff + h * 128:goff + (h + 1) * 128], [128, 4, 128]), ALU.mult, [Ra1, Rc], [dstl[h][1]])
            ev1 = P1.close()
            for u in (10, 11):
                for sub, b in proj_tm(l, u):
                    cp("act" if sub % 2 else "dve", vtok[:, sub, (u - 10) * 256:(u - 9) * 256], PSB[b][:, 0:256], [RPS[b]], [Rvtok])
            for u in (12, 13):
                for j, b in proj_fm(l, u):
                    h = (u - 12) * 2 + j
                    act(gs[h][0][:], PSB[b][:], AF.Silu, [RPS[b]], [gs[h][1]])
            P2 = Phase(ev1)
            Pm = [P2.tile([128, 512], BF16) for _ in range(4)]
            ktok = [P2.tile([128, 4, 128], BF16) for _ in range(2)]
            Sbv = [P2.tile([128, 512], BF16) for _ in range(4)]
            tmpS, RtmpS = P2.tile([128, 512])
            ob, Rob = P2.tile([128, T], BF16); sqb, Rsqb = P2.tile([128, T], BF16)
            mean, Rmean = P2.tile([128, T]); var, Rvar = P2.tile([128, T]); dd, Rdd = P2.tile([128, T])
            for c in range(4):
                cs = slice(c * 128, (c + 1) * 128)
                b = ps_get()
                for h in range(4):
                    mm(PSB[b][:, h * 128:(h + 1) * 128], kt_[h][0][:, cs], qt[h][0][:, cs], True, True,
                       [kt_[h][1], qt[h][1]], [RPS[b]], inc=(h == 3))
                tt("dve", Pm[c][0][:].rearrange("p (h n) -> p h n", h=4), PSB[b][:].rearrange("p (h n) -> p h n", h=4),
                   bc(causT[:], [128, 4, 128]), ALU.mult, [RPS[b], Rc], [Pm[c][1]])
                b = ps_get()
                for h in range(4):
                    mm(PSB[b][:, h * 128:(h + 1) * 128], kt_[h][0][:, cs], identb[:], True, True, [kt_[h][1], Rc], [RPS[b]], inc=(h == 3))
                ktk, Rktk = ktok[c % 2]
                cp("act", ktk[:].rearrange("p h n -> p (h n)"), PSB[b][:], [RPS[b]], [Rktk])
                b = ps_get()
                for h in range(4):
                    mm(PSB[b][:, h * 128:(h + 1) * 128], ktk[:, h, :], vtok[:, c, h * 128:(h + 1) * 128], True, True,
                       [Rktk, Rvtok], [RPS[b]], inc=(h == 3))
                tt("dve", tmpS[:], PSB[b][:], S_ret[l][:], ALU.add, [RPS[b], RS_ret[l]], [RtmpS])
                tt("pool", S_ret[l][:], tmpS[:], retc[:, 1024:1536], ALU.mult, [RtmpS, Rc], [RS_ret[l]])
                if c < 3:
                    cp("act", Sbv[c + 1][0][:], S_ret[l][:], [RS_ret[l]], [Sbv[c + 1][1]])
            for h in range(4):
                hs = slice(h * 128, (h + 1) * 128)
                b = ps_get()
                for c in range(4):
                    cs = slice(c * 128, (c + 1) * 128)
                    mm(PSB[b][:, cs], vtok[:, c, hs], Pm[c][0][:, hs], True, False, [Rvtok, Pm[c][1]], [RPS[b]], inc=False)
                    if c == 0:
                        sbt, Rsbt = Sb_ret[l], RSb_ret[l]
                    else:
                        sbt, Rsbt = Sbv[c]
                    mm(PSB[b][:, cs], sbt[:, hs], qt[h][0][:, cs], False, True, [Rsbt, qt[h][1]], [RPS[b]], inc=(c == 3))
                cp("act", ob[:], PSB[b][:], [RPS[b]], [Rob])
                act(sqb[:], PSB[b][:], AF.Square, [RPS[b]], [Rsqb])
                b1 = ps_get(); b2 = ps_get()
                mm(PSB[b1][:], onesb[:], ob[:], True, True, [Rc, Rob], [RPS[b1]])
                mm(PSB[b2][:], onesb[:], sqb[:], True, True, [Rc, Rsqb], [RPS[b2]])
                ts("dve", mean[:], PSB[b1][:], 1.0 / 128, None, ALU.mult, None, [RPS[b1]], [Rmean])
                tt("pool", var[:], mean[:], mean[:], ALU.mult, [Rmean], [Rvar])
                stt("dve", var[:], PSB[b2][:], 1.0 / 128, var[:], ALU.mult, ALU.subtract, [RPS[b2], Rvar], [Rvar])
                act(var[:], var[:], AF.Ln, [Rvar, Rc], [Rvar], bias=eps_ap[:, 0:1])
                act(var[:], var[:], AF.Exp, [Rvar], [Rvar], scale=-0.5)
                tt("dve", dd[:], PSB[b][:], mean[:], ALU.subtract, [RPS[b], Rmean], [Rdd])
                tt("pool", dd[:], dd[:], var[:], ALU.mult, [Rdd, Rvar], [Rdd])
                tt("dve", mixT[:, 4 + h, :], dd[:], gs[h][0][:], ALU.mult, [Rdd, gs[h][1]], [Rmix[4 + h]])
            cp("act", Sb_ret[l][:], S_ret[l][:], [RS_ret[l]], [RSb_ret[l]])
            ev2 = P2.close()
            P.prev = ev2
            return P.close()

        def mixer_C(l, t, prev):
            P = Phase(prev)
            uT = [P.tile([128, T], BF16) for _ in range(4)]
            vg = [P.tile([128, 512]) for _ in range(4)]
            vnb = [P.tile([128, 512], BF16) for _ in range(4)]
            st6, Rst6 = P.tile([128, 8]); mv, Rmv = P.tile([128, 4])
            for u in (14, 15):
                for j, b in proj_fm(l, u):
                    g = (u - 14) * 2 + j
                    act(uT[g][0][:], PSB[b][:], AF.Gelu, [RPS[b]], [uT[g][1]])
            for u in (16, 17):
                for sub, b in proj_tm(l, u):
                    act(vg[sub][0][:, (u - 16) * 256:(u - 15) * 256], PSB[b][:, 0:256], AF.Gelu, [RPS[b]], [vg[sub][1]])
            for sub in range(4):
                v_, Rv_ = vg[sub]
                S.op("dve", "bn_stats", out=st6[:, 0:6], in_=v_[:], reads=[Rv_], writes=[Rst6])
                S.op("dve", "bn_aggr", out=mv[:, 0:2], in_=st6[:, 0:6], reads=[Rst6], writes=[Rmv])
                act(mv[:, 2:3], mv[:, 1:2], AF.Ln, [Rmv, Rc], [Rmv], bias=eps_ap[:, 0:1])
                act(mv[:, 3:4], mv[:, 2:3], AF.Exp, [Rmv], [Rmv], scale=-0.5)
                ts("dve", v_[:], v_[:], mv[:, 0:1], mv[:, 3:4], ALU.subtract, ALU.mult, [Rv_, Rmv], [Rv_])
                tt("pool", v_[:], v_[:], ggT[:, l * 512:(l + 1) * 512], ALU.mult, [Rv_, Rc], [Rv_])
                tt("dve", vnb[sub][0][:], v_[:], gbT[:, l * 512:(l + 1) * 512], ALU.add, [Rv_, Rc], [vnb[sub][1]])
            for g in range(4):
                gsl = slice(g * 128, (g + 1) * 128)
                b = ps_get()
                for sub in range(4):
                    cs = slice(sub * 128, (sub + 1) * 128)
                    mm(PSB[b][:, cs], vnb[sub][0][:, gsl], wsT[:, l * 4 + g, :], True, False, [vnb[sub][1], Rc], [RPS[b]], inc=False)
                    mm(PSB[b][:, cs], onesb[0:2, :], bs2[:, l * 512 + g * 128:l * 512 + (g + 1) * 128], False, True,
                       [Rc], [RPS[b]], inc=(sub == 3))
                tt("dve", mixT[:, 8 + g, :], uT[g][0][:], PSB[b][:], ALU.mult, [uT[g][1], RPS[b]], [Rmix[8 + g]])
            return P.close()

        def mixer_D(l, t, prev):
            P = Phase(prev)
            qt = [P.tile([128, T], BF16) for _ in range(4)]
            kt_ = [P.tile([128, T], BF16) for _ in range(4)]
            itok, Ritok = P.tile([128, 4, 512], BF16)
            E1s, RE1s = P.tile([128, 4, 8]); E2s, RE2s = P.tile([128, 4, 8])
            for u in (18, 19):
                for j, b in proj_fm(l, u):
                    h = (u - 18) * 2 + j
                    cp("act", qt[h][0][:], PSB[b][:], [RPS[b]], [qt[h][1]])
            P1 = Phase(prev)
            sg, Rsg = P1.tile([128, T]); keyp, Rkeyp = P1.tile([128, T])
            e1, Re1 = P1.tile([128, T]); e2, Re2 = P1.tile([128, T])
            logf, Rlogf = P1.tile([128, T])
            lft, Rlft = P1.tile([128, 4, 128])
            x8, Rx8 = P1.tile([128, 8])
            for u in (20, 21):
                for j, b in proj_fm(l, u):
                    h = (u - 20) * 2 + j
                    col = l * 4 + h
                    act(sg[:], PSB[b][:], AF.Sigmoid, [RPS[b]], [Rsg])
                    ts("dve", logf[:], sg[:], oml[:, col:col + 1], lbp[:, col:col + 1], ALU.mult, ALU.add, [Rsg, Rc], [Rlogf])
                    act(logf[:], logf[:], AF.Ln, [Rlogf], [Rlogf])
                    ts("dve", keyp[:], sg[:], noml[:, col:col + 1], oml[:, col:col + 1], ALU.mult, ALU.add, [Rsg, Rc], [Rkeyp])
                    bt = ps_get()
                    for c in range(4):
                        S.op("pe", "transpose", out=PSB[bt][:, c * 128:(c + 1) * 128], in_=logf[:, c * 128:(c + 1) * 128], identity=ident[:],
                             reads=[Rlogf, Rc], writes=[RPS[bt]], inc=(c == 3))
                    cp("dve", lft[:].rearrange("p c n -> p (c n)"), PSB[bt][:], [RPS[bt]], [Rlft])
                    bd_ = ps_get()
                    for c in range(4):
                        cs = slice(c * 128, (c + 1) * 128)
                        mm(PSB[bd_][:, cs], lft[:, c, :], ud[:], True, True, [Rlft, Rc], [RPS[bd_]], inc=(c == 3))
                    act(e2[:], PSB[bd_][:], AF.Exp, [RPS[bd_]], [Re2], scale=-1.0)
                    tt("dve", kt_[h][0][:], keyp[:], e2[:], ALU.mult, [Rkeyp, Re2], [kt_[h][1]])
                    act(e1[:], PSB[bd_][:], AF.Exp, [RPS[bd_]], [Re1])
                    tt("dve", qt[h][0][:], qt[h][0][:], e1[:], ALU.mult, [qt[h][1], Re1], [qt[h][1]])
                    e1v = e1[:].rearrange("p (cj n) -> p cj n", n=64)
                    lfv = logf[:].rearrange("p (cj n) -> p cj n", n=64)
                    cp("dve", E2s[:, h, :], e1v[:, :, 63], [Re1], [RE2s])
                    cp("dve", x8[:, 0:8], PSB[bd_][:].rearrange("p (cj n) -> p cj n", n=64)[:, :, 0], [RPS[bd_]], [Rx8])
                    tt("dve", x8[:, 0:8], lfv[:, :, 0], x8[:, 0:8], ALU.subtract, [Rlogf, Rx8], [Rx8])
                    act(E1s[:, h, :], x8[:, 0:8], AF.Exp, [Rx8], [RE1s])
            ev1 = P1.close()
            for u in (22, 23):
                for sub, b in proj_tm(l, u):
                    cp("act" if sub % 2 else "dve", itok[:, sub, (u - 22) * 256:(u - 21) * 256], PSB[b][:, 0:256], [RPS[b]], [Ritok])
            P2 = Phase(ev1)
            am32 = [P2.tile([128, 4, 128]) for _ in range(2)]
            gs = [P2.tile([128, T], BF16) for _ in range(4)]
            Am = [P2.tile([128, 4, 128], BF16) for _ in range(2)]
            ktok = [[P2.tile([128, 4, 128], BF16) for _ in range(2)] for _ in range(2)]
            Sbv = [[P2.tile([128, 4, 128], BF16) for _ in range(2)] for _ in range(2)]
            SE, RSE = P2.tile([128, 4, 128]); tmpS, RtmpS = P2.tile([128, 4, 128])
            sqb, Rsqb = P2.tile([128, T], BF16); rs, Rrs = P2.tile([128, T]); r2, Rr2 = P2.tile([128, T])
            for i2 in range(2):
                S.op("pool", "memset", ap=am32[i2][0][:], constant=0.0, writes=[am32[i2][1]])
            for u in (24, 25):
                for j, b in proj_fm(l, u):
                    h = (u - 24) * 2 + j
                    act(gs[h][0][:], PSB[b][:], AF.Silu, [RPS[b]], [gs[h][1]])
            S3 = S_hg[l][:].rearrange("p (h n) -> p h n", h=4)
            bo = [ps_get(hold=True) for _ in range(4)]
            for c in range(4):
                cs = slice(c * 128, (c + 1) * 128)
                am, Ram = Am[c % 2]
                b = ps_get()
                for h in range(4):
                    mm(PSB[b][:, h * 128:(h + 1) * 128], kt_[h][0][:, cs], qt[h][0][:, cs], True, True,
                       [kt_[h][1], qt[h][1]], [RPS[b]], inc=(h == 3))
                a32, Ra32 = am32[c % 2]
                for h in range(4):
                    S.op("dve", "copy_predicated", out=a32[:, h, :], mask=hgmask[:], data=PSB[b][:, h * 128:(h + 1) * 128],
                         reads=[RPS[b], Rc], writes=[Ra32])
                cp("act", am[:].rearrange("p h n -> p (h n)"), a32[:].rearrange("p h n -> p (h n)"), [Ra32], [Ram])
                b = ps_get()
                for h in range(4):
                    mm(PSB[b][:, h * 128:(h + 1) * 128], kt_[h][0][:, cs], identb[:], True, True, [kt_[h][1], Rc], [RPS[b]], inc=(h == 3))
                for j in range(2):
                    ktk, Rktk = ktok[c % 2][j]
                    ts("dve", ktk[:].rearrange("p h n -> p (h n)"), PSB[b][:], colmask[:, j:j + 1], None, ALU.mult, None,
                       [RPS[b], Rc], [Rktk])
                for j in range(2):
                    ktk, Rktk = ktok[c % 2][j]
                    sbv, Rsbv = Sbv[c % 2][j]
                    bd_ = ps_get()
                    for h in range(4):
                        mm(PSB[bd_][:, h * 128:(h + 1) * 128], ktk[:, h, :], itok[:, c, h * 128:(h + 1) * 128], True, True,
                           [Rktk, Ritok], [RPS[bd_]], inc=(h == 3))
                    cj = c * 2 + j
                    for h in range(4):
                        hs_ = slice(h * 128, (h + 1) * 128)
                        ts("dve", sbv[:, h, :], S_hg[l][:, hs_], E1s[:, h, cj:cj + 1], None, ALU.mult, None, [RS_hg[l], RE1s], [Rsbv])
                        stt("dve", tmpS[:, h, :], S_hg[l][:, hs_], E1s[:, h, cj:cj + 1], PSB[bd_][:, hs_], ALU.mult, ALU.add,
                            [RS_hg[l], RE1s, RPS[bd_]], [RtmpS])
                        ts("dve", S_hg[l][:, hs_], tmpS[:, h, :], E2s[:, h, cj:cj + 1], None, ALU.mult, None, [RtmpS, RE2s], [RS_hg[l]])
                for h in range(4):
                    hs = slice(h * 128, (h + 1) * 128)
                    mm(PSB[bo[h]][:, cs], itok[:, c, hs], am[:, h, :], True, False, [Ritok, Ram], [RPS[bo[h]]], inc=False)
                    for j in range(2):
                        js = slice(c * 128 + j * 64, c * 128 + (j + 1) * 64)
                        sbv, Rsbv = Sbv[c % 2][j]
                        mm(PSB[bo[h]][:, js], sbv[:, h, :], qt[h][0][:, js], False, j == 1, [Rsbv, qt[h][1]], [RPS[bo[h]]],
                           inc=(j == 1))
            bss = ps_get(hold=True)
            for h in range(4):
                b = bo[h]
                act(sqb[:], PSB[b][:], AF.Square, [RPS[b]], [Rsqb])
                mm(PSB[bss][:], onesb[:], sqb[:], h == 0, h == 3, [Rc, Rsqb], [RPS[bss]], inc=True)
            rstd_from(rs[:], PSB[bss][:], 1.0 / 512, [RPS[bss]], [Rrs], r2[:], Rr2)
            ps_release(bss)
            for h in range(4):
                col = l * 4 + h
                b = bo[h]
                stt("dve", r2[:], PSB[b][:], par[:, 2 + col:3 + col], rs[:], ALU.mult, ALU.mult, [RPS[b], Rrs, Rc], [Rr2])
                tt("dve", mixT[:, 12 + h, :], r2[:], gs[h][0][:], ALU.mult, [Rr2, gs[h][1]], [Rmix[12 + h]])
                ps_release(b)
            ev2 = P2.close()
            P.prev = ev2
            return P.close()

        def mix_out(l, prev):
            P = Phase(prev)
            sq2 = [P.tile([128, T], BF16) for _ in range(2)]
            lnst = ln_begin()
            for u in range(8):
                slot, Rs = w_next(l, "wmo", u)
                s3 = slot[:].rearrange("p (kc n) -> p kc n", kc=KC)
                for j in range(2):
                    m = u * 2 + j
                    b = ps_get()
                    for kc in range(KC):
                        mm(PSB[b][:], s3[:, kc, j * 128:(j + 1) * 128], mixT[:, kc, :], kc == 0, kc == KC - 1, [Rs, Rmix[kc]], [RPS[b]])
                    resid_chunk(lnst, m, PSB[b][:], [RPS[b]], 1.0 / ALPHA, [s[0] for s in sq2], [s[1] for s in sq2])
            ln_finish(lnst, l, 1, P)
            return P.close()

        def ple(l, t, prev):
            P = Phase(prev)
            sq2 = [P.tile([128, T], BF16) for _ in range(2)]
            pst, Rpst = P.tile([128, 4, PD])
            pT, RpT = P.tile([128, 2, T], BF16)
            gt2 = [P.tile([128, T]) for _ in range(2)]
            S.dma("pool", pst[:], p_d[l, t * T:(t + 1) * T, :].rearrange("(s p) n -> p s n", p=128), writes=[Rpst])
            for kc2 in range(2):
                b = ps_get()
                for sub in range(4):
                    S.op("pe", "transpose", out=PSB[b][:, sub * 128:(sub + 1) * 128], in_=pst[:, sub, kc2 * 128:(kc2 + 1) * 128],
                         identity=ident[:], reads=[Rpst, Rc], writes=[RPS[b]], inc=(sub == 3))
                cp("dve", pT[:, kc2, :], PSB[b][:], [RPS[b]], [RpT])
            gates = []
            G16, RG16_ = P.tile([128, KC, T], BF16)
            RG16 = [Res() for _ in range(KC)]
            for r in RG16:
                r.r = dict(prev)
            P.res.extend(RG16)
            for u in range(8):
                slot, Rs = w_next(l, "wpg", u)
                s3 = slot[:].rearrange("p (kc n) -> p kc n", kc=KC)
                for j in range(2):
                    m = u * 2 + j
                    b = ps_get()
                    for kc in range(KC):
                        mm(PSB[b][:], s3[:, kc, j * 128:(j + 1) * 128], Xb[:, kc, :], kc == 0, kc == KC - 1, [Rs, RXb[kc]], [RPS[b]])
                    act(G16[:, m, :], PSB[b][:], AF.Sigmoid, [RPS[b]], [RG16[m]])
            slot, Rs = w_next(l, "wpp", 0)
            s3 = slot[:].rearrange("p (kc n) -> p kc n", kc=2)
            lnst = ln_begin()
            for m in range(KC):
                b = ps_get()
                for kc2 in range(2):
                    mm(PSB[b][:], s3[:, kc2, m * 128:(m + 1) * 128], pT[:, kc2, :], kc2 == 0, kc2 == 1, [Rs, RpT], [RPS[b]])
                g_, Rg_ = gt2[m % 2]
                tt("dve", g_[:], PSB[b][:], G16[:, m, :], ALU.mult, [RPS[b], RG16[m]], [Rg_])
                resid_chunk(lnst, m, g_[:], [Rg_], 1.0 / ALPHA, [s[0] for s in sq2], [s[1] for s in sq2])
            ln_finish(lnst, l, 3, P)
            return P.close()

        def load_x(t, prev):
            P = Phase(prev)
            stg2 = [P.tile([128, D]) for _ in range(2)]
            for sub in range(4):
                sg_, Rsg_ = stg2[sub % 2]
                S.dma("pool", sg_[:], x_d[t * T + sub * 128:t * T + (sub + 1) * 128, :], writes=[Rsg_])
                for g4 in range(4):
                    b = ps_get()
                    for j in range(4):
                        kc = g4 * 4 + j
                        S.op("pe", "transpose", out=PSB[b][:, j * 128:(j + 1) * 128], in_=sg_[:, kc * 128:(kc + 1) * 128],
                             identity=ident[:], reads=[Rsg_, Rc], writes=[RPS[b]], inc=(j == 3))
                    rx = RX[g4 * 4:g4 * 4 + 4]
                    rxb = RXb[g4 * 4:g4 * 4 + 4]
                    cp("dve", X[:, g4 * 4:g4 * 4 + 4, sub * 128:(sub + 1) * 128], PSB[b][:].rearrange("p (a n) -> p a n", a=4), [RPS[b]], rx)
                    cp("pool", Xb[:, g4 * 4:g4 * 4 + 4, sub * 128:(sub + 1) * 128], X[:, g4 * 4:g4 * 4 + 4, sub * 128:(sub + 1) * 128], rx, rxb)
            return P.close()

        def store_y(t, prev):
            P = Phase(prev)
            stg2 = [P.tile([128, D]) for _ in range(2)]
            for sub in range(4):
                sg_, Rsg_ = stg2[sub % 2]
                for g4 in range(4):
                    b = ps_get()
                    for j in range(4):
                        kc = g4 * 4 + j
                        S.op("pe", "transpose", out=PSB[b][:, j * 128:(j + 1) * 128], in_=X[:, kc, sub * 128:(sub + 1) * 128],
                             identity=ident[:], reads=[RX[kc], Rc], writes=[RPS[b]], inc=(j == 3))
                    cp("act" if g4 % 2 else "dve", sg_[:, g4 * 512:(g4 + 1) * 512], PSB[b][:], [RPS[b]], [Rsg_])
                S.dma("pool", y_d[t * T + sub * 128:t * T + (sub + 1) * 128, :], sg_[:], reads=[Rsg_], writes=[Ry])
            return P.close()

        ev = prev_ev
        for t in range(NT):
            if 'rope' not in SKIP:
                ev = rope_tables(t, ev)
            if 'loadx' not in SKIP:
                ev = load_x(t, ev)
            for l in range(nlayers):
                if stop >= 1:
                    ev = ffn(l, "w1i", "w1o", 0, ev)
                else:
                    for _ in range(FC + 32):
                        st["next"] += 1
                stages = [(2, mixer_A, 6), (3, mixer_B, 8), (4, mixer_C, 4), (5, mixer_D, 8)]
                for sid, fn, nun in stages:
                    if stop >= sid:
                        ev = fn(l, t, ev)
                    else:
                        st["next"] += nun
                if dumpmix:
                    for m in range(KC):
                        cp("dve", X[:, m, :], mixT[:, m, :], [Rmix[m]], [RX[m]])
                if stop >= 6:
                    ev = mix_out(l, ev)
                else:
                    st["next"] += 8
                if stop >= 7:
                    ev = ffn(l, "w2i", "w2o", 2, ev)
                else:
                    st["next"] += FC + 32
                if stop >= 8:
                    ev = ple(l, t, ev)
                else:
                    st["next"] += 9
            for _ in range((L - nlayers) * NU):
                st["next"] += 1
            if 'storey' not in SKIP:
                ev = store_y(t, ev)
        S.finish("pool", [Ry])
        S.finish("sp", Rring)
        S.emit(block)
    return nc


def make_in_map(inputs, b, S_len):
    f32 = np.float32
    m = {}
    m["x"] = np.ascontiguousarray(inputs["x"][b, :S_len])
    m["p"] = np.ascontiguousarray(inputs["p"][:, b, :S_len])
    m["pos"] = np.ascontiguousarray(inputs["positions"][b:b + 1, :S_len]).astype(np.int32)
    m["w1i"] = inputs["ffn1_w_in"]; m["w1o"] = inputs["ffn1_w_out"]
    m["wmi"] = inputs["w_mix_in"]; m["wmo"] = inputs["w_mix_out"]
    m["w2i"] = inputs["ffn2_w_in"]; m["w2o"] = inputs["ffn2_w_out"]
    m["wpg"] = inputs["ple_w_gate"]; m["wpp"] = inputs["ple_w_proj"]
    m["relb"] = np.ascontiguousarray(inputs["rel_bias"]).reshape(1, 128)
    m["dlam"] = np.ascontiguousarray(inputs["diff_lambda"]).reshape(1, -1)
    par = np.zeros((32, 128), f32)
    par[0:2] = inputs["diff_norm_g"]
    par[2:10] = np.ascontiguousarray(inputs["hgrn_norm_g"]).reshape(8, 128)
    par[10:18] = np.ascontiguousarray(inputs["hgrn_lb_logits"]).reshape(8, 128)
    m["par"] = par
    m["gg"] = np.ascontiguousarray(inputs["gmlp_ln_g"]).reshape(1, -1)
    m["gb"] = np.ascontiguousarray(inputs["gmlp_ln_b"]).reshape(1, -1)
    m["gws"] = np.ascontiguousarray(inputs["gmlp_w_s"])
    m["gbs"] = np.ascontiguousarray(inputs["gmlp_b_s"]).reshape(1, -1)
    m["lng"] = np.ascontiguousarray(inputs["ln_g"]).reshape(128, 128)
    m["lnb"] = np.ascontiguousarray(inputs["ln_b"]).reshape(128, 128)
    m.update(host_consts())
    return {k: np.ascontiguousarray(v) for k, v in m.items()}


def kernel(**inputs):
    inputs = {k: np.asarray(v) for k, v in inputs.items()}
    B, S_len = inputs["x"].shape[:2]
    F = inputs["ffn1_w_out"].shape[1]
    nc = build(S_len, F)
    in_maps = [make_in_map(inputs, b, S_len) for b in range(B)]
    res = run_bass_kernel_spmd(nc, in_maps, core_ids=list(range(B)))
    out = np.stack([np.asarray(res.results[b]["y"]) for b in range(B)], axis=0)
    return out.astype(np.float32)
```

```python
import math
from contextlib import ExitStack
import numpy as np
import ml_dtypes
import concourse.bass as bass
import concourse.mybir as mybir
from concourse.bass_utils import run_bass_kernel_spmd

F32 = mybir.dt.float32
BF16 = mybir.dt.bfloat16
I32 = mybir.dt.int32
U32 = mybir.dt.uint32
AF = mybir.ActivationFunctionType
ALU = mybir.AluOpType

D = 2048
KC = 16
T = 512
PD = 256
DEPTH = 2
ALPHA = (2 * DEPTH) ** 0.25
EPS = 1e-5
EPS_LN = EPS / (ALPHA * ALPHA)
SLOT = 4096
NSLOT = 4
MIXCOLS = 6656


class Res:
    __slots__ = ("name", "w", "r")

    def __init__(self, name="r"):
        self.name = name
        self.w = None
        self.r = {}


class Sched:
    ENG = ("pe", "act", "dve", "pool", "sp")

    def __init__(self, nc, es, n_dma_sems=20):
        self.nc = nc
        self.sems = {}
        self.cnt = {}
        for e in self.ENG:
            self.sems[e] = es.enter_context(nc.semaphore("s_" + e))
            self.cnt[e] = 0
        self.dma_sems = []
        for i in range(n_dma_sems):
            k = "d%d" % i
            self.sems[k] = es.enter_context(nc.semaphore("s_" + k))
            self.cnt[k] = 0
            self.dma_sems.append(k)
        self.dma_rr = 0
        self.known = {e: {} for e in self.ENG}
        self.prog = {e: [] for e in self.ENG}
        self.n_wait = 0
        self.n_inst = 0

    def _wait(self, e, key, val):
        if val <= 0:
            return
        if e == "pe" and key == "pe":
            return
        kn = self.known[e]
        if kn.get(key, 0) >= val:
            return
        self.prog[e].append(("wait", key, val))
        kn[key] = val
        self.n_wait += 1

    def _deps(self, e, reads, writes):
        need = {}
        for R in reads:
            if R.w is not None:
                k, v = R.w
                if need.get(k, 0) < v:
                    need[k] = v
        for R in writes:
            if R.w is not None:
                k, v = R.w
                if need.get(k, 0) < v:
                    need[k] = v
            for k, v in R.r.items():
                if need.get(k, 0) < v:
                    need[k] = v
        for k, v in need.items():
            self._wait(e, k, v)

    def _mark(self, ev, reads, writes):
        k, v = ev
        for R in writes:
            R.w = ev
            R.r = {}
        for R in reads:
            if R.r.get(k, 0) < v:
                R.r[k] = v

    def op(self, e, name, reads=(), writes=(), inc=True, **kw):
        self._deps(e, reads, writes)
        if inc:
            self.cnt[e] += 1
            self.prog[e].append(("op", name, kw, e, 1))
            self._mark((e, self.cnt[e]), reads, writes)
        else:
            self.prog[e].append(("op", name, kw, None, 0))
            self._mark((e, self.cnt[e] + 1), reads, writes)
        self.n_inst += 1

    def dma(self, q, out, in_, reads=(), writes=(), **kw):
        k = self.dma_sems[self.dma_rr]
        self.dma_rr = (self.dma_rr + 1) % len(self.dma_sems)
        self._wait(q, k, self.cnt[k])
        self._deps(q, reads, writes)
        self.cnt[k] += 16
        kw = dict(kw)
        kw["out"] = out
        kw["in_"] = in_
        self.prog[q].append(("op", "dma_start", kw, k, 16))
        self._mark((k, self.cnt[k]), reads, writes)
        self.n_inst += 1

    def events(self, resources):
        ev = {}
        for R in resources:
            if R.w is not None:
                k, v = R.w
                if ev.get(k, 0) < v:
                    ev[k] = v
            for k, v in R.r.items():
                if ev.get(k, 0) < v:
                    ev[k] = v
        return ev

    def finish(self, e, resources):
        for k, v in self.events(resources).items():
            self._wait(e, k, v)

    def emit(self, block):
        sems = self.sems

        def run(eng, prog):
            for it in prog:
                if it[0] == "wait":
                    eng.wait_ge(sems[it[1]], it[2])
                else:
                    _, name, kw, sk, inc = it
                    inst = getattr(eng, name)(**kw)
                    if sk is not None:
                        inst.then_inc(sems[sk], inc)

        block.tensor(lambda e: run(e, self.prog["pe"]))
        block.scalar(lambda e: run(e, self.prog["act"]))
        block.vector(lambda e: run(e, self.prog["dve"]))
        block.gpsimd(lambda e: run(e, self.prog["pool"]))
        block.sync(lambda e: run(e, self.prog["sp"]))


def t5_bucket_np(n):
    n = np.maximum(n, 0)
    nf = np.maximum(n, 1).astype(np.float32)
    large = 16 + (np.log(nf / np.float32(16)) / np.float32(math.log(128 / 16)) * np.float32(16)).astype(np.int32)
    large = np.minimum(large, 31)
    return np.where(n < 16, n, large)


def host_consts():
    c = {}
    eye = np.eye(128, dtype=np.float32)
    c["c_ident"] = eye
    k = np.arange(128)[:, None]
    q = np.arange(128)[None, :]
    oh = np.zeros((128, 32, 256), np.float32)
    bd = t5_bucket_np(q - k)
    bs = t5_bucket_np(q - k + 128)
    for b in range(32):
        oh[:, b, 0:128] = (bd == b)
        oh[:, b, 128:256] = (bs == b)
    c["c_oh"] = oh
    c["c_negmask"] = np.where(k > q, np.float32(-1e30), np.float32(0)).astype(np.float32)
    c["c_causT"] = (q >= k).astype(np.float32)
    blk = ((k // 64) == (q // 64)) & (q >= k)
    c["c_hgmask"] = blk.astype(np.float32)
    c["c_tril"] = (k >= q).astype(np.float32)
    inv = (10000.0 ** (-np.linspace(0.0, 1.0, 64))).astype(np.float32)
    c["c_invd"] = np.repeat(inv, 2)[:, None].astype(np.float32)
    prot = np.zeros((128, 128), np.float32)
    for i in range(64):
        prot[2 * i + 1, 2 * i] = -1.0
        prot[2 * i, 2 * i + 1] = 1.0
    c["c_prot"] = prot
    g = 1.0 - 2.0 ** (-5.0 - np.arange(4, dtype=np.float64))
    j = np.arange(128, dtype=np.float64)
    gq = np.stack([g[h] ** (j + 1.0) for h in range(4)])
    gk = np.stack([g[h] ** (-(j + 1.0)) * (128.0 ** -0.5) for h in range(4)])
    gs = np.stack([np.full(128, g[h] ** 128.0) for h in range(4)])
    c["c_ret"] = np.concatenate([gq.reshape(1, 512), gk.reshape(1, 512), gs.reshape(1, 512)], 1).astype(np.float32)
    ud = np.zeros((128, 128), np.float32)
    ux = np.zeros((128, 4), np.float32)
    for s in range(128):
        cs, ls = s // 64, s % 64
        for t in range(128):
            ct, lt = t // 64, t % 64
            if cs != ct:
                continue
            if 31 < ls <= lt:
                ud[s, t] = 1.0
            elif lt < ls <= 31:
                ud[s, t] = -1.0
        if ls <= 31:
            ux[s, 2 * cs] = 1.0
        else:
            ux[s, 2 * cs + 1] = 1.0
    c["c_ud"] = ud
    c["c_ux"] = ux
    cm = np.zeros((128, 4), np.float32)
    cm[0:64, 0] = 1.0
    cm[64:128, 1] = 1.0
    cm[0, 2] = 1.0
    cm[1, 3] = 1.0
    c["c_colmask"] = cm
    return c


WMI_ORDER = list(range(26))


def unit_table(F):
    FC = F // 128
    FH = FC // 2
    units = []
    def ffn_units(tagi, tago):
        u = []
        for hf in range(2):
            for jj in range(FH):
                u.append((tagi, hf * FH + jj))
            for m in range(16):
                u.append((tago, hf * 16 + m))
        return u
    units += ffn_units("w1i", "w1o")
    for u in WMI_ORDER:
        units.append(("wmi", u))
    for u in range(8):
        units.append(("wmo", u))
    units += ffn_units("w2i", "w2o")
    for u in range(8):
        units.append(("wpg", u))
    units.append(("wpp", 0))
    return units


def build(S_len, F, stop=99, nlayers=DEPTH, dumpmix=False):
    NT = S_len // T
    FC = F // 128
    FH = FC // 2
    L = DEPTH
    nc = bass.Bass("TRN2", target_bir_lowering=False)

    def din(name, shape, dt=F32):
        return nc.dram_tensor(name, list(shape), dt, kind="ExternalInput").ap()

    x_d = din("x", [S_len, D])
    p_d = din("p", [L, S_len, PD])
    pos_d = din("pos", [1, S_len], I32)
    W = {
        "w1i": din("w1i", [L, D, 2 * F]), "w1o": din("w1o", [L, F, D]),
        "wmi": din("wmi", [L, D, MIXCOLS]), "wmo": din("wmo", [L, D, D]),
        "w2i": din("w2i", [L, D, 2 * F]), "w2o": din("w2o", [L, F, D]),
        "wpg": din("wpg", [L, D, D]), "wpp": din("wpp", [L, PD, D]),
    }
    relb_d = din("relb", [1, 128])
    dlam_d = din("dlam", [1, L * 256])
    par_d = din("par", [32, 128])
    gg_d = din("gg", [1, L * 512])
    gb_d = din("gb", [1, L * 512])
    gws_d = din("gws", [L, 4, 128, 128])
    gbs_d = din("gbs", [1, L * 512])
    lng_d = din("lng", [128, 128])
    lnb_d = din("lnb", [128, 128])
    cst = host_consts()
    C = {k: din(k, v.shape) for k, v in cst.items()}
    y_d = nc.dram_tensor("y", [S_len, D], F32, kind="ExternalOutput").ap()

    units = unit_table(F)
    debug_mode = (stop < 99 or nlayers < DEPTH)
    import os
    SKIP = os.environ.get('KSKIP', '').split(',')
    NU = len(units)
    uidx = {u: i for i, u in enumerate(units)}
    wsc = [nc.dram_tensor("wsc%d" % l_, [NU, 128, SLOT], BF16).ap() for l_ in range(L)]
    kcache = nc.dram_tensor("kcache", [L, NT, 4, 128, 1024], BF16).ap()
    vcache = nc.dram_tensor("vcache", [L, NT, 4, 128, 512], BF16).ap()

    with ExitStack() as es:
        S = Sched(nc, es)
        block = es.enter_context(nc.Block())

        def sb(name, shape, dt=F32, stack=es):
            return stack.enter_context(nc.sbuf_tensor("sb_" + name, list(shape), dt))

        X = sb("X", [128, KC, T])
        Xb = sb("Xb", [128, KC, T], BF16)
        mixT = sb("mixT", [128, KC, T], BF16)
        RX = [Res("X%d" % i) for i in range(KC)]
        RXb = [Res("Xb%d" % i) for i in range(KC)]
        Rmix = [Res("mix%d" % i) for i in range(KC)]
        ring = [sb("ring%d" % i, [128, SLOT], BF16) for i in range(NSLOT)]
        Rring = [Res("ring%d" % i) for i in range(NSLOT)]
        ident = sb("ident", [128, 128]); identb = sb("identb", [128, 128], BF16)
        ones = sb("ones", [128, 128]); onesb = sb("onesb", [128, 128], BF16)
        prot = sb("prot", [128, 128], BF16)
        tabA = sb("tabA", [128, 4, 256])
        chA = sb("chA", [128, 4])
        causT = sb("causT", [128, 128]); hgmask = sb("hgmask", [128, 128], U32)
        ud = sb("ud", [128, 128]); ux = sb("ux", [128, 4]); colmask = sb("colmask", [128, 4])
        invd = sb("invd", [128, 1])
        retc = sb("retc", [128, 1536])
        lnG = sb("lnG", [128, 128]); lnB = sb("lnB", [128, 128])
        par = sb("par", [128, 32])
        lbp = sb("lbp", [128, 8]); oml = sb("oml", [128, 8]); noml = sb("noml", [128, 8])
        gA = sb("gA", [128, 2]); nlam = sb("nlam", [128, 2])
        ggT = sb("ggT", [128, L * 512]); gbT = sb("gbT", [128, L * 512])
        wsT = sb("wsT", [128, L * 4, 128], BF16)
        bs2 = sb("bs2", [2, L * 512], BF16)
        S_ret = [sb("S_ret%d" % l, [128, 512]) for l in range(L)]
        Sb_ret = [sb("Sb_ret%d" % l, [128, 512], BF16) for l in range(L)]
        S_hg = [sb("S_hg%d" % l, [128, 512]) for l in range(L)]
        cosT = sb("cosT", [128, T]); sinT = sb("sinT", [128, T])
        Rc = Res("consts")
        RS_ret = [Res() for _ in range(L)]; RSb_ret = [Res() for _ in range(L)]; RS_hg = [Res() for _ in range(L)]
        Rcs = Res("cossin")
        Rkc = [[Res() for _ in range(NT)] for _ in range(L)]
        Rvc = [[Res() for _ in range(NT)] for _ in range(L)]
        Ry = Res("y")

        PSB = [es.enter_context(nc.psum_tensor("ps%d" % i, [128, 512], F32)) for i in range(8)]
        RPS = [Res("ps%d" % i) for i in range(8)]
        ps_state = {"rr": 0, "held": set()}

        def ps_get(hold=False):
            for _ in range(16):
                i = ps_state["rr"]
                ps_state["rr"] = (i + 1) % 8
                if i not in ps_state["held"]:
                    if hold:
                        ps_state["held"].add(i)
                    return i
            raise RuntimeError("no psum bank")

        def ps_release(i):
            ps_state["held"].discard(i)

        def mm(out, lhsT, rhs, start, stop, reads, writes, inc=None):
            S.op("pe", "matmul", out=out, lhsT=lhsT, rhs=rhs, start=start, stop=stop,
                 reads=reads, writes=writes, inc=(stop if inc is None else inc))

        def act(out, in_, func, reads, writes, **kw):
            S.op("act", "activation", out=out, in_=in_, func=func, reads=reads, writes=writes, **kw)

        def tt(e, out, in0, in1, op, reads, writes):
            S.op(e, "tensor_tensor", out=out, in0=in0, in1=in1, op=op, reads=reads, writes=writes)

        def ts(e, out, in0, s1, s2, op0, op1, reads, writes):
            if s2 is None:
                S.op(e, "tensor_scalar", out=out, in0=in0, scalar1=s1, scalar2=None, op0=op0, reads=reads, writes=writes)
            else:
                S.op(e, "tensor_scalar", out=out, in0=in0, scalar1=s1, scalar2=s2, op0=op0, op1=op1, reads=reads, writes=writes)

        def stt(e, out, in0, scalar, in1, op0, op1, reads, writes):
            S.op(e, "scalar_tensor_tensor", out=out, in0=in0, scalar=scalar, in1=in1, op0=op0, op1=op1,
                 reads=reads, writes=writes)

        def cp(e, out, in_, reads, writes):
            if e == "act":
                act(out, in_, AF.Copy, reads, writes)
            else:
                S.op(e, "tensor_copy", out=out, in_=in_, reads=reads, writes=writes)

        def bc(ap, shape):
            return ap.unsqueeze(1).broadcast_to(list(shape))

        def rstd_from(out, in_, scale, reads, writes, tmp, Rtmp):
            act(tmp, in_, AF.Ln, reads, [Rtmp], scale=scale, bias=eps_ap[:, 0:1])
            act(out, tmp, AF.Exp, [Rtmp], writes, scale=-0.5)

        class Phase:
            def __init__(self, prev_ev):
                self.es = ExitStack()
                self.res = []
                self.prev = prev_ev
                self.n = 0

            def tile(self, shape, dt=F32):
                phase_ctr[0] += 1
                t = sb("ph%d" % phase_ctr[0], shape, dt, stack=self.es)
                r = Res()
                r.r = dict(self.prev)
                self.res.append(r)
                return t, r

            def close(self):
                ev = S.events(self.res)
                for k, v in self.prev.items():
                    if ev.get(k, 0) < v:
                        ev[k] = v
                self.es.close()
                return ev

        phase_ctr = [0]
        eps_t = sb("eps_t", [128, 2])
        eps_ap = eps_t

        Rwsc = [[Res() for _ in range(NU)] for _ in range(L)]

        def wsrc(l, tag, idx):
            dst = wsc[l][uidx[(tag, idx)]]
            if tag in ("w1i", "w2i"):
                w = W[tag][l]
                d3 = dst.rearrange("p (kc n) -> p kc n", kc=KC)
                return [(d3[:, :, 0:128], w[:, idx * 128:(idx + 1) * 128].rearrange("(kc p) n -> p kc n", p=128)),
                        (d3[:, :, 128:256], w[:, F + idx * 128:F + (idx + 1) * 128].rearrange("(kc p) n -> p kc n", p=128))]
            if tag in ("w1o", "w2o"):
                hf, m = idx // 16, idx % 16
                w = W[tag][l]
                d3 = dst[:, 0:FH * 128].rearrange("p (fc n) -> p fc n", fc=FH)
                return [(d3, w[hf * FH * 128:(hf + 1) * FH * 128, m * 128:(m + 1) * 128].rearrange("(fc p) n -> p fc n", p=128))]
            if tag in ("wmi", "wmo", "wpg"):
                w = W[tag][l]
                d3 = dst.rearrange("p (kc n) -> p kc n", kc=KC)
                return [(d3, w[:, idx * 256:(idx + 1) * 256].rearrange("(kc p) n -> p kc n", p=128))]
            if tag == "wpp":
                w = W[tag][l]
                d3 = dst.rearrange("p (kc n) -> p kc n", kc=2)
                return [(d3, w.rearrange("(kc p) n -> p kc n", p=128))]
            raise KeyError(tag)

        pro_list = []
        for l in range(L):
            for (tag, idx) in units:
                pro_list.append((l, uidx[(tag, idx)], wsrc(l, tag, idx)))
        pro = {"ptr": 0}

        def pump_until(l, u, ahead=6):
            return

        def pump_all():
            target = len(pro_list) if 'prologue' not in SKIP else 0
            while pro["ptr"] < target:
                lj, uj, lst = pro_list[pro["ptr"]]
                for dv, sv in lst:
                    S.dma("pool", dv, sv, writes=[Rwsc[lj][uj]])
                pro["ptr"] += 1

        pump_all()

        stream_seq = []
        for t_ in range(NT):
            for l in range(L):
                for u in range(NU):
                    stream_seq.append((l, u))
        st = {"issued": 0, "next": 0}

        def w_next(l, tag, idx):
            u = uidx[(tag, idx)]
            i = st["next"]
            assert stream_seq[i] == (l, u), (stream_seq[i], (l, u, tag, idx))
            if debug_mode:
                k = st["issued"]
                pump_until(l, u)
                S.dma("sp", ring[k % NSLOT][:], wsc[l][u], reads=[Rwsc[l][u]], writes=[Rring[k % NSLOT]])
                st["issued"] += 1
                st["next"] += 1
                return ring[k % NSLOT], Rring[k % NSLOT]
            while st["issued"] < min(len(stream_seq), i + NSLOT):
                j = st["issued"]
                lj, uj = stream_seq[j]
                pump_until(lj, uj)
                S.dma("sp", ring[j % NSLOT][:], wsc[lj][uj], reads=[Rwsc[lj][uj]], writes=[Rring[j % NSLOT]])
                st["issued"] += 1
            st["next"] += 1
            return ring[i % NSLOT], Rring[i % NSLOT]

        ph = Phase({})
        stg, Rstg = ph.tile([128, 128])
        for name, dst in (("c_ident", ident), ("c_causT", causT), ("c_ud", ud)):
            S.dma("sp", dst[:], C[name], writes=[Rc])
        S.dma("sp", ux[:], C["c_ux"], writes=[Rc])
        S.dma("sp", colmask[:], C["c_colmask"], writes=[Rc])
        S.dma("sp", invd[:], C["c_invd"], writes=[Rc])
        S.dma("sp", retc[:], C["c_ret"].partition_broadcast(128), writes=[Rc])
        S.dma("sp", ggT[:], gg_d.partition_broadcast(128), writes=[Rc])
        S.dma("sp", gbT[:], gb_d.partition_broadcast(128), writes=[Rc])
        S.op("pool", "memset", ap=ones[:], constant=1.0, writes=[Rc])
        S.op("pool", "memset", ap=onesb[:], constant=1.0, writes=[Rc])
        S.op("pool", "memset", ap=eps_t[:, 0:1], constant=EPS, writes=[Rc])
        S.op("pool", "memset", ap=eps_t[:, 1:2], constant=EPS_LN, writes=[Rc])
        for l in range(L):
            S.op("pool", "memset", ap=S_ret[l][:], constant=0.0, writes=[RS_ret[l]])
            S.op("pool", "memset", ap=Sb_ret[l][:], constant=0.0, writes=[RSb_ret[l]])
            S.op("pool", "memset", ap=S_hg[l][:], constant=0.0, writes=[RS_hg[l]])
        cp("dve", identb[:], ident[:], [Rc], [Rc])
        S.dma("sp", stg[:], C["c_prot"], writes=[Rstg])
        cp("dve", prot[:], stg[:], [Rstg], [Rc])
        S.dma("sp", stg[:], C["c_hgmask"], writes=[Rstg])
        cp("dve", hgmask[:], stg[:], [Rstg], [Rc])
        if 'lnp' not in SKIP:
            for src, dst in ((lng_d, lnG), (lnb_d, lnB)):
                S.dma("sp", stg[:], src, writes=[Rstg])
                b = ps_get()
                S.op("pe", "transpose", out=PSB[b][:, 0:128], in_=stg[:], identity=ident[:], reads=[Rstg, Rc], writes=[RPS[b]])
                cp("dve", dst[:], PSB[b][:, 0:128], [RPS[b]], [Rc])
        if 'par' not in SKIP:
            S.op("pool", "memset", ap=stg[:], constant=0.0, writes=[Rstg])
            S.dma("sp", stg[0:32, :], par_d, writes=[Rstg])
            b = ps_get()
            S.op("pe", "transpose", out=PSB[b][:, 0:128], in_=stg[:], identity=ident[:], reads=[Rstg, Rc], writes=[RPS[b]])
            cp("dve", par[:], PSB[b][:, 0:32], [RPS[b]], [Rc])
            S.op("pool", "memset", ap=lbp[:], constant=0.0, writes=[Rc])
            tmp8, Rtmp8 = ph.tile([128, 8])
            tt("dve", tmp8[:, 0:4], par[:, 14:18], par[:, 10:14], ALU.subtract, [Rc], [Rtmp8])
            act(lbp[:, 4:8], tmp8[:, 0:4], AF.Sigmoid, [Rtmp8], [Rc])
            ts("dve", lbp[:], lbp[:], 1e-30, None, ALU.max, None, [Rc], [Rc])
            ts("dve", oml[:], lbp[:], -1.0, 1.0, ALU.mult, ALU.add, [Rc], [Rc])
            ts("dve", noml[:], oml[:], -1.0, None, ALU.mult, None, [Rc], [Rc])
        if 'lam' not in SKIP:
            dl, Rdl = ph.tile([128, L * 256])
            S.dma("sp", dl[:], dlam_d.partition_broadcast(128), writes=[Rdl])
            pr, Rpr = ph.tile([128, 64])
            sm, Rsm = ph.tile([128, 4])
            for l in range(L):
                lam_init = 0.8 - 0.6 * math.exp(-0.3 * l)
                for i in range(2):
                    a0 = l * 256 + i * 128
                    tt("dve", pr[:], dl[:, a0:a0 + 64], dl[:, a0 + 64:a0 + 128], ALU.mult, [Rdl], [Rpr])
                    S.op("dve", "reduce_sum", out=sm[:, i:i + 1], in_=pr[:], axis=mybir.AxisListType.X, reads=[Rpr], writes=[Rsm])
                act(sm[:, 2:4], sm[:, 0:2], AF.Exp, [Rsm], [Rsm])
                tt("dve", nlam[:, l:l + 1], sm[:, 3:4], sm[:, 2:3], ALU.subtract, [Rsm], [Rc])
                ts("dve", nlam[:, l:l + 1], nlam[:, l:l + 1], -lam_init, None, ALU.add, None, [Rc], [Rc])
                ts("dve", gA[:, l:l + 1], par[:, l:l + 1], 1.0 - lam_init, None, ALU.mult, None, [Rc], [Rc])
        if 'tab' not in SKIP:
            rbB, RrbB = ph.tile([128, 128])
            S.dma("sp", rbB[:], relb_d.partition_broadcast(128), writes=[RrbB])
            cp("dve", chA[:], rbB[:, 124:128], [RrbB], [Rc])
            S.op("pool", "memset", ap=tabA[:], constant=0.0, writes=[Rc])
            ohb, Rohb = ph.tile([128, 8, 256])
            for g8 in range(4):
                S.dma("sp", ohb[:], C["c_oh"][:, g8 * 8:(g8 + 1) * 8, :], writes=[Rohb])
                for bb in range(8):
                    bk = g8 * 8 + bb
                    for h in range(4):
                        stt("dve", tabA[:, h, :], ohb[:, bb, :], rbB[:, bk * 4 + h:bk * 4 + h + 1], tabA[:, h, :],
                            ALU.mult, ALU.add, [Rohb, RrbB, Rc], [Rc])
            S.dma("sp", stg[:], C["c_negmask"], writes=[Rstg])
            for h in range(4):
                tt("dve", tabA[:, h, 0:128], tabA[:, h, 0:128], stg[:], ALU.add, [Rc, Rstg], [Rc])
        if 'gws' not in SKIP:
            tril, Rtril = ph.tile([128, 128])
            S.dma("sp", tril[:], C["c_tril"], writes=[Rtril])
            wst, Rwst = ph.tile([128, 128])
            for l in range(L):
                for g in range(4):
                    S.dma("sp", wst[:], gws_d[l, g], writes=[Rwst])
                    tt("dve", wst[:], wst[:], tril[:], ALU.mult, [Rwst, Rtril], [Rwst])
                    b = ps_get()
                    S.op("pe", "transpose", out=PSB[b][:, 0:128], in_=wst[:], identity=ident[:], reads=[Rwst, Rc], writes=[RPS[b]])
                    cp("dve", wsT[:, l * 4 + g, :], PSB[b][:, 0:128], [RPS[b]], [Rc])
        if 'bs2' not in SKIP:
            b2f, Rb2f = ph.tile([2, L * 512])
            b2h, Rb2h = ph.tile([2, L * 512], BF16)
            b2g, Rb2g = ph.tile([2, L * 512])
            S.dma("sp", b2f[:], gbs_d.partition_broadcast(2), writes=[Rb2f])
            cp("dve", b2h[:], b2f[:], [Rb2f], [Rb2h])
            cp("dve", b2g[:], b2h[:], [Rb2h], [Rb2g])
            tt("dve", b2f[:], b2f[:], b2g[:], ALU.subtract, [Rb2f, Rb2g], [Rb2f])
            ts("dve", b2g[:], b2g[:], colmask[0:2, 2:3], None, ALU.mult, None, [Rb2g, Rc], [Rb2g])
            stt("dve", bs2[:], b2f[:], colmask[0:2, 3:4], b2g[:], ALU.mult, ALU.add, [Rb2f, Rb2g, Rc], [Rc])
        prev_ev = ph.close()

        def ln_begin():
            b1 = ps_get(hold=True)
            b2 = ps_get(hold=True)
            return {"b1": b1, "b2": b2, "n": 0}

        def resid_chunk(lnst, m, Yap, Yreads, coef, sq2, Rsq2, extra_in1=None):
            stt("dve", X[:, m, :], Yap, coef, X[:, m, :], ALU.mult, ALU.add, Yreads + [RX[m]], [RX[m]])
            if lnst is not None:
                i = lnst["n"]
                sq, Rsq = sq2[i % 2], Rsq2[i % 2]
                act(sq[:], X[:, m, :], AF.Square, [RX[m]], [Rsq])
                cp("pool", Xb[:, m, :], X[:, m, :], [RX[m]], [RXb[m]])
                mm(PSB[lnst["b1"]][:], onesb[:], Xb[:, m, :], i == 0, i == KC - 1, [Rc, RXb[m]], [RPS[lnst["b1"]]])
                mm(PSB[lnst["b2"]][:], onesb[:], sq[:], i == 0, i == KC - 1, [Rc, Rsq], [RPS[lnst["b2"]]])
                lnst["n"] += 1

        def ln_finish(lnst, l, i, P):
            b1, b2 = lnst["b1"], lnst["b2"]
            mean, Rmean = P.tile([128, T])
            rstd, Rrstd = P.tile([128, T])
            t1, Rt1 = P.tile([128, T])
            ts("dve", mean[:], PSB[b1][:], 1.0 / D, None, ALU.mult, None, [RPS[b1]], [Rmean])
            tt("pool", t1[:], mean[:], mean[:], ALU.mult, [Rmean], [Rt1])
            stt("dve", t1[:], PSB[b2][:], 1.0 / D, t1[:], ALU.mult, ALU.subtract, [RPS[b2], Rt1], [Rt1])
            act(t1[:], t1[:], AF.Ln, [Rt1], [Rt1], bias=eps_ap[:, 1:2])
            act(rstd[:], t1[:], AF.Exp, [Rt1], [Rrstd], scale=-0.5)
            ps_release(b1)
            ps_release(b2)
            for m in range(KC):
                col = (l * 4 + i) * KC + m
                e = "dve" if m % 2 == 0 else "pool"
                tt(e, X[:, m, :], X[:, m, :], mean[:], ALU.subtract, [RX[m], Rmean], [RX[m]])
                tt(e, X[:, m, :], X[:, m, :], rstd[:], ALU.mult, [RX[m], Rrstd], [RX[m]])
                act(X[:, m, :], X[:, m, :], AF.Identity, [RX[m], Rc], [RX[m]], scale=lnG[:, col:col + 1], bias=lnB[:, col:col + 1])
                cp("pool" if m % 2 == 0 else "dve", Xb[:, m, :], X[:, m, :], [RX[m]], [RXb[m]])

        def ffn(l, tagi, tago, lni, prev):
            P = Phase(prev)
            G, RG_ = P.tile([128, FH, T], BF16)
            RG = [Res() for _ in range(FH)]
            for r in RG:
                r.r = dict(prev)
            P.res.extend(RG)
            sg2 = [P.tile([128, T]) for _ in range(2)]
            sq2 = [P.tile([128, T], BF16) for _ in range(2)]
            lnst = None
            for hf in range(2):
                for jj in range(FH):
                    j = hf * FH + jj
                    slot, Rs = w_next(l, tagi, j)
                    s3 = slot[:].rearrange("p (kc n) -> p kc n", kc=KC)
                    bg = ps_get(); bu = ps_get()
                    for kc in range(KC):
                        mm(PSB[bg][:], s3[:, kc, 0:128], Xb[:, kc, :], kc == 0, kc == KC - 1, [Rs, RXb[kc]], [RPS[bg]])
                    for kc in range(KC):
                        mm(PSB[bu][:], s3[:, kc, 128:256], Xb[:, kc, :], kc == 0, kc == KC - 1, [Rs, RXb[kc]], [RPS[bu]])
                    sg, Rsg = sg2[jj % 2]
                    act(sg[:], PSB[bg][:], AF.Silu, [RPS[bg]], [Rsg])
                    tt("dve", G[:, jj, :], sg[:], PSB[bu][:], ALU.mult, [Rsg, RPS[bu]], [RG[jj]])
                if hf == 1:
                    lnst = ln_begin()
                for m in range(KC):
                    slot, Rs = w_next(l, tago, hf * 16 + m)
                    s3 = slot[:, 0:FH * 128].rearrange("p (fc n) -> p fc n", fc=FH)
                    by = ps_get()
                    for fc in range(FH):
                        mm(PSB[by][:], s3[:, fc, :], G[:, fc, :], fc == 0, fc == FH - 1, [Rs, RG[fc]], [RPS[by]])
                    resid_chunk(lnst, m, PSB[by][:], [RPS[by]], 0.5 / ALPHA,
                                [s[0] for s in sq2], [s[1] for s in sq2])
            ln_finish(lnst, l, lni, P)
            return P.close()

        def proj_fm(l, u):
            slot, Rs = w_next(l, "wmi", u)
            s3 = slot[:].rearrange("p (kc n) -> p kc n", kc=KC)
            for j in range(2):
                b = ps_get()
                for kc in range(KC):
                    mm(PSB[b][:], s3[:, kc, j * 128:(j + 1) * 128], Xb[:, kc, :], kc == 0, kc == KC - 1, [Rs, RXb[kc]], [RPS[b]])
                yield j, b

        def proj_tm(l, u):
            slot, Rs = w_next(l, "wmi", u)
            s3 = slot[:].rearrange("p (kc n) -> p kc n", kc=KC)
            for sub in range(4):
                b = ps_get()
                for kc in range(KC):
                    mm(PSB[b][:, 0:256], Xb[:, kc, sub * 128:(sub + 1) * 128], s3[:, kc, :], kc == 0, kc == KC - 1,
                       [Rs, RXb[kc]], [RPS[b]])
                yield sub, b

        SCALE_A = 64 ** -0.5

        def mixer_A(l, t, prev):
            P = Phase(prev)
            qT = [P.tile([128, T], BF16) for _ in range(4)]
            kpad = [P.tile([128, 2, T], BF16) for _ in range(4)]
            vtok, Rvtok = P.tile([128, 4, 512], BF16)
            kbuf = [P.tile([128, 2, T], BF16) for _ in range(3)]
            vbuf = [P.tile([128, 4, 128], BF16) for _ in range(3)]
            PT = [P.tile([128, T], BF16) for _ in range(4)]
            tmpd = [P.tile([128, 128]) for _ in range(2)]
            r0, Rr0 = P.tile([128, T]); t0, Rt0 = P.tile([128, T])
            r1, Rr1 = P.tile([128, T]); t1, Rt1 = P.tile([128, T])
            sqb, Rsqb = P.tile([128, T], BF16)
            for h in range(4):
                S.op("pool", "memset", ap=kpad[h][0][64:128, 0, :], constant=0.0, writes=[kpad[h][1]])
                S.op("pool", "memset", ap=kpad[h][0][0:64, 1, :], constant=0.0, writes=[kpad[h][1]])
            for u in (0, 1):
                for j, b in proj_fm(l, u):
                    h = u * 2 + j
                    cp("act", qT[h][0][:], PSB[b][:], [RPS[b]], [qT[h][1]])
            for u in (2, 3):
                for j, b in proj_fm(l, u):
                    h = (u - 2) * 2 + j
                    cp("act", kpad[h][0][0:64, 0, :], PSB[b][0:64, :], [RPS[b]], [kpad[h][1]])
                    cp("dve", kpad[h][0][64:128, 1, :], PSB[b][64:128, :], [RPS[b]], [kpad[h][1]])
            for u in (4, 5):
                for sub, b in proj_tm(l, u):
                    cp("act" if sub % 2 else "dve", vtok[:, sub, (u - 4) * 256:(u - 3) * 256], PSB[b][:, 0:256], [RPS[b]], [Rvtok])
            if t < NT - 1:
                for h in range(4):
                    S.dma("pool", kcache[l, t, h], kpad[h][0][:].rearrange("p c n -> p (c n)"), reads=[kpad[h][1]], writes=[Rkc[l][t]])
                for h in range(4):
                    S.dma("pool", vcache[l, t, h].rearrange("p (s n) -> p s n", s=4), vtok[:, :, h * 128:(h + 1) * 128], reads=[Rvtok], writes=[Rvc[l][t]])
            nb = 0
            npt = [0]
            PIPE = 2
            for h in range(4):
                acc = [ps_get(hold=True) for _ in range(4)]
                jobs = []
                for kt in range(t + 1):
                    if kt < t:
                        kb_t, Rkb = kbuf[nb % 3]
                        vb_t, Rvb = vbuf[nb % 3]
                        nb += 1
                        ld = (kb_t, Rkb, vb_t, Rvb, kt)
                        kview, vview, Rv_ = kb_t, vb_t[:], Rvb
                    else:
                        ld = None
                        kview, Rkb = kpad[h]
                        vview = vtok[:, :, h * 128:(h + 1) * 128]
                        Rv_ = Rvtok
                    for kb in range(4):
                        q0 = kb * 128 if kt == t else 0
                        for c in range(2):
                            jobs.append(dict(kt=kt, kb=kb, c=c, q0=q0, kview=kview, Rkb=Rkb, vview=vview, Rv=Rv_,
                                             first=(kt == 0 and kb == 0), last=(kt == t and kb == 3),
                                             ld=(ld if (kb == 0 and c == 0) else None)))

                def emit_scores(J):
                    if J["ld"] is not None:
                        kb_t, Rkb_, vb_t, Rvb_, kt_l = J["ld"]
                        S.dma("pool", kb_t[:].rearrange("p c n -> p (c n)"), kcache[l, kt_l, h], reads=[Rkc[l][kt_l]], writes=[Rkb_])
                        S.dma("pool", vb_t[:].rearrange("p s n -> p (s n)"), vcache[l, kt_l, h], reads=[Rvc[l][kt_l]], writes=[Rvb_])
                    kt, kb, c, q0 = J["kt"], J["kb"], J["c"], J["q0"]
                    b = ps_get()
                    mm(PSB[b][:, q0:T], J["kview"][:, c, kb * 128:(kb + 1) * 128], qT[h][0][:, q0:T], True, True,
                       [J["Rkb"], qT[h][1]], [RPS[b]])
                    pt, Rpt = PT[npt[0] % 4]
                    npt[0] += 1
                    far0 = None
                    for qb in range(q0 // 128, 4):
                        rel = (4 * t + qb) - (4 * kt + kb)
                        if rel >= 2:
                            if far0 is None:
                                far0 = qb
                            continue
                        td, Rtd = tmpd[(npt[0] + qb) % 2]
                        stt("dve", td[:], PSB[b][:, qb * 128:(qb + 1) * 128], SCALE_A,
                            tabA[:, h, rel * 128:(rel + 1) * 128], ALU.mult, ALU.add, [RPS[b], Rc], [Rtd])
                        act(pt[:, qb * 128:(qb + 1) * 128], td[:], AF.Exp, [Rtd], [Rpt])
                    if far0 is not None:
                        act(pt[:, far0 * 128:T], PSB[b][:, far0 * 128:T], AF.Exp, [RPS[b], Rc], [Rpt],
                            scale=SCALE_A, bias=chA[:, h:h + 1])
                    J["pt"] = (pt, Rpt)

                def emit_pv(J):
                    pt, Rpt = J["pt"]
                    kb, c, q0 = J["kb"], J["c"], J["q0"]
                    mm(PSB[acc[c]][:, q0:T], J["vview"][:, kb, :], pt[:, q0:T], J["first"], J["last"], [J["Rv"], Rpt], [RPS[acc[c]]], inc=False)
                    mm(PSB[acc[2 + c]][:, q0:T], onesb[:], pt[:, q0:T], J["first"], J["last"], [Rc, Rpt], [RPS[acc[2 + c]]], inc=True)

                pending = []
                for J in jobs:
                    emit_scores(J)
                    pending.append(J)
                    if len(pending) > PIPE:
                        emit_pv(pending.pop(0))
                while pending:
                    emit_pv(pending.pop(0))
                S.op("dve", "reciprocal", out=r0[:], in_=PSB[acc[2]][:], reads=[RPS[acc[2]]], writes=[Rr0])
                tt("dve", t0[:], PSB[acc[0]][:], r0[:], ALU.mult, [RPS[acc[0]], Rr0], [Rt0])
                S.op("dve", "reciprocal", out=r1[:], in_=PSB[acc[3]][:], reads=[RPS[acc[3]]], writes=[Rr1])
                tt("dve", t1[:], PSB[acc[1]][:], r1[:], ALU.mult, [RPS[acc[1]], Rr1], [Rt1])
                for a in acc:
                    ps_release(a)
                stt("dve", t0[:], t1[:], nlam[:, l:l + 1], t0[:], ALU.mult, ALU.add, [Rt1, Rt0, Rc], [Rt0])
                act(sqb[:], t0[:], AF.Square, [Rt0], [Rsqb])
                b = ps_get()
                mm(PSB[b][:], onesb[:], sqb[:], True, True, [Rc, Rsqb], [RPS[b]])
                rstd_from(r0[:], PSB[b][:], 1.0 / 128, [RPS[b]], [Rr0], r1[:], Rr1)
                stt("dve", mixT[:, h, :], t0[:], gA[:, l:l + 1], r0[:], ALU.mult, ALU.mult, [Rt0, Rr0, Rc], [Rmix[h]])
            return P.close()

        def rope_tables(t, prev):
            P = Phase(prev)
            pi_t, Rpi = P.tile([128, T], I32)
            ang, Rang = P.tile([128, T])
            w1, Rw1 = P.tile([128, T])
            ki, Rki = P.tile([128, T], I32)
            S.dma("pool", pi_t[:], pos_d[:, t * T:(t + 1) * T].partition_broadcast(128), writes=[Rpi])
            cp("dve", ang[:], pi_t[:], [Rpi], [Rang])
            ts("dve", ang[:], ang[:], invd[:, 0:1], None, ALU.mult, None, [Rang, Rc], [Rang])
            for which, dst in ((0, sinT), (1, cosT)):
                src = ang
                if which == 1:
                    ts("dve", w1[:], ang[:], math.pi / 2, None, ALU.add, None, [Rang], [Rw1])
                    src = w1
                kf, Rkf = P.tile([128, T])
                ts("dve", kf[:], src[:], 1.0 / (2 * math.pi), None, ALU.mult, None, [Rang, Rw1], [Rkf])
                cp("dve", ki[:], kf[:], [Rkf], [Rki])
                cp("dve", kf[:], ki[:], [Rki], [Rkf])
                stt("dve", kf[:], kf[:], -2 * math.pi, src[:], ALU.mult, ALU.add, [Rkf, Rang, Rw1], [Rkf])
                ts("dve", kf[:], kf[:], 3.141592, -3.141592, ALU.min, ALU.max, [Rkf], [Rkf])
                act(dst[:], kf[:], AF.Sin, [Rkf], [Rcs])
            return P.close()

        def mixer_B(l, t, prev):
            P = Phase(prev)
            qt = [P.tile([128, T], BF16) for _ in range(4)]
            kt_ = [P.tile([128, T], BF16) for _ in range(4)]
            gs = [P.tile([128, T], BF16) for _ in range(4)]
            vtok, Rvtok = P.tile([128, 4, 512], BF16)
            P1 = Phase(prev)
            raw2 = [P1.tile([128, T], BF16) for _ in range(2)]
            a1, Ra1 = P1.tile([128, T]); a2, Ra2 = P1.tile([128, T])
            nraw = 0
            for (u0, dstl, goff) in ((6, qt, 0), (8, kt_, 512)):
                for u in (u0, u0 + 1):
                    for j, b in proj_fm(l, u):
                        h = (u - u0) * 2 + j
                        raw, Rraw = raw2[nraw % 2]
                        nraw += 1
                        cp("act", raw[:], PSB[b][:], [RPS[b]], [Rraw])
                        b2 = ps_get()
                        mm(PSB[b2][:], prot[:], raw[:], True, True, [Rc, Rraw], [RPS[b2]])
                        tt("dve", a1[:], raw[:], cosT[:], ALU.mult, [Rraw, Rcs], [Ra1])
                        tt("dve", a2[:], PSB[b2][:], sinT[:], ALU.mult, [RPS[b2], Rcs], [Ra2])
                        tt("pool", a1[:], a1[:], a2[:], ALU.add, [Ra1, Ra2], [Ra1])
                        tt("dve", dstl[h][0][:].rearrange("p (c n) -> p c n", c=4), a1[:].rearrange("p (c n) -> p c n", c=4),
                           bc(retc[:, goff + h * 128:goff + (h + 1) * 128], [128, 4, 128]), ALU.mult, [Ra1, Rc], [dstl[h][1]])
            ev1 = P1.close()
            for u in (10, 11):
                for sub, b in proj_tm(l, u):
                    cp("act" if sub % 2 else "dve", vtok[:, sub, (u - 10) * 256:(u - 9) * 256], PSB[b][:, 0:256], [RPS[b]], [Rvtok])
            for u in (12, 13):
                for j, b in proj_fm(l, u):
                    h = (u - 12) * 2 + j
                    act(gs[h][0][:], PSB[b][:], AF.Silu, [RPS[b]], [gs[h][1]])
            P2 = Phase(ev1)
            Pm = [P2.tile([128, 512], BF16) for _ in range(4)]
            ktok = [P2.tile([128, 4, 128], BF16) for _ in range(2)]
            Sbv = [P2.tile([128, 512], BF16) for _ in range(4)]
            tmpS, RtmpS = P2.tile([128, 512])
            ob, Rob = P2.tile([128, T], BF16); sqb, Rsqb = P2.tile([128, T], BF16)
            mean, Rmean = P2.tile([128, T]); var, Rvar = P2.tile([128, T]); dd, Rdd = P2.tile([128, T])
            for c in range(4):
                cs = slice(c * 128, (c + 1) * 128)
                b = ps_get()
                for h in range(4):
                    mm(PSB[b][:, h * 128:(h + 1) * 128], kt_[h][0][:, cs], qt[h][0][:, cs], True, True,
                       [kt_[h][1], qt[h][1]], [RPS[b]], inc=(h == 3))
                tt("dve", Pm[c][0][:].rearrange("p (h n) -> p h n", h=4), PSB[b][:].rearrange("p (h n) -> p h n", h=4),
                   bc(causT[:], [128, 4, 128]), ALU.mult, [RPS[b], Rc], [Pm[c][1]])
                b = ps_get()
                for h in range(4):
                    mm(PSB[b][:, h * 128:(h + 1) * 128], kt_[h][0][:, cs], identb[:], True, True, [kt_[h][1], Rc], [RPS[b]], inc=(h == 3))
                ktk, Rktk = ktok[c % 2]
                cp("act", ktk[:].rearrange("p h n -> p (h n)"), PSB[b][:], [RPS[b]], [Rktk])
                b = ps_get()
                for h in range(4):
                    mm(PSB[b][:, h * 128:(h + 1) * 128], ktk[:, h, :], vtok[:, c, h * 128:(h + 1) * 128], True, True,
                       [Rktk, Rvtok], [RPS[b]], inc=(h == 3))
                tt("dve", tmpS[:], PSB[b][:], S_ret[l][:], ALU.add, [RPS[b], RS_ret[l]], [RtmpS])
                tt("pool", S_ret[l][:], tmpS[:], retc[:, 1024:1536], ALU.mult, [RtmpS, Rc], [RS_ret[l]])
                if c < 3:
                    cp("act", Sbv[c + 1][0][:], S_ret[l][:], [RS_ret[l]], [Sbv[c + 1][1]])
            for h in range(4):
                hs = slice(h * 128, (h + 1) * 128)
                b = ps_get()
                for c in range(4):
                    cs = slice(c * 128, (c + 1) * 128)
                    mm(PSB[b][:, cs], vtok[:, c, hs], Pm[c][0][:, hs], True, False, [Rvtok, Pm[c][1]], [RPS[b]], inc=False)
                    if c == 0:
                        sbt, Rsbt = Sb_ret[l], RSb_ret[l]
                    else:
                        sbt, Rsbt = Sbv[c]
                    mm(PSB[b][:, cs], sbt[:, hs], qt[h][0][:, cs], False, True, [Rsbt, qt[h][1]], [RPS[b]], inc=(c == 3))
                cp("act", ob[:], PSB[b][:], [RPS[b]], [Rob])
                act(sqb[:], PSB[b][:], AF.Square, [RPS[b]], [Rsqb])
                b1 = ps_get(); b2 = ps_get()
                mm(PSB[b1][:], onesb[:], ob[:], True, True, [Rc, Rob], [RPS[b1]])
                mm(PSB[b2][:], onesb[:], sqb[:], True, True, [Rc, Rsqb], [RPS[b2]])
                ts("dve", mean[:], PSB[b1][:], 1.0 / 128, None, ALU.mult, None, [RPS[b1]], [Rmean])
                tt("pool", var[:], mean[:], mean[:], ALU.mult, [Rmean], [Rvar])
                stt("dve", var[:], PSB[b2][:], 1.0 / 128, var[:], ALU.mult, ALU.subtract, [RPS[b2], Rvar], [Rvar])
                act(var[:], var[:], AF.Ln, [Rvar, Rc], [Rvar], bias=eps_ap[:, 0:1])
                act(var[:], var[:], AF.Exp, [Rvar], [Rvar], scale=-0.5)
                tt("dve", dd[:], PSB[b][:], mean[:], ALU.subtract, [RPS[b], Rmean], [Rdd])
                tt("pool", dd[:], dd[:], var[:], ALU.mult, [Rdd, Rvar], [Rdd])
                tt("dve", mixT[:, 4 + h, :], dd[:], gs[h][0][:], ALU.mult, [Rdd, gs[h][1]], [Rmix[4 + h]])
            cp("act", Sb_ret[l][:], S_ret[l][:], [RS_ret[l]], [RSb_ret[l]])
            ev2 = P2.close()
            P.prev = ev2
            return P.close()

        def mixer_C(l, t, prev):
            P = Phase(prev)
            uT = [P.tile([128, T], BF16) for _ in range(4)]
            vg = [P.tile([128, 512]) for _ in range(4)]
            vnb = [P.tile([128, 512], BF16) for _ in range(4)]
            st6, Rst6 = P.tile([128, 8]); mv, Rmv = P.tile([128, 4])
            for u in (14, 15):
                for j, b in proj_fm(l, u):
                    g = (u - 14) * 2 + j
                    act(uT[g][0][:], PSB[b][:], AF.Gelu, [RPS[b]], [uT[g][1]])
            for u in (16, 17):
                for sub, b in proj_tm(l, u):
                    act(vg[sub][0][:, (u - 16) * 256:(u - 15) * 256], PSB[b][:, 0:256], AF.Gelu, [RPS[b]], [vg[sub][1]])
            for sub in range(4):
                v_, Rv_ = vg[sub]
                S.op("dve", "bn_stats", out=st6[:, 0:6], in_=v_[:], reads=[Rv_], writes=[Rst6])
                S.op("dve", "bn_aggr", out=mv[:, 0:2], in_=st6[:, 0:6], reads=[Rst6], writes=[Rmv])
                act(mv[:, 2:3], mv[:, 1:2], AF.Ln, [Rmv, Rc], [Rmv], bias=eps_ap[:, 0:1])
                act(mv[:, 3:4], mv[:, 2:3], AF.Exp, [Rmv], [Rmv], scale=-0.5)
                ts("dve", v_[:], v_[:], mv[:, 0:1], mv[:, 3:4], ALU.subtract, ALU.mult, [Rv_, Rmv], [Rv_])
                tt("pool", v_[:], v_[:], ggT[:, l * 512:(l + 1) * 512], ALU.mult, [Rv_, Rc], [Rv_])
                tt("dve", vnb[sub][0][:], v_[:], gbT[:, l * 512:(l + 1) * 512], ALU.add, [Rv_, Rc], [vnb[sub][1]])
            for g in range(4):
                gsl = slice(g * 128, (g + 1) * 128)
                b = ps_get()
                for sub in range(4):
                    cs = slice(sub * 128, (sub + 1) * 128)
                    mm(PSB[b][:, cs], vnb[sub][0][:, gsl], wsT[:, l * 4 + g, :], True, False, [vnb[sub][1], Rc], [RPS[b]], inc=False)
                    mm(PSB[b][:, cs], onesb[0:2, :], bs2[:, l * 512 + g * 128:l * 512 + (g + 1) * 128], False, True,
                       [Rc], [RPS[b]], inc=(sub == 3))
                tt("dve", mixT[:, 8 + g, :], uT[g][0][:], PSB[b][:], ALU.mult, [uT[g][1], RPS[b]], [Rmix[8 + g]])
            return P.close()

        def mixer_D(l, t, prev):
            P = Phase(prev)
            qt = [P.tile([128, T], BF16) for _ in range(4)]
            kt_ = [P.tile([128, T], BF16) for _ in range(4)]
            itok, Ritok = P.tile([128, 4, 512], BF16)
            E1s, RE1s = P.tile([128, 4, 8]); E2s, RE2s = P.tile([128, 4, 8])
            for u in (18, 19):
                for j, b in proj_fm(l, u):
                    h = (u - 18) * 2 + j
                    cp("act", qt[h][0][:], PSB[b][:], [RPS[b]], [qt[h][1]])
            P1 = Phase(prev)
            sg, Rsg = P1.tile([128, T]); keyp, Rkeyp = P1.tile([128, T])
            e1, Re1 = P1.tile([128, T]); e2, Re2 = P1.tile([128, T])
            logf, Rlogf = P1.tile([128, T])
            lft, Rlft = P1.tile([128, 4, 128])
            x8, Rx8 = P1.tile([128, 8])
            for u in (20, 21):
                for j, b in proj_fm(l, u):
                    h = (u - 20) * 2 + j
                    col = l * 4 + h
                    act(sg[:], PSB[b][:], AF.Sigmoid, [RPS[b]], [Rsg])
                    ts("dve", logf[:], sg[:], oml[:, col:col + 1], lbp[:, col:col + 1], ALU.mult, ALU.add, [Rsg, Rc], [Rlogf])
                    act(logf[:], logf[:], AF.Ln, [Rlogf], [Rlogf])
                    ts("dve", keyp[:], sg[:], noml[:, col:col + 1], oml[:, col:col + 1], ALU.mult, ALU.add, [Rsg, Rc], [Rkeyp])
                    bt = ps_get()
                    for c in range(4):
                        S.op("pe", "transpose", out=PSB[bt][:, c * 128:(c + 1) * 128], in_=logf[:, c * 128:(c + 1) * 128], identity=ident[:],
                             reads=[Rlogf, Rc], writes=[RPS[bt]], inc=(c == 3))
                    cp("dve", lft[:].rearrange("p c n -> p (c n)"), PSB[bt][:], [RPS[bt]], [Rlft])
                    bd_ = ps_get()
                    for c in range(4):
                        cs = slice(c * 128, (c + 1) * 128)
                        mm(PSB[bd_][:, cs], lft[:, c, :], ud[:], True, True, [Rlft, Rc], [RPS[bd_]], inc=(c == 3))
                    act(e2[:], PSB[bd_][:], AF.Exp, [RPS[bd_]], [Re2], scale=-1.0)
                    tt("dve", kt_[h][0][:], keyp[:], e2[:], ALU.mult, [Rkeyp, Re2], [kt_[h][1]])
                    act(e1[:], PSB[bd_][:], AF.Exp, [RPS[bd_]], [Re1])
                    tt("dve", qt[h][0][:], qt[h][0][:], e1[:], ALU.mult, [qt[h][1], Re1], [qt[h][1]])
                    e1v = e1[:].rearrange("p (cj n) -> p cj n", n=64)
                    lfv = logf[:].rearrange("p (cj n) -> p cj n", n=64)
                    cp("dve", E2s[:, h, :], e1v[:, :, 63], [Re1], [RE2s])
                    cp("dve", x8[:, 0:8], PSB[bd_][:].rearrange("p (cj n) -> p cj n", n=64)[:, :, 0], [RPS[bd_]], [Rx8])
                    tt("dve", x8[:, 0:8], lfv[:, :, 0], x8[:, 0:8], ALU.subtract, [Rlogf, Rx8], [Rx8])
                    act(E1s[:, h, :], x8[:, 0:8], AF.Exp, [Rx8], [RE1s])
            ev1 = P1.close()
            for u in (22, 23):
                for sub, b in proj_tm(l, u):
                    cp("act" if sub % 2 else "dve", itok[:, sub, (u - 22) * 256:(u - 21) * 256], PSB[b][:, 0:256], [RPS[b]], [Ritok])
            P2 = Phase(ev1)
            am32 = [P2.tile([128, 4, 128]) for _ in range(2)]
            gs = [P2.tile([128, T], BF16) for _ in range(4)]
            Am = [P2.tile([128, 4, 128], BF16) for _ in range(2)]
            ktok = [[P2.tile([128, 4, 128], BF16) for _ in range(2)] for _ in range(2)]
            Sbv = [[P2.tile([128, 4, 128], BF16) for _ in range(2)] for _ in range(2)]
            SE, RSE = P2.tile([128, 4, 128]); tmpS, RtmpS = P2.tile([128, 4, 128])
            sqb, Rsqb = P2.tile([128, T], BF16); rs, Rrs = P2.tile([128, T]); r2, Rr2 = P2.tile([128, T])
            for i2 in range(2):
                S.op("pool", "memset", ap=am32[i2][0][:], constant=0.0, writes=[am32[i2][1]])
            for u in (24, 25):
                for j, b in proj_fm(l, u):
                    h = (u - 24) * 2 + j
                    act(gs[h][0][:], PSB[b][:], AF.Silu, [RPS[b]], [gs[h][1]])
            S3 = S_hg[l][:].rearrange("p (h n) -> p h n", h=4)
            bo = [ps_get(hold=True) for _ in range(4)]
            for c in range(4):
                cs = slice(c * 128, (c + 1) * 128)
                am, Ram = Am[c % 2]
                b = ps_get()
                for h in range(4):
                    mm(PSB[b][:, h * 128:(h + 1) * 128], kt_[h][0][:, cs], qt[h][0][:, cs], True, True,
                       [kt_[h][1], qt[h][1]], [RPS[b]], inc=(h == 3))
                a32, Ra32 = am32[c % 2]
                for h in range(4):
                    S.op("dve", "copy_predicated", out=a32[:, h, :], mask=hgmask[:], data=PSB[b][:, h * 128:(h + 1) * 128],
                         reads=[RPS[b], Rc], writes=[Ra32])
                cp("act", am[:].rearrange("p h n -> p (h n)"), a32[:].rearrange("p h n -> p (h n)"), [Ra32], [Ram])
                b = ps_get()
                for h in range(4):
                    mm(PSB[b][:, h * 128:(h + 1) * 128], kt_[h][0][:, cs], identb[:], True, True, [kt_[h][1], Rc], [RPS[b]], inc=(h == 3))
                for j in range(2):
                    ktk, Rktk = ktok[c % 2][j]
                    ts("dve", ktk[:].rearrange("p h n -> p (h n)"), PSB[b][:], colmask[:, j:j + 1], None, ALU.mult, None,
                       [RPS[b], Rc], [Rktk])
                for j in range(2):
                    ktk, Rktk = ktok[c % 2][j]
                    sbv, Rsbv = Sbv[c % 2][j]
                    bd_ = ps_get()
                    for h in range(4):
                        mm(PSB[bd_][:, h * 128:(h + 1) * 128], ktk[:, h, :], itok[:, c, h * 128:(h + 1) * 128], True, True,
                           [Rktk, Ritok], [RPS[bd_]], inc=(h == 3))
                    cj = c * 2 + j
                    for h in range(4):
                        hs_ = slice(h * 128, (h + 1) * 128)
                        ts("dve", sbv[:, h, :], S_hg[l][:, hs_], E1s[:, h, cj:cj + 1], None, ALU.mult, None, [RS_hg[l], RE1s], [Rsbv])
                        stt("dve", tmpS[:, h, :], S_hg[l][:, hs_], E1s[:, h, cj:cj + 1], PSB[bd_][:, hs_], ALU.mult, ALU.add,
                            [RS_hg[l], RE1s, RPS[bd_]], [RtmpS])
                        ts("dve", S_hg[l][:, hs_], tmpS[:, h, :], E2s[:, h, cj:cj + 1], None, ALU.mult, None, [RtmpS, RE2s], [RS_hg[l]])
                for h in range(4):
                    hs = slice(h * 128, (h + 1) * 128)
                    mm(PSB[bo[h]][:, cs], itok[:, c, hs], am[:, h, :], True, False, [Ritok, Ram], [RPS[bo[h]]], inc=False)
                    for j in range(2):
                        js = slice(c * 128 + j * 64, c * 128 + (j + 1) * 64)
                        sbv, Rsbv = Sbv[c % 2][j]
                        mm(PSB[bo[h]][:, js], sbv[:, h, :], qt[h][0][:, js], False, j == 1, [Rsbv, qt[h][1]], [RPS[bo[h]]],
                           inc=(j == 1))
            bss = ps_get(hold=True)
            for h in range(4):
                b = bo[h]
                act(sqb[:], PSB[b][:], AF.Square, [RPS[b]], [Rsqb])
                mm(PSB[bss][:], onesb[:], sqb[:], h == 0, h == 3, [Rc, Rsqb], [RPS[bss]], inc=True)
            rstd_from(rs[:], PSB[bss][:], 1.0 / 512, [RPS[bss]], [Rrs], r2[:], Rr2)
            ps_release(bss)
            for h in range(4):
                col = l * 4 + h
                b = bo[h]
                stt("dve", r2[:], PSB[b][:], par[:, 2 + col:3 + col], rs[:], ALU.mult, ALU.mult, [RPS[b], Rrs, Rc], [Rr2])
                tt("dve", mixT[:, 12 + h, :], r2[:], gs[h][0][:], ALU.mult, [Rr2, gs[h][1]], [Rmix[12 + h]])
                ps_release(b)
            ev2 = P2.close()
            P.prev = ev2
            return P.close()

        def mix_out(l, prev):
            P = Phase(prev)
            sq2 = [P.tile([128, T], BF16) for _ in range(2)]
            lnst = ln_begin()
            for u in range(8):
                slot, Rs = w_next(l, "wmo", u)
                s3 = slot[:].rearrange("p (kc n) -> p kc n", kc=KC)
                for j in range(2):
                    m = u * 2 + j
                    b = ps_get()
                    for kc in range(KC):
                        mm(PSB[b][:], s3[:, kc, j * 128:(j + 1) * 128], mixT[:, kc, :], kc == 0, kc == KC - 1, [Rs, Rmix[kc]], [RPS[b]])
                    resid_chunk(lnst, m, PSB[b][:], [RPS[b]], 1.0 / ALPHA, [s[0] for s in sq2], [s[1] for s in sq2])
            ln_finish(lnst, l, 1, P)
            return P.close()

        def ple(l, t, prev):
            P = Phase(prev)
            sq2 = [P.tile([128, T], BF16) for _ in range(2)]
            pst, Rpst = P.tile([128, 4, PD])
            pT, RpT = P.tile([128, 2, T], BF16)
            gt2 = [P.tile([128, T]) for _ in range(2)]
            S.dma("pool", pst[:], p_d[l, t * T:(t + 1) * T, :].rearrange("(s p) n -> p s n", p=128), writes=[Rpst])
            for kc2 in range(2):
                b = ps_get()
                for sub in range(4):
                    S.op("pe", "transpose", out=PSB[b][:, sub * 128:(sub + 1) * 128], in_=pst[:, sub, kc2 * 128:(kc2 + 1) * 128],
                         identity=ident[:], reads=[Rpst, Rc], writes=[RPS[b]], inc=(sub == 3))
                cp("dve", pT[:, kc2, :], PSB[b][:], [RPS[b]], [RpT])
            gates = []
            G16, RG16_ = P.tile([128, KC, T], BF16)
            RG16 = [Res() for _ in range(KC)]
            for r in RG16:
                r.r = dict(prev)
            P.res.extend(RG16)
            for u in range(8):
                slot, Rs = w_next(l, "wpg", u)
                s3 = slot[:].rearrange("p (kc n) -> p kc n", kc=KC)
                for j in range(2):
                    m = u * 2 + j
                    b = ps_get()
                    for kc in range(KC):
                        mm(PSB[b][:], s3[:, kc, j * 128:(j + 1) * 128], Xb[:, kc, :], kc == 0, kc == KC - 1, [Rs, RXb[kc]], [RPS[b]])
                    act(G16[:, m, :], PSB[b][:], AF.Sigmoid, [RPS[b]], [RG16[m]])
            slot, Rs = w_next(l, "wpp", 0)
            s3 = slot[:].rearrange("p (kc n) -> p kc n", kc=2)
            lnst = ln_begin()
            for m in range(KC):
                b = ps_get()
                for kc2 in range(2):
                    mm(PSB[b][:], s3[:, kc2, m * 128:(m + 1) * 128], pT[:, kc2, :], kc2 == 0, kc2 == 1, [Rs, RpT], [RPS[b]])
                g_, Rg_ = gt2[m % 2]
                tt("dve", g_[:], PSB[b][:], G16[:, m, :], ALU.mult, [RPS[b], RG16[m]], [Rg_])
                resid_chunk(lnst, m, g_[:], [Rg_], 1.0 / ALPHA, [s[0] for s in sq2], [s[1] for s in sq2])
            ln_finish(lnst, l, 3, P)
            return P.close()

        def load_x(t, prev):
            P = Phase(prev)
            stg2 = [P.tile([128, D]) for _ in range(2)]
            for sub in range(4):
                sg_, Rsg_ = stg2[sub % 2]
                S.dma("pool", sg_[:], x_d[t * T + sub * 128:t * T + (sub + 1) * 128, :], writes=[Rsg_])
                for g4 in range(4):
                    b = ps_get()
                    for j in range(4):
                        kc = g4 * 4 + j
                        S.op("pe", "transpose", out=PSB[b][:, j * 128:(j + 1) * 128], in_=sg_[:, kc * 128:(kc + 1) * 128],
                             identity=ident[:], reads=[Rsg_, Rc], writes=[RPS[b]], inc=(j == 3))
                    rx = RX[g4 * 4:g4 * 4 + 4]
                    rxb = RXb[g4 * 4:g4 * 4 + 4]
                    cp("dve", X[:, g4 * 4:g4 * 4 + 4, sub * 128:(sub + 1) * 128], PSB[b][:].rearrange("p (a n) -> p a n", a=4), [RPS[b]], rx)
                    cp("pool", Xb[:, g4 * 4:g4 * 4 + 4, sub * 128:(sub + 1) * 128], X[:, g4 * 4:g4 * 4 + 4, sub * 128:(sub + 1) * 128], rx, rxb)
            return P.close()

        def store_y(t, prev):
            P = Phase(prev)
            stg2 = [P.tile([128, D]) for _ in range(2)]
            for sub in range(4):
                sg_, Rsg_ = stg2[sub % 2]
                for g4 in range(4):
                    b = ps_get()
                    for j in range(4):
                        kc = g4 * 4 + j
                        S.op("pe", "transpose", out=PSB[b][:, j * 128:(j + 1) * 128], in_=X[:, kc, sub * 128:(sub + 1) * 128],
                             identity=ident[:], reads=[RX[kc], Rc], writes=[RPS[b]], inc=(j == 3))
                    cp("act" if g4 % 2 else "dve", sg_[:, g4 * 512:(g4 + 1) * 512], PSB[b][:], [RPS[b]], [Rsg_])
                S.dma("pool", y_d[t * T + sub * 128:t * T + (sub + 1) * 128, :], sg_[:], reads=[Rsg_], writes=[Ry])
            return P.close()

        ev = prev_ev
        for t in range(NT):
            if 'rope' not in SKIP:
                ev = rope_tables(t, ev)
            if 'loadx' not in SKIP:
                ev = load_x(t, ev)
            for l in range(nlayers):
                if stop >= 1:
                    ev = ffn(l, "w1i", "w1o", 0, ev)
                else:
                    for _ in range(FC + 32):
                        st["next"] += 1
                stages = [(2, mixer_A, 6), (3, mixer_B, 8), (4, mixer_C, 4), (5, mixer_D, 8)]
                for sid, fn, nun in stages:
                    if stop >= sid:
                        ev = fn(l, t, ev)
                    else:
                        st["next"] += nun
                if dumpmix:
                    for m in range(KC):
                        cp("dve", X[:, m, :], mixT[:, m, :], [Rmix[m]], [RX[m]])
                if stop >= 6:
                    ev = mix_out(l, ev)
                else:
                    st["next"] += 8
                if stop >= 7:
                    ev = ffn(l, "w2i", "w2o", 2, ev)
                else:
                    st["next"] += FC + 32
                if stop >= 8:
                    ev = ple(l, t, ev)
                else:
                    st["next"] += 9
            for _ in range((L - nlayers) * NU):
                st["next"] += 1
            if 'storey' not in SKIP:
                ev = store_y(t, ev)
        S.finish("pool", [Ry])
        S.finish("sp", Rring)
        S.emit(block)
    return nc


def make_in_map(inputs, b, S_len):
    f32 = np.float32
    m = {}
    m["x"] = np.ascontiguousarray(inputs["x"][b, :S_len])
    m["p"] = np.ascontiguousarray(inputs["p"][:, b, :S_len])
    m["pos"] = np.ascontiguousarray(inputs["positions"][b:b + 1, :S_len]).astype(np.int32)
    m["w1i"] = inputs["ffn1_w_in"]; m["w1o"] = inputs["ffn1_w_out"]
    m["wmi"] = inputs["w_mix_in"]; m["wmo"] = inputs["w_mix_out"]
    m["w2i"] = inputs["ffn2_w_in"]; m["w2o"] = inputs["ffn2_w_out"]
    m["wpg"] = inputs["ple_w_gate"]; m["wpp"] = inputs["ple_w_proj"]
    m["relb"] = np.ascontiguousarray(inputs["rel_bias"]).reshape(1, 128)
    m["dlam"] = np.ascontiguousarray(inputs["diff_lambda"]).reshape(1, -1)
    par = np.zeros((32, 128), f32)
    par[0:2] = inputs["diff_norm_g"]
    par[2:10] = np.ascontiguousarray(inputs["hgrn_norm_g"]).reshape(8, 128)
    par[10:18] = np.ascontiguousarray(inputs["hgrn_lb_logits"]).reshape(8, 128)
    m["par"] = par
    m["gg"] = np.ascontiguousarray(inputs["gmlp_ln_g"]).reshape(1, -1)
    m["gb"] = np.ascontiguousarray(inputs["gmlp_ln_b"]).reshape(1, -1)
    m["gws"] = np.ascontiguousarray(inputs["gmlp_w_s"])
    m["gbs"] = np.ascontiguousarray(inputs["gmlp_b_s"]).reshape(1, -1)
    m["lng"] = np.ascontiguousarray(inputs["ln_g"]).reshape(128, 128)
    m["lnb"] = np.ascontiguousarray(inputs["ln_b"]).reshape(128, 128)
    m.update(host_consts())
    return {k: np.ascontiguousarray(v) for k, v in m.items()}


def kernel(**inputs):
    inputs = {k: np.asarray(v) for k, v in inputs.items()}
    B, S_len = inputs["x"].shape[:2]
    F = inputs["ffn1_w_out"].shape[1]
    nc = build(S_len, F)
    in_maps = [make_in_map(inputs, b, S_len) for b in range(B)]
    res = run_bass_kernel_spmd(nc, in_maps, core_ids=list(range(B)))
    out = np.stack([np.asarray(res.results[b]["y"]) for b in range(B)], axis=0)
    return out.astype(np.float32)
```

```python
import math
from contextlib import ExitStack
import numpy as np
import ml_dtypes
import concourse.bass as bass
import concourse.mybir as mybir
from concourse.bass_utils import run_bass_kernel_spmd

F32 = mybir.dt.float32
BF16 = mybir.dt.bfloat16
I32 = mybir.dt.int32
U32 = mybir.dt.uint32
AF = mybir.ActivationFunctionType
ALU = mybir.AluOpType

D = 2048
KC = 16
T = 512
PD = 256
DEPTH = 2
ALPHA = (2 * DEPTH) ** 0.25
EPS = 1e-5
EPS_LN = EPS / (ALPHA * ALPHA)
SLOT = 4096
NSLOT = 4
MIXCOLS = 6656


class Res:
    __slots__ = ("name", "w", "r")

    def __init__(self, name="r"):
        self.name = name
        self.w = None
        self.r = {}


class Sched:
    ENG = ("pe", "act", "dve", "pool", "sp")

    def __init__(self, nc, es, n_dma_sems=20):
        self.nc = nc
        self.sems = {}
        self.cnt = {}
        for e in self.ENG:
            self.sems[e] = es.enter_context(nc.semaphore("s_" + e))
            self.cnt[e] = 0
        self.dma_sems = []
        for i in range(n_dma_sems):
            k = "d%d" % i
            self.sems[k] = es.enter_context(nc.semaphore("s_" + k))
            self.cnt[k] = 0
            self.dma_sems.append(k)
        self.dma_rr = 0
        self.known = {e: {} for e in self.ENG}
        self.prog = {e: [] for e in self.ENG}
        self.n_wait = 0
        self.n_inst = 0

    def _wait(self, e, key, val):
        if val <= 0:
            return
        if e == "pe" and key == "pe":
            return
        kn = self.known[e]
        if kn.get(key, 0) >= val:
            return
        self.prog[e].append(("wait", key, val))
        kn[key] = val
        self.n_wait += 1

    def _deps(self, e, reads, writes):
        need = {}
        for R in reads:
            if R.w is not None:
                k, v = R.w
                if need.get(k, 0) < v:
                    need[k] = v
        for R in writes:
            if R.w is not None:
                k, v = R.w
                if need.get(k, 0) < v:
                    need[k] = v
            for k, v in R.r.items():
                if need.get(k, 0) < v:
                    need[k] = v
        for k, v in need.items():
            self._wait(e, k, v)

    def _mark(self, ev, reads, writes):
        k, v = ev
        for R in writes:
            R.w = ev
            R.r = {}
        for R in reads:
            if R.r.get(k, 0) < v:
                R.r[k] = v

    def op(self, e, name, reads=(), writes=(), inc=True, **kw):
        self._deps(e, reads, writes)
        if inc:
            self.cnt[e] += 1
            self.prog[e].append(("op", name, kw, e, 1))
            self._mark((e, self.cnt[e]), reads, writes)
        else:
            self.prog[e].append(("op", name, kw, None, 0))
            self._mark((e, self.cnt[e] + 1), reads, writes)
        self.n_inst += 1

    def dma(self, q, out, in_, reads=(), writes=(), **kw):
        k = self.dma_sems[self.dma_rr]
        self.dma_rr = (self.dma_rr + 1) % len(self.dma_sems)
        self._wait(q, k, self.cnt[k])
        self._deps(q, reads, writes)
        self.cnt[k] += 16
        kw = dict(kw)
        kw["out"] = out
        kw["in_"] = in_
        self.prog[q].append(("op", "dma_start", kw, k, 16))
        self._mark((k, self.cnt[k]), reads, writes)
        self.n_inst += 1

    def events(self, resources):
        ev = {}
        for R in resources:
            if R.w is not None:
                k, v = R.w
                if ev.get(k, 0) < v:
                    ev[k] = v
            for k, v in R.r.items():
                if ev.get(k, 0) < v:
                    ev[k] = v
        return ev

    def finish(self, e, resources):
        for k, v in self.events(resources).items():
            self._wait(e, k, v)

    def emit(self, block):
        sems = self.sems

        def run(eng, prog):
            for it in prog:
                if it[0] == "wait":
                    eng.wait_ge(sems[it[1]], it[2])
                else:
                    _, name, kw, sk, inc = it
                    inst = getattr(eng, name)(**kw)
                    if sk is not None:
                        inst.then_inc(sems[sk], inc)

        block.tensor(lambda e: run(e, self.prog["pe"]))
        block.scalar(lambda e: run(e, self.prog["act"]))
        block.vector(lambda e: run(e, self.prog["dve"]))
        block.gpsimd(lambda e: run(e, self.prog["pool"]))
        block.sync(lambda e: run(e, self.prog["sp"]))


def t5_bucket_np(n):
    n = np.maximum(n, 0)
    nf = np.maximum(n, 1).astype(np.float32)
    large = 16 + (np.log(nf / np.float32(16)) / np.float32(math.log(128 / 16)) * np.float32(16)).astype(np.int32)
    large = np.minimum(large, 31)
    return np.where(n < 16, n, large)


def host_consts():
    c = {}
    eye = np.eye(128, dtype=np.float32)
    c["c_ident"] = eye
    k = np.arange(128)[:, None]
    q = np.arange(128)[None, :]
    oh = np.zeros((128, 32, 256), np.float32)
    bd = t5_bucket_np(q - k)
    bs = t5_bucket_np(q - k + 128)
    for b in range(32):
        oh[:, b, 0:128] = (bd == b)
        oh[:, b, 128:256] = (bs == b)
    c["c_oh"] = oh
    c["c_negmask"] = np.where(k > q, np.float32(-1e30), np.float32(0)).astype(np.float32)
    c["c_causT"] = (q >= k).astype(np.float32)
    blk = ((k // 64) == (q // 64)) & (q >= k)
    c["c_hgmask"] = blk.astype(np.float32)
    c["c_tril"] = (k >= q).astype(np.float32)
    inv = (10000.0 ** (-np.linspace(0.0, 1.0, 64))).astype(np.float32)
    c["c_invd"] = np.repeat(inv, 2)[:, None].astype(np.float32)
    prot = np.zeros((128, 128), np.float32)
    for i in range(64):
        prot[2 * i + 1, 2 * i] = -1.0
        prot[2 * i, 2 * i + 1] = 1.0
    c["c_prot"] = prot
    g = 1.0 - 2.0 ** (-5.0 - np.arange(4, dtype=np.float64))
    j = np.arange(128, dtype=np.float64)
    gq = np.stack([g[h] ** (j + 1.0) for h in range(4)])
    gk = np.stack([g[h] ** (-(j + 1.0)) * (128.0 ** -0.5) for h in range(4)])
    gs = np.stack([np.full(128, g[h] ** 128.0) for h in range(4)])
    c["c_ret"] = np.concatenate([gq.reshape(1, 512), gk.reshape(1, 512), gs.reshape(1, 512)], 1).astype(np.float32)
    ud = np.zeros((128, 128), np.float32)
    ux = np.zeros((128, 4), np.float32)
    for s in range(128):
        cs, ls = s // 64, s % 64
        for t in range(128):
            ct, lt = t // 64, t % 64
            if cs != ct:
                continue
            if 31 < ls <= lt:
                ud[s, t] = 1.0
            elif lt < ls <= 31:
                ud[s, t] = -1.0
        if ls <= 31:
            ux[s, 2 * cs] = 1.0
        else:
            ux[s, 2 * cs + 1] = 1.0
    c["c_ud"] = ud
    c["c_ux"] = ux
    cm = np.zeros((128, 4), np.float32)
    cm[0:64, 0] = 1.0
    cm[64:128, 1] = 1.0
    cm[0, 2] = 1.0
    cm[1, 3] = 1.0
    c["c_colmask"] = cm
    return c


WMI_ORDER = list(range(26))


def unit_table(F):
    FC = F // 128
    FH = FC // 2
    units = []
    def ffn_units(tagi, tago):
        u = []
        for hf in range(2):
            for jj in range(FH):
                u.append((tagi, hf * FH + jj))
            for m in range(16):
                u.append((tago, hf * 16 + m))
        return u
    units += ffn_units("w1i", "w1o")
    for u in WMI_ORDER:
        units.append(("wmi", u))
    for u in range(8):
        units.append(("wmo", u))
    units += ffn_units("w2i", "w2o")
    for u in range(8):
        units.append(("wpg", u))
    units.append(("wpp", 0))
    return units


def build(S_len, F, stop=99, nlayers=DEPTH, dumpmix=False):
    NT = S_len // T
    FC = F // 128
    FH = FC // 2
    L = DEPTH
    nc = bass.Bass("TRN2", target_bir_lowering=False)

    def din(name, shape, dt=F32):
        return nc.dram_tensor(name, list(shape), dt, kind="ExternalInput").ap()

    x_d = din("x", [S_len, D])
    p_d = din("p", [L, S_len, PD])
    pos_d = din("pos", [1, S_len], I32)
    W = {
        "w1i": din("w1i", [L, D, 2 * F]), "w1o": din("w1o", [L, F, D]),
        "wmi": din("wmi", [L, D, MIXCOLS]), "wmo": din("wmo", [L, D, D]),
        "w2i": din("w2i", [L, D, 2 * F]), "w2o": din("w2o", [L, F, D]),
        "wpg": din("wpg", [L, D, D]), "wpp": din("wpp", [L, PD, D]),
    }
    relb_d = din("relb", [1, 128])
    dlam_d = din("dlam", [1, L * 256])
    par_d = din("par", [32, 128])
    gg_d = din("gg", [1, L * 512])
    gb_d = din("gb", [1, L * 512])
    gws_d = din("gws", [L, 4, 128, 128])
    gbs_d = din("gbs", [1, L * 512])
    lng_d = din("lng", [128, 128])
    lnb_d = din("lnb", [128, 128])
    cst = host_consts()
    C = {k: din(k, v.shape) for k, v in cst.items()}
    y_d = nc.dram_tensor("y", [S_len, D], F32, kind="ExternalOutput").ap()

    units = unit_table(F)
    debug_mode = (stop < 99 or nlayers < DEPTH)
    import os
    SKIP = os.environ.get('KSKIP', '').split(',')
    NU = len(units)
    uidx = {u: i for i, u in enumerate(units)}
    wsc = [nc.dram_tensor("wsc%d" % l_, [NU, 128, SLOT], BF16).ap() for l_ in range(L)]
    kcache = nc.dram_tensor("kcache", [L, NT, 4, 128, 1024], BF16).ap()
    vcache = nc.dram_tensor("vcache", [L, NT, 4, 128, 512], BF16).ap()

    with ExitStack() as es:
        S = Sched(nc, es)
        block = es.enter_context(nc.Block())

        def sb(name, shape, dt=F32, stack=es):
            return stack.enter_context(nc.sbuf_tensor("sb_" + name, list(shape), dt))

        X = sb("X", [128, KC, T])
        Xb = sb("Xb", [128, KC, T], BF16)
        mixT = sb("mixT", [128, KC, T], BF16)
        RX = [Res("X%d" % i) for i in range(KC)]
        RXb = [Res("Xb%d" % i) for i in range(KC)]
        Rmix = [Res("mix%d" % i) for i in range(KC)]
        ring = [sb("ring%d" % i, [128, SLOT], BF16) for i in range(NSLOT)]
        Rring = [Res("ring%d" % i) for i in range(NSLOT)]
        ident = sb("ident", [128, 128]); identb = sb("identb", [128, 128], BF16)
        ones = sb("ones", [128, 128]); onesb = sb("onesb", [128, 128], BF16)
        prot = sb("prot", [128, 128], BF16)
        tabA = sb("tabA", [128, 4, 256])
        chA = sb("chA", [128, 4])
        causT = sb("causT", [128, 128]); hgmask = sb("hgmask", [128, 128], U32)
        ud = sb("ud", [128, 128]); ux = sb("ux", [128, 4]); colmask = sb("colmask", [128, 4])
        invd = sb("invd", [128, 1])
        retc = sb("retc", [128, 1536])
        lnG = sb("lnG", [128, 128]); lnB = sb("lnB", [128, 128])
        par = sb("par", [128, 32])
        lbp = sb("lbp", [128, 8]); oml = sb("oml", [128, 8]); noml = sb("noml", [128, 8])
        gA = sb("gA", [128, 2]); nlam = sb("nlam", [128, 2])
        ggT = sb("ggT", [128, L * 512]); gbT = sb("gbT", [128, L * 512])
        wsT = sb("wsT", [128, L * 4, 128], BF16)
        bs2 = sb("bs2", [2, L * 512], BF16)
        S_ret = [sb("S_ret%d" % l, [128, 512]) for l in range(L)]
        Sb_ret = [sb("Sb_ret%d" % l, [128, 512], BF16) for l in range(L)]
        S_hg = [sb("S_hg%d" % l, [128, 512]) for l in range(L)]
        cosT = sb("cosT", [128, T]); sinT = sb("sinT", [128, T])
        Rc = Res("consts")
        RS_ret = [Res() for _ in range(L)]; RSb_ret = [Res() for _ in range(L)]; RS_hg = [Res() for _ in range(L)]
        Rcs = Res("cossin")
        Rkc = [[Res() for _ in range(NT)] for _ in range(L)]
        Rvc = [[Res() for _ in range(NT)] for _ in range(L)]
        Ry = Res("y")

        PSB = [es.enter_context(nc.psum_tensor("ps%d" % i, [128, 512], F32)) for i in range(8)]
        RPS = [Res("ps%d" % i) for i in range(8)]
        ps_state = {"rr": 0, "held": set()}

        def ps_get(hold=False):
            for _ in range(16):
                i = ps_state["rr"]
                ps_state["rr"] = (i + 1) % 8
                if i not in ps_state["held"]:
                    if hold:
                        ps_state["held"].add(i)
                    return i
            raise RuntimeError("no psum bank")

        def ps_release(i):
            ps_state["held"].discard(i)

        def mm(out, lhsT, rhs, start, stop, reads, writes, inc=None):
            S.op("pe", "matmul", out=out, lhsT=lhsT, rhs=rhs, start=start, stop=stop,
                 reads=reads, writes=writes, inc=(stop if inc is None else inc))

        def act(out, in_, func, reads, writes, **kw):
            S.op("act", "activation", out=out, in_=in_, func=func, reads=reads, writes=writes, **kw)

        def tt(e, out, in0, in1, op, reads, writes):
            S.op(e, "tensor_tensor", out=out, in0=in0, in1=in1, op=op, reads=reads, writes=writes)

        def ts(e, out, in0, s1, s2, op0, op1, reads, writes):
            if s2 is None:
                S.op(e, "tensor_scalar", out=out, in0=in0, scalar1=s1, scalar2=None, op0=op0, reads=reads, writes=writes)
            else:
                S.op(e, "tensor_scalar", out=out, in0=in0, scalar1=s1, scalar2=s2, op0=op0, op1=op1, reads=reads, writes=writes)

        def stt(e, out, in0, scalar, in1, op0, op1, reads, writes):
            S.op(e, "scalar_tensor_tensor", out=out, in0=in0, scalar=scalar, in1=in1, op0=op0, op1=op1,
                 reads=reads, writes=writes)

        def cp(e, out, in_, reads, writes):
            if e == "act":
                act(out, in_, AF.Copy, reads, writes)
            else:
                S.op(e, "tensor_copy", out=out, in_=in_, reads=reads, writes=writes)

        def bc(ap, shape):
            return ap.unsqueeze(1).broadcast_to(list(shape))

        def rstd_from(out, in_, scale, reads, writes, tmp, Rtmp):
            act(tmp, in_, AF.Ln, reads, [Rtmp], scale=scale, bias=eps_ap[:, 0:1])
            act(out, tmp, AF.Exp, [Rtmp], writes, scale=-0.5)

        class Phase:
            def __init__(self, prev_ev):
                self.es = ExitStack()
                self.res = []
                self.prev = prev_ev
                self.n = 0

            def tile(self, shape, dt=F32):
                phase_ctr[0] += 1
                t = sb("ph%d" % phase_ctr[0], shape, dt, stack=self.es)
                r = Res()
                r.r = dict(self.prev)
                self.res.append(r)
                return t, r

            def close(self):
                ev = S.events(self.res)
                for k, v in self.prev.items():
                    if ev.get(k, 0) < v:
                        ev[k] = v
                self.es.close()
                return ev

        phase_ctr = [0]
        eps_t = sb("eps_t", [128, 2])
        eps_ap = eps_t

        Rwsc = [[Res() for _ in range(NU)] for _ in range(L)]

        def wsrc(l, tag, idx):
            dst = wsc[l][uidx[(tag, idx)]]
            if tag in ("w1i", "w2i"):
                w = W[tag][l]
                d3 = dst.rearrange("p (kc n) -> p kc n", kc=KC)
                return [(d3[:, :, 0:128], w[:, idx * 128:(idx + 1) * 128].rearrange("(kc p) n -> p kc n", p=128)),
                        (d3[:, :, 128:256], w[:, F + idx * 128:F + (idx + 1) * 128].rearrange("(kc p) n -> p kc n", p=128))]
            if tag in ("w1o", "w2o"):
                hf, m = idx // 16, idx % 16
                w = W[tag][l]
                d3 = dst[:, 0:FH * 128].rearrange("p (fc n) -> p fc n", fc=FH)
                return [(d3, w[hf * FH * 128:(hf + 1) * FH * 128, m * 128:(m + 1) * 128].rearrange("(fc p) n -> p fc n", p=128))]
            if tag in ("wmi", "wmo", "wpg"):
                w = W[tag][l]
                d3 = dst.rearrange("p (kc n) -> p kc n", kc=KC)
                return [(d3, w[:, idx * 256:(idx + 1) * 256].rearrange("(kc p) n -> p kc n", p=128))]
            if tag == "wpp":
                w = W[tag][l]
                d3 = dst.rearrange("p (kc n) -> p kc n", kc=2)
                return [(d3, w.rearrange("(kc p) n -> p kc n", p=128))]
            raise KeyError(tag)

        pro_list = []
        for l in range(L):
            for (tag, idx) in units:
                pro_list.append((l, uidx[(tag, idx)], wsrc(l, tag, idx)))
        pro = {"ptr": 0}

        def pump_until(l, u, ahead=6):
            return

        def pump_all():
            target = len(pro_list) if 'prologue' not in SKIP else 0
            while pro["ptr"] < target:
                lj, uj, lst = pro_list[pro["ptr"]]
                for dv, sv in lst:
                    S.dma("pool", dv, sv, writes=[Rwsc[lj][uj]])
                pro["ptr"] += 1

        pump_all()

        stream_seq = []
        for t_ in range(NT):
            for l in range(L):
                for u in range(NU):
                    stream_seq.append((l, u))
        st = {"issued": 0, "next": 0}

        def w_next(l, tag, idx):
            u = uidx[(tag, idx)]
            i = st["next"]
            assert stream_seq[i] == (l, u), (stream_seq[i], (l, u, tag, idx))
            if debug_mode:
                k = st["issued"]
                pump_until(l, u)
                S.dma("sp", ring[k % NSLOT][:], wsc[l][u], reads=[Rwsc[l][u]], writes=[Rring[k % NSLOT]])
                st["issued"] += 1
                st["next"] += 1
                return ring[k % NSLOT], Rring[k % NSLOT]
            while st["issued"] < min(len(stream_seq), i + NSLOT):
                j = st["issued"]
                lj, uj = stream_seq[j]
                pump_until(lj, uj)
                S.dma("sp", ring[j % NSLOT][:], wsc[lj][uj], reads=[Rwsc[lj][uj]], writes=[Rring[j % NSLOT]])
                st["issued"] += 1
            st["next"] += 1
            return ring[i % NSLOT], Rring[i % NSLOT]

        ph = Phase({})
        stg, Rstg = ph.tile([128, 128])
        for name, dst in (("c_ident", ident), ("c_causT", causT), ("c_ud", ud)):
            S.dma("sp", dst[:], C[name], writes=[Rc])
        S.dma("sp", ux[:], C["c_ux"], writes=[Rc])
        S.dma("sp", colmask[:], C["c_colmask"], writes=[Rc])
        S.dma("sp", invd[:], C["c_invd"], writes=[Rc])
        S.dma("sp", retc[:], C["c_ret"].partition_broadcast(128), writes=[Rc])
        S.dma("sp", ggT[:], gg_d.partition_broadcast(128), writes=[Rc])
        S.dma("sp", gbT[:], gb_d.partition_broadcast(128), writes=[Rc])
        S.op("pool", "memset", ap=ones[:], constant=1.0, writes=[Rc])
        S.op("pool", "memset", ap=onesb[:], constant=1.0, writes=[Rc])
        S.op("pool", "memset", ap=eps_t[:, 0:1], constant=EPS, writes=[Rc])
        S.op("pool", "memset", ap=eps_t[:, 1:2], constant=EPS_LN, writes=[Rc])
        for l in range(L):
            S.op("pool", "memset", ap=S_ret[l][:], constant=0.0, writes=[RS_ret[l]])
            S.op("pool", "memset", ap=Sb_ret[l][:], constant=0.0, writes=[RSb_ret[l]])
            S.op("pool", "memset", ap=S_hg[l][:], constant=0.0, writes=[RS_hg[l]])
        cp("dve", identb[:], ident[:], [Rc], [Rc])
        S.dma("sp", stg[:], C["c_prot"], writes=[Rstg])
        cp("dve", prot[:], stg[:], [Rstg], [Rc])
        S.dma("sp", stg[:], C["c_hgmask"], writes=[Rstg])
        cp("dve", hgmask[:], stg[:], [Rstg], [Rc])
        if 'lnp' not in SKIP:
            for src, dst in ((lng_d, lnG), (lnb_d, lnB)):
                S.dma("sp", stg[:], src, writes=[Rstg])
                b = ps_get()
                S.op("pe", "transpose", out=PSB[b][:, 0:128], in_=stg[:], identity=ident[:], reads=[Rstg, Rc], writes=[RPS[b]])
                cp("dve", dst[:], PSB[b][:, 0:128], [RPS[b]], [Rc])
        if 'par' not in SKIP:
            S.op("pool", "memset", ap=stg[:], constant=0.0, writes=[Rstg])
            S.dma("sp", stg[0:32, :], par_d, writes=[Rstg])
            b = ps_get()
            S.op("pe", "transpose", out=PSB[b][:, 0:128], in_=stg[:], identity=ident[:], reads=[Rstg, Rc], writes=[RPS[b]])
            cp("dve", par[:], PSB[b][:, 0:32], [RPS[b]], [Rc])
            S.op("pool", "memset", ap=lbp[:], constant=0.0, writes=[Rc])
            tmp8, Rtmp8 = ph.tile([128, 8])
            tt("dve", tmp8[:, 0:4], par[:, 14:18], par[:, 10:14], ALU.subtract, [Rc], [Rtmp8])
            act(lbp[:, 4:8], tmp8[:, 0:4], AF.Sigmoid, [Rtmp8], [Rc])
            ts("dve", lbp[:], lbp[:], 1e-30, None, ALU.max, None, [Rc], [Rc])
            ts("dve", oml[:], lbp[:], -1.0, 1.0, ALU.mult, ALU.add, [Rc], [Rc])
            ts("dve", noml[:], oml[:], -1.0, None, ALU.mult, None, [Rc], [Rc])
        if 'lam' not in SKIP:
            dl, Rdl = ph.tile([128, L * 256])
            S.dma("sp", dl[:], dlam_d.partition_broadcast(128), writes=[Rdl])
            pr, Rpr = ph.tile([128, 64])
            sm, Rsm = ph.tile([128, 4])
            for l in range(L):
                lam_init = 0.8 - 0.6 * math.exp(-0.3 * l)
                for i in range(2):
                    a0 = l * 256 + i * 128
                    tt("dve", pr[:], dl[:, a0:a0 + 64], dl[:, a0 + 64:a0 + 128], ALU.mult, [Rdl], [Rpr])
                    S.op("dve", "reduce_sum", out=sm[:, i:i + 1], in_=pr[:], axis=mybir.AxisListType.X, reads=[Rpr], writes=[Rsm])
                act(sm[:, 2:4], sm[:, 0:2], AF.Exp, [Rsm], [Rsm])
                tt("dve", nlam[:, l:l + 1], sm[:, 3:4], sm[:, 2:3], ALU.subtract, [Rsm], [Rc])
                ts("dve", nlam[:, l:l + 1], nlam[:, l:l + 1], -lam_init, None, ALU.add, None, [Rc], [Rc])
                ts("dve", gA[:, l:l + 1], par[:, l:l + 1], 1.0 - lam_init, None, ALU.mult, None, [Rc], [Rc])
        if 'tab' not in SKIP:
            rbB, RrbB = ph.tile([128, 128])
            S.dma("sp", rbB[:], relb_d.partition_broadcast(128), writes=[RrbB])
            cp("dve", chA[:], rbB[:, 124:128], [RrbB], [Rc])
            S.op("pool", "memset", ap=tabA[:], constant=0.0, writes=[Rc])
            ohb, Rohb = ph.tile([128, 8, 256])
            for g8 in range(4):
                S.dma("sp", ohb[:], C["c_oh"][:, g8 * 8:(g8 + 1) * 8, :], writes=[Rohb])
                for bb in range(8):
                    bk = g8 * 8 + bb
                    for h in range(4):
                        stt("dve", tabA[:, h, :], ohb[:, bb, :], rbB[:, bk * 4 + h:bk * 4 + h + 1], tabA[:, h, :],
                            ALU.mult, ALU.add, [Rohb, RrbB, Rc], [Rc])
            S.dma("sp", stg[:], C["c_negmask"], writes=[Rstg])
            for h in range(4):
                tt("dve", tabA[:, h, 0:128], tabA[:, h, 0:128], stg[:], ALU.add, [Rc, Rstg], [Rc])
        if 'gws' not in SKIP:
            tril, Rtril = ph.tile([128, 128])
            S.dma("sp", tril[:], C["c_tril"], writes=[Rtril])
            wst, Rwst = ph.tile([128, 128])
            for l in range(L):
                for g in range(4):
                    S.dma("sp", wst[:], gws_d[l, g], writes=[Rwst])
                    tt("dve", wst[:], wst[:], tril[:], ALU.mult, [Rwst, Rtril], [Rwst])
                    b = ps_get()
                    S.op("pe", "transpose", out=PSB[b][:, 0:128], in_=wst[:], identity=ident[:], reads=[Rwst, Rc], writes=[RPS[b]])
                    cp("dve", wsT[:, l * 4 + g, :], PSB[b][:, 0:128], [RPS[b]], [Rc])
        if 'bs2' not in SKIP:
            b2f, Rb2f = ph.tile([2, L * 512])
            b2h, Rb2h = ph.tile([2, L * 512], BF16)
            b2g, Rb2g = ph.tile([2, L * 512])
            S.dma("sp", b2f[:], gbs_d.partition_broadcast(2), writes=[Rb2f])
            cp("dve", b2h[:], b2f[:], [Rb2f], [Rb2h])
            cp("dve", b2g[:], b2h[:], [Rb2h], [Rb2g])
            tt("dve", b2f[:], b2f[:], b2g[:], ALU.subtract, [Rb2f, Rb2g], [Rb2f])
            ts("dve", b2g[:], b2g[:], colmask[0:2, 2:3], None, ALU.mult, None, [Rb2g, Rc], [Rb2g])
            stt("dve", bs2[:], b2f[:], colmask[0:2, 3:4], b2g[:], ALU.mult, ALU.add, [Rb2f, Rb2g, Rc], [Rc])
        prev_ev = ph.close()

        def ln_begin():
            b1 = ps_get(hold=True)
            b2 = ps_get(hold=True)
            return {"b1": b1, "b2": b2, "n": 0, "pending": None}

        def ln_flush(lnst):
            pend = lnst["pending"]
            if pend is not None:
                i, m, sq, Rsq = pend
                mm(PSB[lnst["b1"]][:], onesb[:], Xb[:, m, :], i == 0, i == KC - 1, [Rc, RXb[m]], [RPS[lnst["b1"]]])
                mm(PSB[lnst["b2"]][:], onesb[:], sq[:], i == 0, i == KC - 1, [Rc, Rsq], [RPS[lnst["b2"]]])
                lnst["pending"] = None

        def resid_chunk(lnst, m, Yap, Yreads, coef, sq2, Rsq2, extra_in1=None):
            if lnst is not None:
                ln_flush(lnst)
            stt("dve", X[:, m, :], Yap, coef, X[:, m, :], ALU.mult, ALU.add, Yreads + [RX[m]], [RX[m]])
            if lnst is not None:
                i = lnst["n"]
                sq, Rsq = sq2[i % 2], Rsq2[i % 2]
                act(sq[:], X[:, m, :], AF.Square, [RX[m]], [Rsq])
                cp("pool", Xb[:, m, :], X[:, m, :], [RX[m]], [RXb[m]])
                lnst["pending"] = (i, m, sq, Rsq)
                lnst["n"] += 1

        def ln_finish(lnst, l, i, P):
            ln_flush(lnst)
            b1, b2 = lnst["b1"], lnst["b2"]
            mean, Rmean = P.tile([128, T])
            rstd, Rrstd = P.tile([128, T])
            t1, Rt1 = P.tile([128, T])
            ts("dve", mean[:], PSB[b1][:], 1.0 / D, None, ALU.mult, None, [RPS[b1]], [Rmean])
            tt("pool", t1[:], mean[:], mean[:], ALU.mult, [Rmean], [Rt1])
            stt("dve", t1[:], PSB[b2][:], 1.0 / D, t1[:], ALU.mult, ALU.subtract, [RPS[b2], Rt1], [Rt1])
            act(t1[:], t1[:], AF.Ln, [Rt1], [Rt1], bias=eps_ap[:, 1:2])
            act(rstd[:], t1[:], AF.Exp, [Rt1], [Rrstd], scale=-0.5)
            ps_release(b1)
            ps_release(b2)
            for m in range(KC):
                col = (l * 4 + i) * KC + m
                e = "dve" if m % 2 == 0 else "pool"
                tt(e, X[:, m, :], X[:, m, :], mean[:], ALU.subtract, [RX[m], Rmean], [RX[m]])
                tt(e, X[:, m, :], X[:, m, :], rstd[:], ALU.mult, [RX[m], Rrstd], [RX[m]])
                act(X[:, m, :], X[:, m, :], AF.Identity, [RX[m], Rc], [RX[m]], scale=lnG[:, col:col + 1], bias=lnB[:, col:col + 1])
                cp("pool" if m % 2 == 0 else "dve", Xb[:, m, :], X[:, m, :], [RX[m]], [RXb[m]])

        def ffn(l, tagi, tago, lni, prev):
            P = Phase(prev)
            G, RG_ = P.tile([128, FH, T], BF16)
            RG = [Res() for _ in range(FH)]
            for r in RG:
                r.r = dict(prev)
            P.res.extend(RG)
            sg2 = [P.tile([128, T]) for _ in range(2)]
            sq2 = [P.tile([128, T], BF16) for _ in range(2)]
            lnst = None
            for hf in range(2):
                for jj in range(FH):
                    j = hf * FH + jj
                    slot, Rs = w_next(l, tagi, j)
                    s3 = slot[:].rearrange("p (kc n) -> p kc n", kc=KC)
                    bg = ps_get(); bu = ps_get()
                    for kc in range(KC):
                        mm(PSB[bg][:], s3[:, kc, 0:128], Xb[:, kc, :], kc == 0, kc == KC - 1, [Rs, RXb[kc]], [RPS[bg]])
                    for kc in range(KC):
                        mm(PSB[bu][:], s3[:, kc, 128:256], Xb[:, kc, :], kc == 0, kc == KC - 1, [Rs, RXb[kc]], [RPS[bu]])
                    sg, Rsg = sg2[jj % 2]
                    act(sg[:], PSB[bg][:], AF.Silu, [RPS[bg]], [Rsg])
                    tt("dve", G[:, jj, :], sg[:], PSB[bu][:], ALU.mult, [Rsg, RPS[bu]], [RG[jj]])
                if hf == 1:
                    lnst = ln_begin()
                for m in range(KC):
                    slot, Rs = w_next(l, tago, hf * 16 + m)
                    s3 = slot[:, 0:FH * 128].rearrange("p (fc n) -> p fc n", fc=FH)
                    by = ps_get()
                    for fc in range(FH):
                        mm(PSB[by][:], s3[:, fc, :], G[:, fc, :], fc == 0, fc == FH - 1, [Rs, RG[fc]], [RPS[by]])
                    resid_chunk(lnst, m, PSB[by][:], [RPS[by]], 0.5 / ALPHA,
                                [s[0] for s in sq2], [s[1] for s in sq2])
            ln_finish(lnst, l, lni, P)
            return P.close()

        def proj_fm(l, u):
            slot, Rs = w_next(l, "wmi", u)
            s3 = slot[:].rearrange("p (kc n) -> p kc n", kc=KC)
            for j in range(2):
                b = ps_get()
                for kc in range(KC):
                    mm(PSB[b][:], s3[:, kc, j * 128:(j + 1) * 128], Xb[:, kc, :], kc == 0, kc == KC - 1, [Rs, RXb[kc]], [RPS[b]])
                yield j, b

        def proj_tm(l, u):
            slot, Rs = w_next(l, "wmi", u)
            s3 = slot[:].rearrange("p (kc n) -> p kc n", kc=KC)
            for sub in range(4):
                b = ps_get()
                for kc in range(KC):
                    mm(PSB[b][:, 0:256], Xb[:, kc, sub * 128:(sub + 1) * 128], s3[:, kc, :], kc == 0, kc == KC - 1,
                       [Rs, RXb[kc]], [RPS[b]])
                yield sub, b

        SCALE_A = 64 ** -0.5

        def mixer_A(l, t, prev):
            P = Phase(prev)
            qT = [P.tile([128, T], BF16) for _ in range(4)]
            kpad = [P.tile([128, 2, T], BF16) for _ in range(4)]
            vtok, Rvtok = P.tile([128, 4, 512], BF16)
            kbuf = [P.tile([128, 2, T], BF16) for _ in range(3)]
            vbuf = [P.tile([128, 4, 128], BF16) for _ in range(3)]
            PT = [P.tile([128, T], BF16) for _ in range(4)]
            tmpd = [P.tile([128, 128]) for _ in range(2)]
            r0, Rr0 = P.tile([128, T]); t0, Rt0 = P.tile([128, T])
            r1, Rr1 = P.tile([128, T]); t1, Rt1 = P.tile([128, T])
            sqb, Rsqb = P.tile([128, T], BF16)
            for h in range(4):
                S.op("pool", "memset", ap=kpad[h][0][64:128, 0, :], constant=0.0, writes=[kpad[h][1]])
                S.op("pool", "memset", ap=kpad[h][0][0:64, 1, :], constant=0.0, writes=[kpad[h][1]])
            for u in (0, 1):
                for j, b in proj_fm(l, u):
                    h = u * 2 + j
                    cp("act", qT[h][0][:], PSB[b][:], [RPS[b]], [qT[h][1]])
            for u in (2, 3):
                for j, b in proj_fm(l, u):
                    h = (u - 2) * 2 + j
                    cp("act", kpad[h][0][0:64, 0, :], PSB[b][0:64, :], [RPS[b]], [kpad[h][1]])
                    cp("dve", kpad[h][0][64:128, 1, :], PSB[b][64:128, :], [RPS[b]], [kpad[h][1]])
            for u in (4, 5):
                for sub, b in proj_tm(l, u):
                    cp("act" if sub % 2 else "dve", vtok[:, sub, (u - 4) * 256:(u - 3) * 256], PSB[b][:, 0:256], [RPS[b]], [Rvtok])
            if t < NT - 1:
                for h in range(4):
                    S.dma("pool", kcache[l, t, h], kpad[h][0][:].rearrange("p c n -> p (c n)"), reads=[kpad[h][1]], writes=[Rkc[l][t]])
                for h in range(4):
                    S.dma("pool", vcache[l, t, h].rearrange("p (s n) -> p s n", s=4), vtok[:, :, h * 128:(h + 1) * 128], reads=[Rvtok], writes=[Rvc[l][t]])
            nb = 0
            npt = [0]
            PIPE = 2
            for h in range(4):
                acc = [ps_get(hold=True) for _ in range(4)]
                jobs = []
                for kt in range(t + 1):
                    if kt < t:
                        kb_t, Rkb = kbuf[nb % 3]
                        vb_t, Rvb = vbuf[nb % 3]
                        nb += 1
                        ld = (kb_t, Rkb, vb_t, Rvb, kt)
                        kview, vview, Rv_ = kb_t, vb_t[:], Rvb
                    else:
                        ld = None
                        kview, Rkb = kpad[h]
                        vview = vtok[:, :, h * 128:(h + 1) * 128]
                        Rv_ = Rvtok
                    for kb in range(4):
                        q0 = kb * 128 if kt == t else 0
                        for c in range(2):
                            jobs.append(dict(kt=kt, kb=kb, c=c, q0=q0, kview=kview, Rkb=Rkb, vview=vview, Rv=Rv_,
                                             first=(kt == 0 and kb == 0), last=(kt == t and kb == 3),
                                             ld=(ld if (kb == 0 and c == 0) else None)))

                def emit_scores(J):
                    if J["ld"] is not None:
                        kb_t, Rkb_, vb_t, Rvb_, kt_l = J["ld"]
                        S.dma("pool", kb_t[:].rearrange("p c n -> p (c n)"), kcache[l, kt_l, h], reads=[Rkc[l][kt_l]], writes=[Rkb_])
                        S.dma("pool", vb_t[:].rearrange("p s n -> p (s n)"), vcache[l, kt_l, h], reads=[Rvc[l][kt_l]], writes=[Rvb_])
                    kt, kb, c, q0 = J["kt"], J["kb"], J["c"], J["q0"]
                    b = ps_get()
                    mm(PSB[b][:, q0:T], J["kview"][:, c, kb * 128:(kb + 1) * 128], qT[h][0][:, q0:T], True, True,
                       [J["Rkb"], qT[h][1]], [RPS[b]])
                    pt, Rpt = PT[npt[0] % 4]
                    npt[0] += 1
                    far0 = None
                    for qb in range(q0 // 128, 4):
                        rel = (4 * t + qb) - (4 * kt + kb)
                        if rel >= 2:
                            if far0 is None:
                                far0 = qb
                            continue
                        td, Rtd = tmpd[(npt[0] + qb) % 2]
                        stt("dve", td[:], PSB[b][:, qb * 128:(qb + 1) * 128], SCALE_A,
                            tabA[:, h, rel * 128:(rel + 1) * 128], ALU.mult, ALU.add, [RPS[b], Rc], [Rtd])
                        act(pt[:, qb * 128:(qb + 1) * 128], td[:], AF.Exp, [Rtd], [Rpt])
                    if far0 is not None:
                        act(pt[:, far0 * 128:T], PSB[b][:, far0 * 128:T], AF.Exp, [RPS[b], Rc], [Rpt],
                            scale=SCALE_A, bias=chA[:, h:h + 1])
                    J["pt"] = (pt, Rpt)

                def emit_pv(J):
                    pt, Rpt = J["pt"]
                    kb, c, q0 = J["kb"], J["c"], J["q0"]
                    mm(PSB[acc[c]][:, q0:T], J["vview"][:, kb, :], pt[:, q0:T], J["first"], J["last"], [J["Rv"], Rpt], [RPS[acc[c]]], inc=False)
                    mm(PSB[acc[2 + c]][:, q0:T], onesb[:], pt[:, q0:T], J["first"], J["last"], [Rc, Rpt], [RPS[acc[2 + c]]], inc=True)

                pending = []
                for J in jobs:
                    emit_scores(J)
                    pending.append(J)
                    if len(pending) > PIPE:
                        emit_pv(pending.pop(0))
                while pending:
                    emit_pv(pending.pop(0))
                S.op("dve", "reciprocal", out=r0[:], in_=PSB[acc[2]][:], reads=[RPS[acc[2]]], writes=[Rr0])
                tt("dve", t0[:], PSB[acc[0]][:], r0[:], ALU.mult, [RPS[acc[0]], Rr0], [Rt0])
                S.op("dve", "reciprocal", out=r1[:], in_=PSB[acc[3]][:], reads=[RPS[acc[3]]], writes=[Rr1])
                tt("dve", t1[:], PSB[acc[1]][:], r1[:], ALU.mult, [RPS[acc[1]], Rr1], [Rt1])
                for a in acc:
                    ps_release(a)
                stt("dve", t0[:], t1[:], nlam[:, l:l + 1], t0[:], ALU.mult, ALU.add, [Rt1, Rt0, Rc], [Rt0])
                act(sqb[:], t0[:], AF.Square, [Rt0], [Rsqb])
                b = ps_get()
                mm(PSB[b][:], onesb[:], sqb[:], True, True, [Rc, Rsqb], [RPS[b]])
                rstd_from(r0[:], PSB[b][:], 1.0 / 128, [RPS[b]], [Rr0], r1[:], Rr1)
                stt("dve", mixT[:, h, :], t0[:], gA[:, l:l + 1], r0[:], ALU.mult, ALU.mult, [Rt0, Rr0, Rc], [Rmix[h]])
            return P.close()

        def rope_tables(t, prev):
            P = Phase(prev)
            pi_t, Rpi = P.tile([128, T], I32)
            ang, Rang = P.tile([128, T])
            w1, Rw1 = P.tile([128, T])
            ki, Rki = P.tile([128, T], I32)
            S.dma("pool", pi_t[:], pos_d[:, t * T:(t + 1) * T].partition_broadcast(128), writes=[Rpi])
            cp("dve", ang[:], pi_t[:], [Rpi], [Rang])
            ts("dve", ang[:], ang[:], invd[:, 0:1], None, ALU.mult, None, [Rang, Rc], [Rang])
            for which, dst in ((0, sinT), (1, cosT)):
                src = ang
                if which == 1:
                    ts("dve", w1[:], ang[:], math.pi / 2, None, ALU.add, None, [Rang], [Rw1])
                    src = w1
                kf, Rkf = P.tile([128, T])
                ts("dve", kf[:], src[:], 1.0 / (2 * math.pi), None, ALU.mult, None, [Rang, Rw1], [Rkf])
                cp("dve", ki[:], kf[:], [Rkf], [Rki])
                cp("dve", kf[:], ki[:], [Rki], [Rkf])
                stt("dve", kf[:], kf[:], -2 * math.pi, src[:], ALU.mult, ALU.add, [Rkf, Rang, Rw1], [Rkf])
                ts("dve", kf[:], kf[:], 3.141592, -3.141592, ALU.min, ALU.max, [Rkf], [Rkf])
                act(dst[:], kf[:], AF.Sin, [Rkf], [Rcs])
            return P.close()

        def mixer_B(l, t, prev):
            P = Phase(prev)
            qt = [P.tile([128, T], BF16) for _ in range(4)]
            kt_ = [P.tile([128, T], BF16) for _ in range(4)]
            gs = [P.tile([128, T], BF16) for _ in range(4)]
            vtok, Rvtok = P.tile([128, 4, 512], BF16)
            P1 = Phase(prev)
            raw2 = [P1.tile([128, T], BF16) for _ in range(2)]
            a1, Ra1 = P1.tile([128, T]); a2, Ra2 = P1.tile([128, T])
            nraw = 0
            for (u0, dstl, goff) in ((6, qt, 0), (8, kt_, 512)):
                for u in (u0, u0 + 1):
                    for j, b in proj_fm(l, u):
                        h = (u - u0) * 2 + j
                        raw, Rraw = raw2[nraw % 2]
                        nraw += 1
                        cp("act", raw[:], PSB[b][:], [RPS[b]], [Rraw])
                        b2 = ps_get()
                        mm(PSB[b2][:], prot[:], raw[:], True, True, [Rc, Rraw], [RPS[b2]])
                        tt("dve", a1[:], raw[:], cosT[:], ALU.mult, [Rraw, Rcs], [Ra1])
                        tt("dve", a2[:], PSB[b2][:], sinT[:], ALU.mult, [RPS[b2], Rcs], [Ra2])
                        tt("pool", a1[:], a1[:], a2[:], ALU.add, [Ra1, Ra2], [Ra1])
                        tt("dve", dstl[h][0][:].rearrange("p (c n) -> p c n", c=4), a1[:].rearrange("p (c n) -> p c n", c=4),
                           bc(retc[:, goff + h * 128:goff + (h + 1) * 128], [128, 4, 128]), ALU.mult, [Ra1, Rc], [dstl[h][1]])
            ev1 = P1.close()
            for u in (10, 11):
                for sub, b in proj_tm(l, u):
                    cp("act" if sub % 2 else "dve", vtok[:, sub, (u - 10) * 256:(u - 9) * 256], PSB[b][:, 0:256], [RPS[b]], [Rvtok])
            for u in (12, 13):
                for j, b in proj_fm(l, u):
                    h = (u - 12) * 2 + j
                    act(gs[h][0][:], PSB[b][:], AF.Silu, [RPS[b]], [gs[h][1]])
            P2 = Phase(ev1)
            Pm = [P2.tile([128, 512], BF16) for _ in range(4)]
            ktok = [P2.tile([128, 4, 128], BF16) for _ in range(2)]
            Sbv = [P2.tile([128, 512], BF16) for _ in range(4)]
            tmpS, RtmpS = P2.tile([128, 512])
            ob, Rob = P2.tile([128, T], BF16); sqb, Rsqb = P2.tile([128, T], BF16)
            mean, Rmean = P2.tile([128, T]); var, Rvar = P2.tile([128, T]); dd, Rdd = P2.tile([128, T])
            for c in range(4):
                cs = slice(c * 128, (c + 1) * 128)
                b = ps_get()
                for h in range(4):
                    mm(PSB[b][:, h * 128:(h + 1) * 128], kt_[h][0][:, cs], qt[h][0][:, cs], True, True,
                       [kt_[h][1], qt[h][1]], [RPS[b]], inc=(h == 3))
                tt("dve", Pm[c][0][:].rearrange("p (h n) -> p h n", h=4), PSB[b][:].rearrange("p (h n) -> p h n", h=4),
                   bc(causT[:], [128, 4, 128]), ALU.mult, [RPS[b], Rc], [Pm[c][1]])
                b = ps_get()
                for h in range(4):
                    mm(PSB[b][:, h * 128:(h + 1) * 128], kt_[h][0][:, cs], identb[:], True, True, [kt_[h][1], Rc], [RPS[b]], inc=(h == 3))
                ktk, Rktk = ktok[c % 2]
                cp("act", ktk[:].rearrange("p h n -> p (h n)"), PSB[b][:], [RPS[b]], [Rktk])
                b = ps_get()
                for h in range(4):
                    mm(PSB[b][:, h * 128:(h + 1) * 128], ktk[:, h, :], vtok[:, c, h * 128:(h + 1) * 128], True, True,
                       [Rktk, Rvtok], [RPS[b]], inc=(h == 3))
                tt("dve", tmpS[:], PSB[b][:], S_ret[l][:], ALU.add, [RPS[b], RS_ret[l]], [RtmpS])
                tt("pool", S_ret[l][:], tmpS[:], retc[:, 1024:1536], ALU.mult, [RtmpS, Rc], [RS_ret[l]])
                if c < 3:
                    cp("act", Sbv[c + 1][0][:], S_ret[l][:], [RS_ret[l]], [Sbv[c + 1][1]])
            for h in range(4):
                hs = slice(h * 128, (h + 1) * 128)
                b = ps_get()
                for c in range(4):
                    cs = slice(c * 128, (c + 1) * 128)
                    mm(PSB[b][:, cs], vtok[:, c, hs], Pm[c][0][:, hs], True, False, [Rvtok, Pm[c][1]], [RPS[b]], inc=False)
                    if c == 0:
                        sbt, Rsbt = Sb_ret[l], RSb_ret[l]
                    else:
                        sbt, Rsbt = Sbv[c]
                    mm(PSB[b][:, cs], sbt[:, hs], qt[h][0][:, cs], False, True, [Rsbt, qt[h][1]], [RPS[b]], inc=(c == 3))
                cp("act", ob[:], PSB[b][:], [RPS[b]], [Rob])
                act(sqb[:], PSB[b][:], AF.Square, [RPS[b]], [Rsqb])
                b1 = ps_get(); b2 = ps_get()
                mm(PSB[b1][:], onesb[:], ob[:], True, True, [Rc, Rob], [RPS[b1]])
                mm(PSB[b2][:], onesb[:], sqb[:], True, True, [Rc, Rsqb], [RPS[b2]])
                ts("dve", mean[:], PSB[b1][:], 1.0 / 128, None, ALU.mult, None, [RPS[b1]], [Rmean])
                tt("pool", var[:], mean[:], mean[:], ALU.mult, [Rmean], [Rvar])
                stt("dve", var[:], PSB[b2][:], 1.0 / 128, var[:], ALU.mult, ALU.subtract, [RPS[b2], Rvar], [Rvar])
                act(var[:], var[:], AF.Ln, [Rvar, Rc], [Rvar], bias=eps_ap[:, 0:1])
                act(var[:], var[:], AF.Exp, [Rvar], [Rvar], scale=-0.5)
                tt("dve", dd[:], PSB[b][:], mean[:], ALU.subtract, [RPS[b], Rmean], [Rdd])
                tt("pool", dd[:], dd[:], var[:], ALU.mult, [Rdd, Rvar], [Rdd])
                tt("dve", mixT[:, 4 + h, :], dd[:], gs[h][0][:], ALU.mult, [Rdd, gs[h][1]], [Rmix[4 + h]])
            cp("act", Sb_ret[l][:], S_ret[l][:], [RS_ret[l]], [RSb_ret[l]])
            ev2 = P2.close()
            P.prev = ev2
            return P.close()

        def mixer_C(l, t, prev):
            P = Phase(prev)
            uT = [P.tile([128, T], BF16) for _ in range(4)]
            vg = [P.tile([128, 512]) for _ in range(4)]
            vnb = [P.tile([128, 512], BF16) for _ in range(4)]
            st6, Rst6 = P.tile([128, 8]); mv, Rmv = P.tile([128, 4])
            for u in (14, 15):
                for j, b in proj_fm(l, u):
                    g = (u - 14) * 2 + j
                    act(uT[g][0][:], PSB[b][:], AF.Gelu, [RPS[b]], [uT[g][1]])
            for u in (16, 17):
                for sub, b in proj_tm(l, u):
                    act(vg[sub][0][:, (u - 16) * 256:(u - 15) * 256], PSB[b][:, 0:256], AF.Gelu, [RPS[b]], [vg[sub][1]])
            for sub in range(4):
                v_, Rv_ = vg[sub]
                S.op("dve", "bn_stats", out=st6[:, 0:6], in_=v_[:], reads=[Rv_], writes=[Rst6])
                S.op("dve", "bn_aggr", out=mv[:, 0:2], in_=st6[:, 0:6], reads=[Rst6], writes=[Rmv])
                act(mv[:, 2:3], mv[:, 1:2], AF.Ln, [Rmv, Rc], [Rmv], bias=eps_ap[:, 0:1])
                act(mv[:, 3:4], mv[:, 2:3], AF.Exp, [Rmv], [Rmv], scale=-0.5)
                ts("dve", v_[:], v_[:], mv[:, 0:1], mv[:, 3:4], ALU.subtract, ALU.mult, [Rv_, Rmv], [Rv_])
                tt("pool", v_[:], v_[:], ggT[:, l * 512:(l + 1) * 512], ALU.mult, [Rv_, Rc], [Rv_])
                tt("dve", vnb[sub][0][:], v_[:], gbT[:, l * 512:(l + 1) * 512], ALU.add, [Rv_, Rc], [vnb[sub][1]])
            for g in range(4):
                gsl = slice(g * 128, (g + 1) * 128)
                b = ps_get()
                for sub in range(4):
                    cs = slice(sub * 128, (sub + 1) * 128)
                    mm(PSB[b][:, cs], vnb[sub][0][:, gsl], wsT[:, l * 4 + g, :], True, False, [vnb[sub][1], Rc], [RPS[b]], inc=False)
                    mm(PSB[b][:, cs], onesb[0:2, :], bs2[:, l * 512 + g * 128:l * 512 + (g + 1) * 128], False, True,
                       [Rc], [RPS[b]], inc=(sub == 3))
                tt("dve", mixT[:, 8 + g, :], uT[g][0][:], PSB[b][:], ALU.mult, [uT[g][1], RPS[b]], [Rmix[8 + g]])
            return P.close()

        def mixer_D(l, t, prev):
            P = Phase(prev)
            qt = [P.tile([128, T], BF16) for _ in range(4)]
            kt_ = [P.tile([128, T], BF16) for _ in range(4)]
            itok, Ritok = P.tile([128, 4, 512], BF16)
            E1s, RE1s = P.tile([128, 4, 8]); E2s, RE2s = P.tile([128, 4, 8])
            for u in (18, 19):
                for j, b in proj_fm(l, u):
                    h = (u - 18) * 2 + j
                    cp("act", qt[h][0][:], PSB[b][:], [RPS[b]], [qt[h][1]])
            P1 = Phase(prev)
            sg, Rsg = P1.tile([128, T]); keyp, Rkeyp = P1.tile([128, T])
            e1, Re1 = P1.tile([128, T]); e2, Re2 = P1.tile([128, T])
            logf, Rlogf = P1.tile([128, T])
            lft, Rlft = P1.tile([128, 4, 128])
            x8, Rx8 = P1.tile([128, 8])
            for u in (20, 21):
                for j, b in proj_fm(l, u):
                    h = (u - 20) * 2 + j
                    col = l * 4 + h
                    act(sg[:], PSB[b][:], AF.Sigmoid, [RPS[b]], [Rsg])
                    ts("dve", logf[:], sg[:], oml[:, col:col + 1], lbp[:, col:col + 1], ALU.mult, ALU.add, [Rsg, Rc], [Rlogf])
                    act(logf[:], logf[:], AF.Ln, [Rlogf], [Rlogf])
                    ts("dve", keyp[:], sg[:], noml[:, col:col + 1], oml[:, col:col + 1], ALU.mult, ALU.add, [Rsg, Rc], [Rkeyp])
                    bt = ps_get()
                    for c in range(4):
                        S.op("pe", "transpose", out=PSB[bt][:, c * 128:(c + 1) * 128], in_=logf[:, c * 128:(c + 1) * 128], identity=ident[:],
                             reads=[Rlogf, Rc], writes=[RPS[bt]], inc=(c == 3))
                    cp("dve", lft[:].rearrange("p c n -> p (c n)"), PSB[bt][:], [RPS[bt]], [Rlft])
                    bd_ = ps_get()
                    for c in range(4):
                        cs = slice(c * 128, (c + 1) * 128)
                        mm(PSB[bd_][:, cs], lft[:, c, :], ud[:], True, True, [Rlft, Rc], [RPS[bd_]], inc=(c == 3))
                    act(e2[:], PSB[bd_][:], AF.Exp, [RPS[bd_]], [Re2], scale=-1.0)
                    tt("dve", kt_[h][0][:], keyp[:], e2[:], ALU.mult, [Rkeyp, Re2], [kt_[h][1]])
                    act(e1[:], PSB[bd_][:], AF.Exp, [RPS[bd_]], [Re1])
                    tt("dve", qt[h][0][:], qt[h][0][:], e1[:], ALU.mult, [qt[h][1], Re1], [qt[h][1]])
                    e1v = e1[:].rearrange("p (cj n) -> p cj n", n=64)
                    lfv = logf[:].rearrange("p (cj n) -> p cj n", n=64)
                    cp("dve", E2s[:, h, :], e1v[:, :, 63], [Re1], [RE2s])
                    cp("dve", x8[:, 0:8], PSB[bd_][:].rearrange("p (cj n) -> p cj n", n=64)[:, :, 0], [RPS[bd_]], [Rx8])
                    tt("dve", x8[:, 0:8], lfv[:, :, 0], x8[:, 0:8], ALU.subtract, [Rlogf, Rx8], [Rx8])
                    act(E1s[:, h, :], x8[:, 0:8], AF.Exp, [Rx8], [RE1s])
            ev1 = P1.close()
            for u in (22, 23):
                for sub, b in proj_tm(l, u):
                    cp("act" if sub % 2 else "dve", itok[:, sub, (u - 22) * 256:(u - 21) * 256], PSB[b][:, 0:256], [RPS[b]], [Ritok])
            P2 = Phase(ev1)
            am32 = [P2.tile([128, 4, 128]) for _ in range(2)]
            gs = [P2.tile([128, T], BF16) for _ in range(4)]
            Am = [P2.tile([128, 4, 128], BF16) for _ in range(2)]
            ktok = [[P2.tile([128, 4, 128], BF16) for _ in range(2)] for _ in range(2)]
            Sbv = [[P2.tile([128, 4, 128], BF16) for _ in range(2)] for _ in range(2)]
            SE, RSE = P2.tile([128, 4, 128]); tmpS, RtmpS = P2.tile([128, 4, 128])
            sqb, Rsqb = P2.tile([128, T], BF16); rs, Rrs = P2.tile([128, T]); r2, Rr2 = P2.tile([128, T])
            for i2 in range(2):
                S.op("pool", "memset", ap=am32[i2][0][:], constant=0.0, writes=[am32[i2][1]])
            for u in (24, 25):
                for j, b in proj_fm(l, u):
                    h = (u - 24) * 2 + j
                    act(gs[h][0][:], PSB[b][:], AF.Silu, [RPS[b]], [gs[h][1]])
            S3 = S_hg[l][:].rearrange("p (h n) -> p h n", h=4)
            bo = [ps_get(hold=True) for _ in range(4)]
            for c in range(4):
                cs = slice(c * 128, (c + 1) * 128)
                am, Ram = Am[c % 2]
                b = ps_get()
                for h in range(4):
                    mm(PSB[b][:, h * 128:(h + 1) * 128], kt_[h][0][:, cs], qt[h][0][:, cs], True, True,
                       [kt_[h][1], qt[h][1]], [RPS[b]], inc=(h == 3))
                a32, Ra32 = am32[c % 2]
                for h in range(4):
                    S.op("dve", "copy_predicated", out=a32[:, h, :], mask=hgmask[:], data=PSB[b][:, h * 128:(h + 1) * 128],
                         reads=[RPS[b], Rc], writes=[Ra32])
                cp("act", am[:].rearrange("p h n -> p (h n)"), a32[:].rearrange("p h n -> p (h n)"), [Ra32], [Ram])
                b = ps_get()
                for h in range(4):
                    mm(PSB[b][:, h * 128:(h + 1) * 128], kt_[h][0][:, cs], identb[:], True, True, [kt_[h][1], Rc], [RPS[b]], inc=(h == 3))
                for j in range(2):
                    ktk, Rktk = ktok[c % 2][j]
                    ts("dve", ktk[:].rearrange("p h n -> p (h n)"), PSB[b][:], colmask[:, j:j + 1], None, ALU.mult, None,
                       [RPS[b], Rc], [Rktk])
                for j in range(2):
                    ktk, Rktk = ktok[c % 2][j]
                    sbv, Rsbv = Sbv[c % 2][j]
                    bd_ = ps_get()
                    for h in range(4):
                        mm(PSB[bd_][:, h * 128:(h + 1) * 128], ktk[:, h, :], itok[:, c, h * 128:(h + 1) * 128], True, True,
                           [Rktk, Ritok], [RPS[bd_]], inc=(h == 3))
                    cj = c * 2 + j
                    for h in range(4):
                        hs_ = slice(h * 128, (h + 1) * 128)
                        ts("dve", sbv[:, h, :], S_hg[l][:, hs_], E1s[:, h, cj:cj + 1], None, ALU.mult, None, [RS_hg[l], RE1s], [Rsbv])
                        stt("dve", tmpS[:, h, :], S_hg[l][:, hs_], E1s[:, h, cj:cj + 1], PSB[bd_][:, hs_], ALU.mult, ALU.add,
                            [RS_hg[l], RE1s, RPS[bd_]], [RtmpS])
                        ts("dve", S_hg[l][:, hs_], tmpS[:, h, :], E2s[:, h, cj:cj + 1], None, ALU.mult, None, [RtmpS, RE2s], [RS_hg[l]])
                for h in range(4):
                    hs = slice(h * 128, (h + 1) * 128)
                    mm(PSB[bo[h]][:, cs], itok[:, c, hs], am[:, h, :], True, False, [Ritok, Ram], [RPS[bo[h]]], inc=False)
                    for j in range(2):
                        js = slice(c * 128 + j * 64, c * 128 + (j + 1) * 64)
                        sbv, Rsbv = Sbv[c % 2][j]
                        mm(PSB[bo[h]][:, js], sbv[:, h, :], qt[h][0][:, js], False, j == 1, [Rsbv, qt[h][1]], [RPS[bo[h]]],
                           inc=(j == 1))
            bss = ps_get(hold=True)
            for h in range(4):
                b = bo[h]
                act(sqb[:], PSB[b][:], AF.Square, [RPS[b]], [Rsqb])
                mm(PSB[bss][:], onesb[:], sqb[:], h == 0, h == 3, [Rc, Rsqb], [RPS[bss]], inc=True)
            rstd_from(rs[:], PSB[bss][:], 1.0 / 512, [RPS[bss]], [Rrs], r2[:], Rr2)
            ps_release(bss)
            for h in range(4):
                col = l * 4 + h
                b = bo[h]
                stt("dve", r2[:], PSB[b][:], par[:, 2 + col:3 + col], rs[:], ALU.mult, ALU.mult, [RPS[b], Rrs, Rc], [Rr2])
                tt("dve", mixT[:, 12 + h, :], r2[:], gs[h][0][:], ALU.mult, [Rr2, gs[h][1]], [Rmix[12 + h]])
                ps_release(b)
            ev2 = P2.close()
            P.prev = ev2
            return P.close()

        def mix_out(l, prev):
            P = Phase(prev)
            sq2 = [P.tile([128, T], BF16) for _ in range(2)]
            lnst = ln_begin()
            for u in range(8):
                slot, Rs = w_next(l, "wmo", u)
                s3 = slot[:].rearrange("p (kc n) -> p kc n", kc=KC)
                for j in range(2):
                    m = u * 2 + j
                    b = ps_get()
                    for kc in range(KC):
                        mm(PSB[b][:], s3[:, kc, j * 128:(j + 1) * 128], mixT[:, kc, :], kc == 0, kc == KC - 1, [Rs, Rmix[kc]], [RPS[b]])
                    resid_chunk(lnst, m, PSB[b][:], [RPS[b]], 1.0 / ALPHA, [s[0] for s in sq2], [s[1] for s in sq2])
            ln_finish(lnst, l, 1, P)
            return P.close()

        def ple(l, t, prev):
            P = Phase(prev)
            sq2 = [P.tile([128, T], BF16) for _ in range(2)]
            pst, Rpst = P.tile([128, 4, PD])
            pT, RpT = P.tile([128, 2, T], BF16)
            gt2 = [P.tile([128, T]) for _ in range(2)]
            S.dma("pool", pst[:], p_d[l, t * T:(t + 1) * T, :].rearrange("(s p) n -> p s n", p=128), writes=[Rpst])
            for kc2 in range(2):
                b = ps_get()
                for sub in range(4):
                    S.op("pe", "transpose", out=PSB[b][:, sub * 128:(sub + 1) * 128], in_=pst[:, sub, kc2 * 128:(kc2 + 1) * 128],
                         identity=ident[:], reads=[Rpst, Rc], writes=[RPS[b]], inc=(sub == 3))
                cp("dve", pT[:, kc2, :], PSB[b][:], [RPS[b]], [RpT])
            gates = []
            G16, RG16_ = P.tile([128, KC, T], BF16)
            RG16 = [Res() for _ in range(KC)]
            for r in RG16:
                r.r = dict(prev)
            P.res.extend(RG16)
            for u in range(8):
                slot, Rs = w_next(l, "wpg", u)
                s3 = slot[:].rearrange("p (kc n) -> p kc n", kc=KC)
                for j in range(2):
                    m = u * 2 + j
                    b = ps_get()
                    for kc in range(KC):
                        mm(PSB[b][:], s3[:, kc, j * 128:(j + 1) * 128], Xb[:, kc, :], kc == 0, kc == KC - 1, [Rs, RXb[kc]], [RPS[b]])
                    act(G16[:, m, :], PSB[b][:], AF.Sigmoid, [RPS[b]], [RG16[m]])
            slot, Rs = w_next(l, "wpp", 0)
            s3 = slot[:].rearrange("p (kc n) -> p kc n", kc=2)
            lnst = ln_begin()
            for m in range(KC):
                b = ps_get()
                for kc2 in range(2):
                    mm(PSB[b][:], s3[:, kc2, m * 128:(m + 1) * 128], pT[:, kc2, :], kc2 == 0, kc2 == 1, [Rs, RpT], [RPS[b]])
                g_, Rg_ = gt2[m % 2]
                tt("dve", g_[:], PSB[b][:], G16[:, m, :], ALU.mult, [RPS[b], RG16[m]], [Rg_])
                resid_chunk(lnst, m, g_[:], [Rg_], 1.0 / ALPHA, [s[0] for s in sq2], [s[1] for s in sq2])
            ln_finish(lnst, l, 3, P)
            return P.close()

        def load_x(t, prev):
            P = Phase(prev)
            stg2 = [P.tile([128, D]) for _ in range(2)]
            for sub in range(4):
                sg_, Rsg_ = stg2[sub % 2]
                S.dma("pool", sg_[:], x_d[t * T + sub * 128:t * T + (sub + 1) * 128, :], writes=[Rsg_])
                for g4 in range(4):
                    b = ps_get()
                    for j in range(4):
                        kc = g4 * 4 + j
                        S.op("pe", "transpose", out=PSB[b][:, j * 128:(j + 1) * 128], in_=sg_[:, kc * 128:(kc + 1) * 128],
                             identity=ident[:], reads=[Rsg_, Rc], writes=[RPS[b]], inc=(j == 3))
                    rx = RX[g4 * 4:g4 * 4 + 4]
                    rxb = RXb[g4 * 4:g4 * 4 + 4]
                    cp("dve", X[:, g4 * 4:g4 * 4 + 4, sub * 128:(sub + 1) * 128], PSB[b][:].rearrange("p (a n) -> p a n", a=4), [RPS[b]], rx)
                    cp("pool", Xb[:, g4 * 4:g4 * 4 + 4, sub * 128:(sub + 1) * 128], X[:, g4 * 4:g4 * 4 + 4, sub * 128:(sub + 1) * 128], rx, rxb)
            return P.close()

        def store_y(t, prev):
            P = Phase(prev)
            stg2 = [P.tile([128, D]) for _ in range(2)]
            for sub in range(4):
                sg_, Rsg_ = stg2[sub % 2]
                for g4 in range(4):
                    b = ps_get()
                    for j in range(4):
                        kc = g4 * 4 + j
                        S.op("pe", "transpose", out=PSB[b][:, j * 128:(j + 1) * 128], in_=X[:, kc, sub * 128:(sub + 1) * 128],
                             identity=ident[:], reads=[RX[kc], Rc], writes=[RPS[b]], inc=(j == 3))
                    cp("act" if g4 % 2 else "dve", sg_[:, g4 * 512:(g4 + 1) * 512], PSB[b][:], [RPS[b]], [Rsg_])
                S.dma("pool", y_d[t * T + sub * 128:t * T + (sub + 1) * 128, :], sg_[:], reads=[Rsg_], writes=[Ry])
            return P.close()

        ev = prev_ev
        for t in range(NT):
            if 'rope' not in SKIP:
                ev = rope_tables(t, ev)
            if 'loadx' not in SKIP:
                ev = load_x(t, ev)
            for l in range(nlayers):
                if stop >= 1:
                    ev = ffn(l, "w1i", "w1o", 0, ev)
                else:
                    for _ in range(FC + 32):
                        st["next"] += 1
                stages = [(2, mixer_A, 6), (3, mixer_B, 8), (4, mixer_C, 4), (5, mixer_D, 8)]
                for sid, fn, nun in stages:
                    if stop >= sid:
                        ev = fn(l, t, ev)
                    else:
                        st["next"] += nun
                if dumpmix:
                    for m in range(KC):
                        cp("dve", X[:, m, :], mixT[:, m, :], [Rmix[m]], [RX[m]])
                if stop >= 6:
                    ev = mix_out(l, ev)
                else:
                    st["next"] += 8
                if stop >= 7:
                    ev = ffn(l, "w2i", "w2o", 2, ev)
                else:
                    st["next"] += FC + 32
                if stop >= 8:
                    ev = ple(l, t, ev)
                else:
                    st["next"] += 9
            for _ in range((L - nlayers) * NU):
                st["next"] += 1
            if 'storey' not in SKIP:
                ev = store_y(t, ev)
        S.finish("pool", [Ry])
        S.finish("sp", Rring)
        S.emit(block)
    return nc


def make_in_map(inputs, b, S_len):
    f32 = np.float32
    m = {}
    m["x"] = np.ascontiguousarray(inputs["x"][b, :S_len])
    m["p"] = np.ascontiguousarray(inputs["p"][:, b, :S_len])
    m["pos"] = np.ascontiguousarray(inputs["positions"][b:b + 1, :S_len]).astype(np.int32)
    m["w1i"] = inputs["ffn1_w_in"]; m["w1o"] = inputs["ffn1_w_out"]
    m["wmi"] = inputs["w_mix_in"]; m["wmo"] = inputs["w_mix_out"]
    m["w2i"] = inputs["ffn2_w_in"]; m["w2o"] = inputs["ffn2_w_out"]
    m["wpg"] = inputs["ple_w_gate"]; m["wpp"] = inputs["ple_w_proj"]
    m["relb"] = np.ascontiguousarray(inputs["rel_bias"]).reshape(1, 128)
    m["dlam"] = np.ascontiguousarray(inputs["diff_lambda"]).reshape(1, -1)
    par = np.zeros((32, 128), f32)
    par[0:2] = inputs["diff_norm_g"]
    par[2:10] = np.ascontiguousarray(inputs["hgrn_norm_g"]).reshape(8, 128)
    par[10:18] = np.ascontiguousarray(inputs["hgrn_lb_logits"]).reshape(8, 128)
    m["par"] = par
    m["gg"] = np.ascontiguousarray(inputs["gmlp_ln_g"]).reshape(1, -1)
    m["gb"] = np.ascontiguousarray(inputs["gmlp_ln_b"]).reshape(1, -1)
    m["gws"] = np.ascontiguousarray(inputs["gmlp_w_s"])
    m["gbs"] = np.ascontiguousarray(inputs["gmlp_b_s"]).reshape(1, -1)
    m["lng"] = np.ascontiguousarray(inputs["ln_g"]).reshape(128, 128)
    m["lnb"] = np.ascontiguousarray(inputs["ln_b"]).reshape(128, 128)
    m.update(host_consts())
    return {k: np.ascontiguousarray(v) for k, v in m.items()}


def kernel(**inputs):
    inputs = {k: np.asarray(v) for k, v in inputs.items()}
    B, S_len = inputs["x"].shape[:2]
    F = inputs["ffn1_w_out"].shape[1]
    nc = build(S_len, F)
    in_maps = [make_in_map(inputs, b, S_len) for b in range(B)]
    res = run_bass_kernel_spmd(nc, in_maps, core_ids=list(range(B)))
    out = np.stack([np.asarray(res.results[b]["y"]) for b in range(B)], axis=0)
    return out.astype(np.float32)
```

```python
import math
from contextlib import ExitStack
import numpy as np
import ml_dtypes
import concourse.bass as bass
import concourse.mybir as mybir
from concourse.bass_utils import run_bass_kernel_spmd

F32 = mybir.dt.float32
BF16 = mybir.dt.bfloat16
I32 = mybir.dt.int32
U32 = mybir.dt.uint32
AF = mybir.ActivationFunctionType
ALU = mybir.AluOpType

D = 2048
KC = 16
T = 512
PD = 256
DEPTH = 2
ALPHA = (2 * DEPTH) ** 0.25
EPS = 1e-5
EPS_LN = EPS / (ALPHA * ALPHA)
SLOT = 4096
NSLOT = 4
MIXCOLS = 6656


class Res:
    __slots__ = ("name", "w", "r")

    def __init__(self, name="r"):
        self.name = name
        self.w = None
        self.r = {}


class Sched:
    ENG = ("pe", "act", "dve", "pool", "sp")

    def __init__(self, nc, es, n_dma_sems=20):
        self.nc = nc
        self.sems = {}
        self.cnt = {}
        for e in self.ENG:
            self.sems[e] = es.enter_context(nc.semaphore("s_" + e))
            self.cnt[e] = 0
        self.dma_sems = []
        for i in range(n_dma_sems):
            k = "d%d" % i
            self.sems[k] = es.enter_context(nc.semaphore("s_" + k))
            self.cnt[k] = 0
            self.dma_sems.append(k)
        self.dma_rr = 0
        self.known = {e: {} for e in self.ENG}
        self.prog = {e: [] for e in self.ENG}
        self.n_wait = 0
        self.n_inst = 0

    def _wait(self, e, key, val):
        if val <= 0:
            return
        if e == "pe" and key == "pe":
            return
        kn = self.known[e]
        if kn.get(key, 0) >= val:
            return
        self.prog[e].append(("wait", key, val))
        kn[key] = val
        self.n_wait += 1

    def _deps(self, e, reads, writes):
        need = {}
        for R in reads:
            if R.w is not None:
                k, v = R.w
                if need.get(k, 0) < v:
                    need[k] = v
        for R in writes:
            if R.w is not None:
                k, v = R.w
                if need.get(k, 0) < v:
                    need[k] = v
            for k, v in R.r.items():
                if need.get(k, 0) < v:
                    need[k] = v
        for k, v in need.items():
            self._wait(e, k, v)

    def _mark(self, ev, reads, writes):
        k, v = ev
        for R in writes:
            R.w = ev
            R.r = {}
        for R in reads:
            if R.r.get(k, 0) < v:
                R.r[k] = v

    def op(self, e, name, reads=(), writes=(), inc=True, **kw):
        self._deps(e, reads, writes)
        if inc:
            self.cnt[e] += 1
            self.prog[e].append(("op", name, kw, e, 1))
            self._mark((e, self.cnt[e]), reads, writes)
        else:
            self.prog[e].append(("op", name, kw, None, 0))
            self._mark((e, self.cnt[e] + 1), reads, writes)
        self.n_inst += 1

    def dma(self, q, out, in_, reads=(), writes=(), **kw):
        k = self.dma_sems[self.dma_rr]
        self.dma_rr = (self.dma_rr + 1) % len(self.dma_sems)
        self._wait(q, k, self.cnt[k])
        self._deps(q, reads, writes)
        self.cnt[k] += 16
        kw = dict(kw)
        kw["out"] = out
        kw["in_"] = in_
        self.prog[q].append(("op", "dma_start", kw, k, 16))
        self._mark((k, self.cnt[k]), reads, writes)
        self.n_inst += 1

    def events(self, resources):
        ev = {}
        for R in resources:
            if R.w is not None:
                k, v = R.w
                if ev.get(k, 0) < v:
                    ev[k] = v
            for k, v in R.r.items():
                if ev.get(k, 0) < v:
                    ev[k] = v
        return ev

    def finish(self, e, resources):
        for k, v in self.events(resources).items():
            self._wait(e, k, v)

    def emit(self, block):
        sems = self.sems

        def run(eng, prog):
            for it in prog:
                if it[0] == "wait":
                    eng.wait_ge(sems[it[1]], it[2])
                else:
                    _, name, kw, sk, inc = it
                    inst = getattr(eng, name)(**kw)
                    if sk is not None:
                        inst.then_inc(sems[sk], inc)

        block.tensor(lambda e: run(e, self.prog["pe"]))
        block.scalar(lambda e: run(e, self.prog["act"]))
        block.vector(lambda e: run(e, self.prog["dve"]))
        block.gpsimd(lambda e: run(e, self.prog["pool"]))
        block.sync(lambda e: run(e, self.prog["sp"]))


def t5_bucket_np(n):
    n = np.maximum(n, 0)
    nf = np.maximum(n, 1).astype(np.float32)
    large = 16 + (np.log(nf / np.float32(16)) / np.float32(math.log(128 / 16)) * np.float32(16)).astype(np.int32)
    large = np.minimum(large, 31)
    return np.where(n < 16, n, large)


def host_consts():
    c = {}
    eye = np.eye(128, dtype=np.float32)
    c["c_ident"] = eye
    k = np.arange(128)[:, None]
    q = np.arange(128)[None, :]
    oh = np.zeros((128, 32, 256), np.float32)
    bd = t5_bucket_np(q - k)
    bs = t5_bucket_np(q - k + 128)
    for b in range(32):
        oh[:, b, 0:128] = (bd == b)
        oh[:, b, 128:256] = (bs == b)
    c["c_oh"] = oh
    c["c_negmask"] = np.where(k > q, np.float32(-1e30), np.float32(0)).astype(np.float32)
    c["c_causT"] = (q >= k).astype(np.float32)
    blk = ((k // 64) == (q // 64)) & (q >= k)
    c["c_hgmask"] = blk.astype(np.float32)
    c["c_tril"] = (k >= q).astype(np.float32)
    inv = (10000.0 ** (-np.linspace(0.0, 1.0, 64))).astype(np.float32)
    c["c_invd"] = np.repeat(inv, 2)[:, None].astype(np.float32)
    prot = np.zeros((128, 128), np.float32)
    for i in range(64):
        prot[2 * i + 1, 2 * i] = -1.0
        prot[2 * i, 2 * i + 1] = 1.0
    c["c_prot"] = prot
    g = 1.0 - 2.0 ** (-5.0 - np.arange(4, dtype=np.float64))
    j = np.arange(128, dtype=np.float64)
    gq = np.stack([g[h] ** (j + 1.0) for h in range(4)])
    gk = np.stack([g[h] ** (-(j + 1.0)) * (128.0 ** -0.5) for h in range(4)])
    gs = np.stack([np.full(128, g[h] ** 128.0) for h in range(4)])
    c["c_ret"] = np.concatenate([gq.reshape(1, 512), gk.reshape(1, 512), gs.reshape(1, 512)], 1).astype(np.float32)
    ud = np.zeros((128, 128), np.float32)
    ux = np.zeros((128, 4), np.float32)
    for s in range(128):
        cs, ls = s // 64, s % 64
        for t in range(128):
            ct, lt = t // 64, t % 64
            if cs != ct:
                continue
            if 31 < ls <= lt:
                ud[s, t] = 1.0
            elif lt < ls <= 31:
                ud[s, t] = -1.0
        if ls <= 31:
            ux[s, 2 * cs] = 1.0
        else:
            ux[s, 2 * cs + 1] = 1.0
    c["c_ud"] = ud
    c["c_ux"] = ux
    cm = np.zeros((128, 4), np.float32)
    cm[0:64, 0] = 1.0
    cm[64:128, 1] = 1.0
    cm[0, 2] = 1.0
    cm[1, 3] = 1.0
    c["c_colmask"] = cm
    return c


WMI_ORDER = list(range(26))


def unit_table(F):
    FC = F // 128
    FH = FC // 2
    units = []
    def ffn_units(tagi, tago):
        u = []
        for hf in range(2):
            for jj in range(FH):
                u.append((tagi, hf * FH + jj))
            for m in range(16):
                u.append((tago, hf * 16 + m))
        return u
    units += ffn_units("w1i", "w1o")
    for u in WMI_ORDER:
        units.append(("wmi", u))
    for u in range(8):
        units.append(("wmo", u))
    units += ffn_units("w2i", "w2o")
    for u in range(8):
        units.append(("wpg", u))
    units.append(("wpp", 0))
    return units


def build(S_len, F, stop=99, nlayers=DEPTH, dumpmix=False):
    NT = S_len // T
    FC = F // 128
    FH = FC // 2
    L = DEPTH
    nc = bass.Bass("TRN2", target_bir_lowering=False)

    def din(name, shape, dt=F32):
        return nc.dram_tensor(name, list(shape), dt, kind="ExternalInput").ap()

    x_d = din("x", [S_len, D])
    p_d = din("p", [L, S_len, PD])
    pos_d = din("pos", [1, S_len], I32)
    W = {
        "w1i": din("w1i", [L, D, 2 * F]), "w1o": din("w1o", [L, F, D]),
        "wmi": din("wmi", [L, D, MIXCOLS]), "wmo": din("wmo", [L, D, D]),
        "w2i": din("w2i", [L, D, 2 * F]), "w2o": din("w2o", [L, F, D]),
        "wpg": din("wpg", [L, D, D]), "wpp": din("wpp", [L, PD, D]),
    }
    relb_d = din("relb", [1, 128])
    dlam_d = din("dlam", [1, L * 256])
    par_d = din("par", [32, 128])
    gg_d = din("gg", [1, L * 512])
    gb_d = din("gb", [1, L * 512])
    gws_d = din("gws", [L, 4, 128, 128])
    gbs_d = din("gbs", [1, L * 512])
    lng_d = din("lng", [128, 128])
    lnb_d = din("lnb", [128, 128])
    cst = host_consts()
    C = {k: din(k, v.shape) for k, v in cst.items()}
    y_d = nc.dram_tensor("y", [S_len, D], F32, kind="ExternalOutput").ap()

    units = unit_table(F)
    debug_mode = (stop < 99 or nlayers < DEPTH)
    import os
    SKIP = os.environ.get('KSKIP', '').split(',')
    NU = len(units)
    uidx = {u: i for i, u in enumerate(units)}
    wsc = [nc.dram_tensor("wsc%d" % l_, [NU, 128, SLOT], BF16).ap() for l_ in range(L)]
    kcache = nc.dram_tensor("kcache", [L, NT, 4, 128, 1024], BF16).ap()
    vcache = nc.dram_tensor("vcache", [L, NT, 4, 128, 512], BF16).ap()

    with ExitStack() as es:
        S = Sched(nc, es)
        block = es.enter_context(nc.Block())

        def sb(name, shape, dt=F32, stack=es):
            return stack.enter_context(nc.sbuf_tensor("sb_" + name, list(shape), dt))

        X = sb("X", [128, KC, T])
        Xb = sb("Xb", [128, KC, T], BF16)
        mixT = sb("mixT", [128, KC, T], BF16)
        RX = [Res("X%d" % i) for i in range(KC)]
        RXb = [Res("Xb%d" % i) for i in range(KC)]
        Rmix = [Res("mix%d" % i) for i in range(KC)]
        ring = [sb("ring%d" % i, [128, SLOT], BF16) for i in range(NSLOT)]
        Rring = [Res("ring%d" % i) for i in range(NSLOT)]
        ident = sb("ident", [128, 128]); identb = sb("identb", [128, 128], BF16)
        ones = sb("ones", [128, 128]); onesb = sb("onesb", [128, 128], BF16)
        prot = sb("prot", [128, 128], BF16)
        tabA = sb("tabA", [128, 4, 256])
        chA = sb("chA", [128, 4])
        causT = sb("causT", [128, 128]); hgmask = sb("hgmask", [128, 128], U32)
        ud = sb("ud", [128, 128]); ux = sb("ux", [128, 4]); colmask = sb("colmask", [128, 4])
        invd = sb("invd", [128, 1])
        retc = sb("retc", [128, 1536])
        lnG = sb("lnG", [128, 128]); lnB = sb("lnB", [128, 128])
        par = sb("par", [128, 32])
        lbp = sb("lbp", [128, 8]); oml = sb("oml", [128, 8]); noml = sb("noml", [128, 8])
        gA = sb("gA", [128, 2]); nlam = sb("nlam", [128, 2])
        ggT = sb("ggT", [128, L * 512]); gbT = sb("gbT", [128, L * 512])
        wsT = sb("wsT", [128, L * 4, 128], BF16)
        bs2 = sb("bs2", [2, L * 512], BF16)
        S_ret = [sb("S_ret%d" % l, [128, 512]) for l in range(L)]
        Sb_ret = [sb("Sb_ret%d" % l, [128, 512], BF16) for l in range(L)]
        S_hg = [sb("S_hg%d" % l, [128, 512]) for l in range(L)]
        cosT = sb("cosT", [128, T]); sinT = sb("sinT", [128, T])
        Rc = Res("consts")
        RS_ret = [Res() for _ in range(L)]; RSb_ret = [Res() for _ in range(L)]; RS_hg = [Res() for _ in range(L)]
        Rcs = Res("cossin")
        Rkc = [[Res() for _ in range(NT)] for _ in range(L)]
        Rvc = [[Res() for _ in range(NT)] for _ in range(L)]
        Ry = Res("y")

        PSB = [es.enter_context(nc.psum_tensor("ps%d" % i, [128, 512], F32)) for i in range(8)]
        RPS = [Res("ps%d" % i) for i in range(8)]
        ps_state = {"rr": 0, "held": set()}

        def ps_get(hold=False):
            for _ in range(16):
                i = ps_state["rr"]
                ps_state["rr"] = (i + 1) % 8
                if i not in ps_state["held"]:
                    if hold:
                        ps_state["held"].add(i)
                    return i
            raise RuntimeError("no psum bank")

        def ps_release(i):
            ps_state["held"].discard(i)

        def mm(out, lhsT, rhs, start, stop, reads, writes, inc=None):
            S.op("pe", "matmul", out=out, lhsT=lhsT, rhs=rhs, start=start, stop=stop,
                 reads=reads, writes=writes, inc=(stop if inc is None else inc))

        def act(out, in_, func, reads, writes, **kw):
            S.op("act", "activation", out=out, in_=in_, func=func, reads=reads, writes=writes, **kw)

        def tt(e, out, in0, in1, op, reads, writes):
            S.op(e, "tensor_tensor", out=out, in0=in0, in1=in1, op=op, reads=reads, writes=writes)

        def ts(e, out, in0, s1, s2, op0, op1, reads, writes):
            if s2 is None:
                S.op(e, "tensor_scalar", out=out, in0=in0, scalar1=s1, scalar2=None, op0=op0, reads=reads, writes=writes)
            else:
                S.op(e, "tensor_scalar", out=out, in0=in0, scalar1=s1, scalar2=s2, op0=op0, op1=op1, reads=reads, writes=writes)

        def stt(e, out, in0, scalar, in1, op0, op1, reads, writes):
            S.op(e, "scalar_tensor_tensor", out=out, in0=in0, scalar=scalar, in1=in1, op0=op0, op1=op1,
                 reads=reads, writes=writes)

        def cp(e, out, in_, reads, writes):
            if e == "act":
                act(out, in_, AF.Copy, reads, writes)
            else:
                S.op(e, "tensor_copy", out=out, in_=in_, reads=reads, writes=writes)

        def bc(ap, shape):
            return ap.unsqueeze(1).broadcast_to(list(shape))

        def rstd_from(out, in_, scale, reads, writes, tmp, Rtmp):
            act(tmp, in_, AF.Ln, reads, [Rtmp], scale=scale, bias=eps_ap[:, 0:1])
            act(out, tmp, AF.Exp, [Rtmp], writes, scale=-0.5)

        class Phase:
            def __init__(self, prev_ev):
                self.es = ExitStack()
                self.res = []
                self.prev = prev_ev
                self.n = 0

            def tile(self, shape, dt=F32):
                phase_ctr[0] += 1
                t = sb("ph%d" % phase_ctr[0], shape, dt, stack=self.es)
                r = Res()
                r.r = dict(self.prev)
                self.res.append(r)
                return t, r

            def close(self):
                ev = S.events(self.res)
                for k, v in self.prev.items():
                    if ev.get(k, 0) < v:
                        ev[k] = v
                self.es.close()
                return ev

        phase_ctr = [0]
        eps_t = sb("eps_t", [128, 2])
        eps_ap = eps_t

        Rwsc = [[Res() for _ in range(NU)] for _ in range(L)]

        def wsrc(l, tag, idx):
            dst = wsc[l][uidx[(tag, idx)]]
            if tag in ("w1i", "w2i"):
                w = W[tag][l]
                d3 = dst.rearrange("p (kc n) -> p kc n", kc=KC)
                return [(d3[:, :, 0:128], w[:, idx * 128:(idx + 1) * 128].rearrange("(kc p) n -> p kc n", p=128)),
                        (d3[:, :, 128:256], w[:, F + idx * 128:F + (idx + 1) * 128].rearrange("(kc p) n -> p kc n", p=128))]
            if tag in ("w1o", "w2o"):
                hf, m = idx // 16, idx % 16
                w = W[tag][l]
                d3 = dst[:, 0:FH * 128].rearrange("p (fc n) -> p fc n", fc=FH)
                return [(d3, w[hf * FH * 128:(hf + 1) * FH * 128, m * 128:(m + 1) * 128].rearrange("(fc p) n -> p fc n", p=128))]
            if tag in ("wmi", "wmo", "wpg"):
                w = W[tag][l]
                d3 = dst.rearrange("p (kc n) -> p kc n", kc=KC)
                return [(d3, w[:, idx * 256:(idx + 1) * 256].rearrange("(kc p) n -> p kc n", p=128))]
            if tag == "wpp":
                w = W[tag][l]
                d3 = dst.rearrange("p (kc n) -> p kc n", kc=2)
                return [(d3, w.rearrange("(kc p) n -> p kc n", p=128))]
            raise KeyError(tag)

        pro_list = []
        for l in range(L):
            for (tag, idx) in units:
                pro_list.append((l, uidx[(tag, idx)], wsrc(l, tag, idx)))
        pro = {"ptr": 0}

        def pump_until(l, u, ahead=6):
            return

        def pump_all():
            target = len(pro_list) if 'prologue' not in SKIP else 0
            while pro["ptr"] < target:
                lj, uj, lst = pro_list[pro["ptr"]]
                for dv, sv in lst:
                    S.dma("pool", dv, sv, writes=[Rwsc[lj][uj]])
                pro["ptr"] += 1

        pump_all()

        stream_seq = []
        for t_ in range(NT):
            for l in range(L):
                for u in range(NU):
                    stream_seq.append((l, u))
        st = {"issued": 0, "next": 0}

        def w_next(l, tag, idx):
            u = uidx[(tag, idx)]
            i = st["next"]
            assert stream_seq[i] == (l, u), (stream_seq[i], (l, u, tag, idx))
            if debug_mode:
                k = st["issued"]
                pump_until(l, u)
                S.dma("sp", ring[k % NSLOT][:], wsc[l][u], reads=[Rwsc[l][u]], writes=[Rring[k % NSLOT]])
                st["issued"] += 1
                st["next"] += 1
                return ring[k % NSLOT], Rring[k % NSLOT]
            while st["issued"] < min(len(stream_seq), i + NSLOT):
                j = st["issued"]
                lj, uj = stream_seq[j]
                pump_until(lj, uj)
                S.dma("sp", ring[j % NSLOT][:], wsc[lj][uj], reads=[Rwsc[lj][uj]], writes=[Rring[j % NSLOT]])
                st["issued"] += 1
            st["next"] += 1
            return ring[i % NSLOT], Rring[i % NSLOT]

        ph = Phase({})
        stg, Rstg = ph.tile([128, 128])
        for name, dst in (("c_ident", ident), ("c_causT", causT), ("c_ud", ud)):
            S.dma("sp", dst[:], C[name], writes=[Rc])
        S.dma("sp", ux[:], C["c_ux"], writes=[Rc])
        S.dma("sp", colmask[:], C["c_colmask"], writes=[Rc])
        S.dma("sp", invd[:], C["c_invd"], writes=[Rc])
        S.dma("sp", retc[:], C["c_ret"].partition_broadcast(128), writes=[Rc])
        S.dma("sp", ggT[:], gg_d.partition_broadcast(128), writes=[Rc])
        S.dma("sp", gbT[:], gb_d.partition_broadcast(128), writes=[Rc])
        S.op("pool", "memset", ap=ones[:], constant=1.0, writes=[Rc])
        S.op("pool", "memset", ap=onesb[:], constant=1.0, writes=[Rc])
        S.op("pool", "memset", ap=eps_t[:, 0:1], constant=EPS, writes=[Rc])
        S.op("pool", "memset", ap=eps_t[:, 1:2], constant=EPS_LN, writes=[Rc])
        for l in range(L):
            S.op("pool", "memset", ap=S_ret[l][:], constant=0.0, writes=[RS_ret[l]])
            S.op("pool", "memset", ap=Sb_ret[l][:], constant=0.0, writes=[RSb_ret[l]])
            S.op("pool", "memset", ap=S_hg[l][:], constant=0.0, writes=[RS_hg[l]])
        cp("dve", identb[:], ident[:], [Rc], [Rc])
        S.dma("sp", stg[:], C["c_prot"], writes=[Rstg])
        cp("dve", prot[:], stg[:], [Rstg], [Rc])
        S.dma("sp", stg[:], C["c_hgmask"], writes=[Rstg])
        cp("dve", hgmask[:], stg[:], [Rstg], [Rc])
        if 'lnp' not in SKIP:
            for src, dst in ((lng_d, lnG), (lnb_d, lnB)):
                S.dma("sp", stg[:], src, writes=[Rstg])
                b = ps_get()
                S.op("pe", "transpose", out=PSB[b][:, 0:128], in_=stg[:], identity=ident[:], reads=[Rstg, Rc], writes=[RPS[b]])
                cp("dve", dst[:], PSB[b][:, 0:128], [RPS[b]], [Rc])
        if 'par' not in SKIP:
            S.op("pool", "memset", ap=stg[:], constant=0.0, writes=[Rstg])
            S.dma("sp", stg[0:32, :], par_d, writes=[Rstg])
            b = ps_get()
            S.op("pe", "transpose", out=PSB[b][:, 0:128], in_=stg[:], identity=ident[:], reads=[Rstg, Rc], writes=[RPS[b]])
            cp("dve", par[:], PSB[b][:, 0:32], [RPS[b]], [Rc])
            S.op("pool", "memset", ap=lbp[:], constant=0.0, writes=[Rc])
            tmp8, Rtmp8 = ph.tile([128, 8])
            tt("dve", tmp8[:, 0:4], par[:, 14:18], par[:, 10:14], ALU.subtract, [Rc], [Rtmp8])
            act(lbp[:, 4:8], tmp8[:, 0:4], AF.Sigmoid, [Rtmp8], [Rc])
            ts("dve", lbp[:], lbp[:], 1e-30, None, ALU.max, None, [Rc], [Rc])
            ts("dve", oml[:], lbp[:], -1.0, 1.0, ALU.mult, ALU.add, [Rc], [Rc])
            ts("dve", noml[:], oml[:], -1.0, None, ALU.mult, None, [Rc], [Rc])
        if 'lam' not in SKIP:
            dl, Rdl = ph.tile([128, L * 256])
            S.dma("sp", dl[:], dlam_d.partition_broadcast(128), writes=[Rdl])
            pr, Rpr = ph.tile([128, 64])
            sm, Rsm = ph.tile([128, 4])
            for l in range(L):
                lam_init = 0.8 - 0.6 * math.exp(-0.3 * l)
                for i in range(2):
                    a0 = l * 256 + i * 128
                    tt("dve", pr[:], dl[:, a0:a0 + 64], dl[:, a0 + 64:a0 + 128], ALU.mult, [Rdl], [Rpr])
                    S.op("dve", "reduce_sum", out=sm[:, i:i + 1], in_=pr[:], axis=mybir.AxisListType.X, reads=[Rpr], writes=[Rsm])
                act(sm[:, 2:4], sm[:, 0:2], AF.Exp, [Rsm], [Rsm])
                tt("dve", nlam[:, l:l + 1], sm[:, 3:4], sm[:, 2:3], ALU.subtract, [Rsm], [Rc])
                ts("dve", nlam[:, l:l + 1], nlam[:, l:l + 1], -lam_init, None, ALU.add, None, [Rc], [Rc])
                ts("dve", gA[:, l:l + 1], par[:, l:l + 1], 1.0 - lam_init, None, ALU.mult, None, [Rc], [Rc])
        if 'tab' not in SKIP:
            rbB, RrbB = ph.tile([128, 128])
            S.dma("sp", rbB[:], relb_d.partition_broadcast(128), writes=[RrbB])
            cp("dve", chA[:], rbB[:, 124:128], [RrbB], [Rc])
            S.op("pool", "memset", ap=tabA[:], constant=0.0, writes=[Rc])
            ohb, Rohb = ph.tile([128, 8, 256])
            for g8 in range(4):
                S.dma("sp", ohb[:], C["c_oh"][:, g8 * 8:(g8 + 1) * 8, :], writes=[Rohb])
                for bb in range(8):
                    bk = g8 * 8 + bb
                    for h in range(4):
                        stt("dve", tabA[:, h, :], ohb[:, bb, :], rbB[:, bk * 4 + h:bk * 4 + h + 1], tabA[:, h, :],
                            ALU.mult, ALU.add, [Rohb, RrbB, Rc], [Rc])
            S.dma("sp", stg[:], C["c_negmask"], writes=[Rstg])
            for h in range(4):
                tt("dve", tabA[:, h, 0:128], tabA[:, h, 0:128], stg[:], ALU.add, [Rc, Rstg], [Rc])
        if 'gws' not in SKIP:
            tril, Rtril = ph.tile([128, 128])
            S.dma("sp", tril[:], C["c_tril"], writes=[Rtril])
            wst, Rwst = ph.tile([128, 128])
            for l in range(L):
                for g in range(4):
                    S.dma("sp", wst[:], gws_d[l, g], writes=[Rwst])
                    tt("dve", wst[:], wst[:], tril[:], ALU.mult, [Rwst, Rtril], [Rwst])
                    b = ps_get()
                    S.op("pe", "transpose", out=PSB[b][:, 0:128], in_=wst[:], identity=ident[:], reads=[Rwst, Rc], writes=[RPS[b]])
                    cp("dve", wsT[:, l * 4 + g, :], PSB[b][:, 0:128], [RPS[b]], [Rc])
        if 'bs2' not in SKIP:
            b2f, Rb2f = ph.tile([2, L * 512])
            b2h, Rb2h = ph.tile([2, L * 512], BF16)
            b2g, Rb2g = ph.tile([2, L * 512])
            S.dma("sp", b2f[:], gbs_d.partition_broadcast(2), writes=[Rb2f])
            cp("dve", b2h[:], b2f[:], [Rb2f], [Rb2h])
            cp("dve", b2g[:], b2h[:], [Rb2h], [Rb2g])
            tt("dve", b2f[:], b2f[:], b2g[:], ALU.subtract, [Rb2f, Rb2g], [Rb2f])
            ts("dve", b2g[:], b2g[:], colmask[0:2, 2:3], None, ALU.mult, None, [Rb2g, Rc], [Rb2g])
            stt("dve", bs2[:], b2f[:], colmask[0:2, 3:4], b2g[:], ALU.mult, ALU.add, [Rb2f, Rb2g, Rc], [Rc])
        prev_ev = ph.close()

        def ln_begin():
            b1 = ps_get(hold=True)
            b2 = ps_get(hold=True)
            return {"b1": b1, "b2": b2, "n": 0, "pending": None}

        def ln_flush(lnst):
            pend = lnst["pending"]
            if pend is not None:
                i, m, sq, Rsq = pend
                mm(PSB[lnst["b1"]][:], onesb[:], Xb[:, m, :], i == 0, i == KC - 1, [Rc, RXb[m]], [RPS[lnst["b1"]]])
                mm(PSB[lnst["b2"]][:], onesb[:], sq[:], i == 0, i == KC - 1, [Rc, Rsq], [RPS[lnst["b2"]]])
                lnst["pending"] = None

        def resid_chunk(lnst, m, Yap, Yreads, coef, sq2, Rsq2, extra_in1=None):
            if lnst is not None:
                ln_flush(lnst)
            stt("dve", X[:, m, :], Yap, coef, X[:, m, :], ALU.mult, ALU.add, Yreads + [RX[m]], [RX[m]])
            if lnst is not None:
                i = lnst["n"]
                sq, Rsq = sq2[i % 2], Rsq2[i % 2]
                act(sq[:], X[:, m, :], AF.Square, [RX[m]], [Rsq])
                cp("pool", Xb[:, m, :], X[:, m, :], [RX[m]], [RXb[m]])
                lnst["pending"] = (i, m, sq, Rsq)
                lnst["n"] += 1

        def ln_finish(lnst, l, i, P):
            ln_flush(lnst)
            b1, b2 = lnst["b1"], lnst["b2"]
            mean, Rmean = P.tile([128, T])
            rstd, Rrstd = P.tile([128, T])
            t1, Rt1 = P.tile([128, T])
            ts("dve", mean[:], PSB[b1][:], 1.0 / D, None, ALU.mult, None, [RPS[b1]], [Rmean])
            tt("pool", t1[:], mean[:], mean[:], ALU.mult, [Rmean], [Rt1])
            stt("dve", t1[:], PSB[b2][:], 1.0 / D, t1[:], ALU.mult, ALU.subtract, [RPS[b2], Rt1], [Rt1])
            act(t1[:], t1[:], AF.Ln, [Rt1], [Rt1], bias=eps_ap[:, 1:2])
            act(rstd[:], t1[:], AF.Exp, [Rt1], [Rrstd], scale=-0.5)
            ps_release(b1)
            ps_release(b2)
            for m in range(KC):
                col = (l * 4 + i) * KC + m
                e = "dve" if m % 2 == 0 else "pool"
                tt(e, X[:, m, :], X[:, m, :], mean[:], ALU.subtract, [RX[m], Rmean], [RX[m]])
                tt(e, X[:, m, :], X[:, m, :], rstd[:], ALU.mult, [RX[m], Rrstd], [RX[m]])
                act(X[:, m, :], X[:, m, :], AF.Identity, [RX[m], Rc], [RX[m]], scale=lnG[:, col:col + 1], bias=lnB[:, col:col + 1])
                cp("pool" if m % 2 == 0 else "dve", Xb[:, m, :], X[:, m, :], [RX[m]], [RXb[m]])

        def ffn(l, tagi, tago, lni, prev):
            P = Phase(prev)
            G, RG_ = P.tile([128, FH, T], BF16)
            RG = [Res() for _ in range(FH)]
            for r in RG:
                r.r = dict(prev)
            P.res.extend(RG)
            sg2 = [P.tile([128, T]) for _ in range(2)]
            sq2 = [P.tile([128, T], BF16) for _ in range(2)]
            lnst = None
            for hf in range(2):
                for jj in range(FH):
                    j = hf * FH + jj
                    slot, Rs = w_next(l, tagi, j)
                    s3 = slot[:].rearrange("p (kc n) -> p kc n", kc=KC)
                    bg = ps_get(); bu = ps_get()
                    for kc in range(KC):
                        mm(PSB[bg][:], s3[:, kc, 0:128], Xb[:, kc, :], kc == 0, kc == KC - 1, [Rs, RXb[kc]], [RPS[bg]])
                    for kc in range(KC):
                        mm(PSB[bu][:], s3[:, kc, 128:256], Xb[:, kc, :], kc == 0, kc == KC - 1, [Rs, RXb[kc]], [RPS[bu]])
                    sg, Rsg = sg2[jj % 2]
                    act(sg[:], PSB[bg][:], AF.Silu, [RPS[bg]], [Rsg])
                    tt("dve", G[:, jj, :], sg[:], PSB[bu][:], ALU.mult, [Rsg, RPS[bu]], [RG[jj]])
                if hf == 1:
                    lnst = ln_begin()
                for m in range(KC):
                    slot, Rs = w_next(l, tago, hf * 16 + m)
                    s3 = slot[:, 0:FH * 128].rearrange("p (fc n) -> p fc n", fc=FH)
                    by = ps_get()
                    for fc in range(FH):
                        mm(PSB[by][:], s3[:, fc, :], G[:, fc, :], fc == 0, fc == FH - 1, [Rs, RG[fc]], [RPS[by]])
                    resid_chunk(lnst, m, PSB[by][:], [RPS[by]], 0.5 / ALPHA,
                                [s[0] for s in sq2], [s[1] for s in sq2])
            ln_finish(lnst, l, lni, P)
            return P.close()

        def proj_fm(l, u):
            slot, Rs = w_next(l, "wmi", u)
            s3 = slot[:].rearrange("p (kc n) -> p kc n", kc=KC)
            for j in range(2):
                b = ps_get()
                for kc in range(KC):
                    mm(PSB[b][:], s3[:, kc, j * 128:(j + 1) * 128], Xb[:, kc, :], kc == 0, kc == KC - 1, [Rs, RXb[kc]], [RPS[b]])
                yield j, b

        def proj_tm(l, u):
            slot, Rs = w_next(l, "wmi", u)
            s3 = slot[:].rearrange("p (kc n) -> p kc n", kc=KC)
            for sub in range(4):
                b = ps_get()
                for kc in range(KC):
                    mm(PSB[b][:, 0:256], Xb[:, kc, sub * 128:(sub + 1) * 128], s3[:, kc, :], kc == 0, kc == KC - 1,
                       [Rs, RXb[kc]], [RPS[b]])
                yield sub, b

        SCALE_A = 64 ** -0.5

        def mixer_A(l, t, prev):
            P = Phase(prev)
            qT = [P.tile([128, T], BF16) for _ in range(4)]
            kpad = [P.tile([128, 2, T], BF16) for _ in range(4)]
            vtok, Rvtok = P.tile([128, 4, 512], BF16)
            kbuf = [P.tile([128, 2, T], BF16) for _ in range(3)]
            vbuf = [P.tile([128, 4, 128], BF16) for _ in range(3)]
            PT = [P.tile([128, T], BF16) for _ in range(4)]
            tmpd = [P.tile([128, 128]) for _ in range(2)]
            r0, Rr0 = P.tile([128, T]); t0, Rt0 = P.tile([128, T])
            r1, Rr1 = P.tile([128, T]); t1, Rt1 = P.tile([128, T])
            sqb, Rsqb = P.tile([128, T], BF16)
            for h in range(4):
                S.op("pool", "memset", ap=kpad[h][0][64:128, 0, :], constant=0.0, writes=[kpad[h][1]])
                S.op("pool", "memset", ap=kpad[h][0][0:64, 1, :], constant=0.0, writes=[kpad[h][1]])
            for u in (0, 1):
                for j, b in proj_fm(l, u):
                    h = u * 2 + j
                    cp("act", qT[h][0][:], PSB[b][:], [RPS[b]], [qT[h][1]])
            for u in (2, 3):
                for j, b in proj_fm(l, u):
                    h = (u - 2) * 2 + j
                    cp("act", kpad[h][0][0:64, 0, :], PSB[b][0:64, :], [RPS[b]], [kpad[h][1]])
                    cp("dve", kpad[h][0][64:128, 1, :], PSB[b][64:128, :], [RPS[b]], [kpad[h][1]])
            for u in (4, 5):
                for sub, b in proj_tm(l, u):
                    cp("act" if sub % 2 else "dve", vtok[:, sub, (u - 4) * 256:(u - 3) * 256], PSB[b][:, 0:256], [RPS[b]], [Rvtok])
            if t < NT - 1:
                for h in range(4):
                    S.dma("sp" if t == 0 else "pool", kcache[l, t, h], kpad[h][0][:].rearrange("p c n -> p (c n)"), reads=[kpad[h][1]], writes=[Rkc[l][t]])
                for h in range(4):
                    S.dma("sp" if t == 0 else "pool", vcache[l, t, h].rearrange("p (s n) -> p s n", s=4), vtok[:, :, h * 128:(h + 1) * 128], reads=[Rvtok], writes=[Rvc[l][t]])
            nb = 0
            npt = [0]
            PIPE = 2
            for h in range(4):
                acc = [ps_get(hold=True) for _ in range(4)]
                jobs = []
                for kt in range(t + 1):
                    if kt < t:
                        kb_t, Rkb = kbuf[nb % 3]
                        vb_t, Rvb = vbuf[nb % 3]
                        nb += 1
                        ld = (kb_t, Rkb, vb_t, Rvb, kt)
                        kview, vview, Rv_ = kb_t, vb_t[:], Rvb
                    else:
                        ld = None
                        kview, Rkb = kpad[h]
                        vview = vtok[:, :, h * 128:(h + 1) * 128]
                        Rv_ = Rvtok
                    for kb in range(4):
                        q0 = kb * 128 if kt == t else 0
                        for c in range(2):
                            jobs.append(dict(kt=kt, kb=kb, c=c, q0=q0, kview=kview, Rkb=Rkb, vview=vview, Rv=Rv_,
                                             first=(kt == 0 and kb == 0), last=(kt == t and kb == 3),
                                             ld=(ld if (kb == 0 and c == 0) else None)))

                def emit_scores(J):
                    if J["ld"] is not None:
                        kb_t, Rkb_, vb_t, Rvb_, kt_l = J["ld"]
                        S.dma("pool", kb_t[:].rearrange("p c n -> p (c n)"), kcache[l, kt_l, h], reads=[Rkc[l][kt_l]], writes=[Rkb_])
                        S.dma("pool", vb_t[:].rearrange("p s n -> p (s n)"), vcache[l, kt_l, h], reads=[Rvc[l][kt_l]], writes=[Rvb_])
                    kt, kb, c, q0 = J["kt"], J["kb"], J["c"], J["q0"]
                    b = ps_get()
                    mm(PSB[b][:, q0:T], J["kview"][:, c, kb * 128:(kb + 1) * 128], qT[h][0][:, q0:T], True, True,
                       [J["Rkb"], qT[h][1]], [RPS[b]])
                    pt, Rpt = PT[npt[0] % 4]
                    npt[0] += 1
                    far0 = None
                    for qb in range(q0 // 128, 4):
                        rel = (4 * t + qb) - (4 * kt + kb)
                        if rel >= 2:
                            if far0 is None:
                                far0 = qb
                            continue
                        td, Rtd = tmpd[(npt[0] + qb) % 2]
                        stt("dve", td[:], PSB[b][:, qb * 128:(qb + 1) * 128], SCALE_A,
                            tabA[:, h, rel * 128:(rel + 1) * 128], ALU.mult, ALU.add, [RPS[b], Rc], [Rtd])
                        act(pt[:, qb * 128:(qb + 1) * 128], td[:], AF.Exp, [Rtd], [Rpt])
                    if far0 is not None:
                        act(pt[:, far0 * 128:T], PSB[b][:, far0 * 128:T], AF.Exp, [RPS[b], Rc], [Rpt],
                            scale=SCALE_A, bias=chA[:, h:h + 1])
                    J["pt"] = (pt, Rpt)

                def emit_pv(J):
                    pt, Rpt = J["pt"]
                    kb, c, q0 = J["kb"], J["c"], J["q0"]
                    mm(PSB[acc[c]][:, q0:T], J["vview"][:, kb, :], pt[:, q0:T], J["first"], J["last"], [J["Rv"], Rpt], [RPS[acc[c]]], inc=False)
                    mm(PSB[acc[2 + c]][:, q0:T], onesb[:], pt[:, q0:T], J["first"], J["last"], [Rc, Rpt], [RPS[acc[2 + c]]], inc=True)

                pending = []
                for J in jobs:
                    emit_scores(J)
                    pending.append(J)
                    if len(pending) > PIPE:
                        emit_pv(pending.pop(0))
                while pending:
                    emit_pv(pending.pop(0))
                S.op("dve", "reciprocal", out=r0[:], in_=PSB[acc[2]][:], reads=[RPS[acc[2]]], writes=[Rr0])
                tt("dve", t0[:], PSB[acc[0]][:], r0[:], ALU.mult, [RPS[acc[0]], Rr0], [Rt0])
                S.op("dve", "reciprocal", out=r1[:], in_=PSB[acc[3]][:], reads=[RPS[acc[3]]], writes=[Rr1])
                tt("dve", t1[:], PSB[acc[1]][:], r1[:], ALU.mult, [RPS[acc[1]], Rr1], [Rt1])
                for a in acc:
                    ps_release(a)
                stt("dve", t0[:], t1[:], nlam[:, l:l + 1], t0[:], ALU.mult, ALU.add, [Rt1, Rt0, Rc], [Rt0])
                act(sqb[:], t0[:], AF.Square, [Rt0], [Rsqb])
                b = ps_get()
                mm(PSB[b][:], onesb[:], sqb[:], True, True, [Rc, Rsqb], [RPS[b]])
                rstd_from(r0[:], PSB[b][:], 1.0 / 128, [RPS[b]], [Rr0], r1[:], Rr1)
                stt("dve", mixT[:, h, :], t0[:], gA[:, l:l + 1], r0[:], ALU.mult, ALU.mult, [Rt0, Rr0, Rc], [Rmix[h]])
            return P.close()

        def rope_tables(t, prev):
            P = Phase(prev)
            pi_t, Rpi = P.tile([128, T], I32)
            ang, Rang = P.tile([128, T])
            w1, Rw1 = P.tile([128, T])
            ki, Rki = P.tile([128, T], I32)
            S.dma("sp" if t == 0 else "pool", pi_t[:], pos_d[:, t * T:(t + 1) * T].partition_broadcast(128), writes=[Rpi])
            cp("dve", ang[:], pi_t[:], [Rpi], [Rang])
            ts("dve", ang[:], ang[:], invd[:, 0:1], None, ALU.mult, None, [Rang, Rc], [Rang])
            for which, dst in ((0, sinT), (1, cosT)):
                src = ang
                if which == 1:
                    ts("dve", w1[:], ang[:], math.pi / 2, None, ALU.add, None, [Rang], [Rw1])
                    src = w1
                kf, Rkf = P.tile([128, T])
                ts("dve", kf[:], src[:], 1.0 / (2 * math.pi), None, ALU.mult, None, [Rang, Rw1], [Rkf])
                cp("dve", ki[:], kf[:], [Rkf], [Rki])
                cp("dve", kf[:], ki[:], [Rki], [Rkf])
                stt("dve", kf[:], kf[:], -2 * math.pi, src[:], ALU.mult, ALU.add, [Rkf, Rang, Rw1], [Rkf])
                ts("dve", kf[:], kf[:], 3.141592, -3.141592, ALU.min, ALU.max, [Rkf], [Rkf])
                act(dst[:], kf[:], AF.Sin, [Rkf], [Rcs])
            return P.close()

        def mixer_B(l, t, prev):
            P = Phase(prev)
            qt = [P.tile([128, T], BF16) for _ in range(4)]
            kt_ = [P.tile([128, T], BF16) for _ in range(4)]
            gs = [P.tile([128, T], BF16) for _ in range(4)]
            vtok, Rvtok = P.tile([128, 4, 512], BF16)
            P1 = Phase(prev)
            raw2 = [P1.tile([128, T], BF16) for _ in range(2)]
            a1, Ra1 = P1.tile([128, T]); a2, Ra2 = P1.tile([128, T])
            nraw = 0
            for (u0, dstl, goff) in ((6, qt, 0), (8, kt_, 512)):
                for u in (u0, u0 + 1):
                    for j, b in proj_fm(l, u):
                        h = (u - u0) * 2 + j
                        raw, Rraw = raw2[nraw % 2]
                        nraw += 1
                        cp("act", raw[:], PSB[b][:], [RPS[b]], [Rraw])
                        b2 = ps_get()
                        mm(PSB[b2][:], prot[:], raw[:], True, True, [Rc, Rraw], [RPS[b2]])
                        tt("dve", a1[:], raw[:], cosT[:], ALU.mult, [Rraw, Rcs], [Ra1])
                        tt("dve", a2[:], PSB[b2][:], sinT[:], ALU.mult, [RPS[b2], Rcs], [Ra2])
                        tt("pool", a1[:], a1[:], a2[:], ALU.add, [Ra1, Ra2], [Ra1])
                        tt("dve", dstl[h][0][:].rearrange("p (c n) -> p c n", c=4), a1[:].rearrange("p (c n) -> p c n", c=4),
                           bc(retc[:, goff + h * 128:goff + (h + 1) * 128], [128, 4, 128]), ALU.mult, [Ra1, Rc], [dstl[h][1]])
            ev1 = P1.close()
            for u in (10, 11):
                for sub, b in proj_tm(l, u):
                    cp("act" if sub % 2 else "dve", vtok[:, sub, (u - 10) * 256:(u - 9) * 256], PSB[b][:, 0:256], [RPS[b]], [Rvtok])
            for u in (12, 13):
                for j, b in proj_fm(l, u):
                    h = (u - 12) * 2 + j
                    act(gs[h][0][:], PSB[b][:], AF.Silu, [RPS[b]], [gs[h][1]])
            P2 = Phase(ev1)
            Pm = [P2.tile([128, 512], BF16) for _ in range(4)]
            ktok = [P2.tile([128, 4, 128], BF16) for _ in range(2)]
            Sbv = [P2.tile([128, 512], BF16) for _ in range(4)]
            tmpS, RtmpS = P2.tile([128, 512])
            ob, Rob = P2.tile([128, T], BF16); sqb, Rsqb = P2.tile([128, T], BF16)
            mean, Rmean = P2.tile([128, T]); var, Rvar = P2.tile([128, T]); dd, Rdd = P2.tile([128, T])
            for c in range(4):
                cs = slice(c * 128, (c + 1) * 128)
                b = ps_get()
                for h in range(4):
                    mm(PSB[b][:, h * 128:(h + 1) * 128], kt_[h][0][:, cs], qt[h][0][:, cs], True, True,
                       [kt_[h][1], qt[h][1]], [RPS[b]], inc=(h == 3))
                tt("dve", Pm[c][0][:].rearrange("p (h n) -> p h n", h=4), PSB[b][:].rearrange("p (h n) -> p h n", h=4),
                   bc(causT[:], [128, 4, 128]), ALU.mult, [RPS[b], Rc], [Pm[c][1]])
                b = ps_get()
                for h in range(4):
                    mm(PSB[b][:, h * 128:(h + 1) * 128], kt_[h][0][:, cs], identb[:], True, True, [kt_[h][1], Rc], [RPS[b]], inc=(h == 3))
                ktk, Rktk = ktok[c % 2]
                cp("act", ktk[:].rearrange("p h n -> p (h n)"), PSB[b][:], [RPS[b]], [Rktk])
                b = ps_get()
                for h in range(4):
                    mm(PSB[b][:, h * 128:(h + 1) * 128], ktk[:, h, :], vtok[:, c, h * 128:(h + 1) * 128], True, True,
                       [Rktk, Rvtok], [RPS[b]], inc=(h == 3))
                tt("dve", tmpS[:], PSB[b][:], S_ret[l][:], ALU.add, [RPS[b], RS_ret[l]], [RtmpS])
                tt("pool", S_ret[l][:], tmpS[:], retc[:, 1024:1536], ALU.mult, [RtmpS, Rc], [RS_ret[l]])
                if c < 3:
                    cp("act", Sbv[c + 1][0][:], S_ret[l][:], [RS_ret[l]], [Sbv[c + 1][1]])
            for h in range(4):
                hs = slice(h * 128, (h + 1) * 128)
                b = ps_get()
                for c in range(4):
                    cs = slice(c * 128, (c + 1) * 128)
                    mm(PSB[b][:, cs], vtok[:, c, hs], Pm[c][0][:, hs], True, False, [Rvtok, Pm[c][1]], [RPS[b]], inc=False)
                    if c == 0:
                        sbt, Rsbt = Sb_ret[l], RSb_ret[l]
                    else:
                        sbt, Rsbt = Sbv[c]
                    mm(PSB[b][:, cs], sbt[:, hs], qt[h][0][:, cs], False, True, [Rsbt, qt[h][1]], [RPS[b]], inc=(c == 3))
                cp("act", ob[:], PSB[b][:], [RPS[b]], [Rob])
                act(sqb[:], PSB[b][:], AF.Square, [RPS[b]], [Rsqb])
                b1 = ps_get(); b2 = ps_get()
                mm(PSB[b1][:], onesb[:], ob[:], True, True, [Rc, Rob], [RPS[b1]])
                mm(PSB[b2][:], onesb[:], sqb[:], True, True, [Rc, Rsqb], [RPS[b2]])
                ts("dve", mean[:], PSB[b1][:], 1.0 / 128, None, ALU.mult, None, [RPS[b1]], [Rmean])
                tt("pool", var[:], mean[:], mean[:], ALU.mult, [Rmean], [Rvar])
                stt("dve", var[:], PSB[b2][:], 1.0 / 128, var[:], ALU.mult, ALU.subtract, [RPS[b2], Rvar], [Rvar])
                act(var[:], var[:], AF.Ln, [Rvar, Rc], [Rvar], bias=eps_ap[:, 0:1])
                act(var[:], var[:], AF.Exp, [Rvar], [Rvar], scale=-0.5)
                tt("dve", dd[:], PSB[b][:], mean[:], ALU.subtract, [RPS[b], Rmean], [Rdd])
                tt("pool", dd[:], dd[:], var[:], ALU.mult, [Rdd, Rvar], [Rdd])
                tt("dve", mixT[:, 4 + h, :], dd[:], gs[h][0][:], ALU.mult, [Rdd, gs[h][1]], [Rmix[4 + h]])
            cp("act", Sb_ret[l][:], S_ret[l][:], [RS_ret[l]], [RSb_ret[l]])
            ev2 = P2.close()
            P.prev = ev2
            return P.close()

        def mixer_C(l, t, prev):
            P = Phase(prev)
            uT = [P.tile([128, T], BF16) for _ in range(4)]
            vg = [P.tile([128, 512]) for _ in range(4)]
            vnb = [P.tile([128, 512], BF16) for _ in range(4)]
            st6, Rst6 = P.tile([128, 8]); mv, Rmv = P.tile([128, 4])
            for u in (14, 15):
                for j, b in proj_fm(l, u):
                    g = (u - 14) * 2 + j
                    act(uT[g][0][:], PSB[b][:], AF.Gelu, [RPS[b]], [uT[g][1]])
            for u in (16, 17):
                for sub, b in proj_tm(l, u):
                    act(vg[sub][0][:, (u - 16) * 256:(u - 15) * 256], PSB[b][:, 0:256], AF.Gelu, [RPS[b]], [vg[sub][1]])
            for sub in range(4):
                v_, Rv_ = vg[sub]
                S.op("dve", "bn_stats", out=st6[:, 0:6], in_=v_[:], reads=[Rv_], writes=[Rst6])
                S.op("dve", "bn_aggr", out=mv[:, 0:2], in_=st6[:, 0:6], reads=[Rst6], writes=[Rmv])
                act(mv[:, 2:3], mv[:, 1:2], AF.Ln, [Rmv, Rc], [Rmv], bias=eps_ap[:, 0:1])
                act(mv[:, 3:4], mv[:, 2:3], AF.Exp, [Rmv], [Rmv], scale=-0.5)
                ts("dve", v_[:], v_[:], mv[:, 0:1], mv[:, 3:4], ALU.subtract, ALU.mult, [Rv_, Rmv], [Rv_])
                tt("pool", v_[:], v_[:], ggT[:, l * 512:(l + 1) * 512], ALU.mult, [Rv_, Rc], [Rv_])
                tt("dve", vnb[sub][0][:], v_[:], gbT[:, l * 512:(l + 1) * 512], ALU.add, [Rv_, Rc], [vnb[sub][1]])
            for g in range(4):
                gsl = slice(g * 128, (g + 1) * 128)
                b = ps_get()
                for sub in range(4):
                    cs = slice(sub * 128, (sub + 1) * 128)
                    mm(PSB[b][:, cs], vnb[sub][0][:, gsl], wsT[:, l * 4 + g, :], True, False, [vnb[sub][1], Rc], [RPS[b]], inc=False)
                    mm(PSB[b][:, cs], onesb[0:2, :], bs2[:, l * 512 + g * 128:l * 512 + (g + 1) * 128], False, True,
                       [Rc], [RPS[b]], inc=(sub == 3))
                tt("dve", mixT[:, 8 + g, :], uT[g][0][:], PSB[b][:], ALU.mult, [uT[g][1], RPS[b]], [Rmix[8 + g]])
            return P.close()

        def mixer_D(l, t, prev):
            P = Phase(prev)
            qt = [P.tile([128, T], BF16) for _ in range(4)]
            kt_ = [P.tile([128, T], BF16) for _ in range(4)]
            itok, Ritok = P.tile([128, 4, 512], BF16)
            E1s, RE1s = P.tile([128, 4, 8]); E2s, RE2s = P.tile([128, 4, 8])
            for u in (18, 19):
                for j, b in proj_fm(l, u):
                    h = (u - 18) * 2 + j
                    cp("act", qt[h][0][:], PSB[b][:], [RPS[b]], [qt[h][1]])
            P1 = Phase(prev)
            sg, Rsg = P1.tile([128, T]); keyp, Rkeyp = P1.tile([128, T])
            e1, Re1 = P1.tile([128, T]); e2, Re2 = P1.tile([128, T])
            logf, Rlogf = P1.tile([128, T])
            lft, Rlft = P1.tile([128, 4, 128])
            x8, Rx8 = P1.tile([128, 8])
            for u in (20, 21):
                for j, b in proj_fm(l, u):
                    h = (u - 20) * 2 + j
                    col = l * 4 + h
                    act(sg[:], PSB[b][:], AF.Sigmoid, [RPS[b]], [Rsg])
                    ts("dve", logf[:], sg[:], oml[:, col:col + 1], lbp[:, col:col + 1], ALU.mult, ALU.add, [Rsg, Rc], [Rlogf])
                    act(logf[:], logf[:], AF.Ln, [Rlogf], [Rlogf])
                    ts("dve", keyp[:], sg[:], noml[:, col:col + 1], oml[:, col:col + 1], ALU.mult, ALU.add, [Rsg, Rc], [Rkeyp])
                    bt = ps_get()
                    for c in range(4):
                        S.op("pe", "transpose", out=PSB[bt][:, c * 128:(c + 1) * 128], in_=logf[:, c * 128:(c + 1) * 128], identity=ident[:],
                             reads=[Rlogf, Rc], writes=[RPS[bt]], inc=(c == 3))
                    cp("dve", lft[:].rearrange("p c n -> p (c n)"), PSB[bt][:], [RPS[bt]], [Rlft])
                    bd_ = ps_get()
                    for c in range(4):
                        cs = slice(c * 128, (c + 1) * 128)
                        mm(PSB[bd_][:, cs], lft[:, c, :], ud[:], True, True, [Rlft, Rc], [RPS[bd_]], inc=(c == 3))
                    act(e2[:], PSB[bd_][:], AF.Exp, [RPS[bd_]], [Re2], scale=-1.0)
                    tt("dve", kt_[h][0][:], keyp[:], e2[:], ALU.mult, [Rkeyp, Re2], [kt_[h][1]])
                    act(e1[:], PSB[bd_][:], AF.Exp, [RPS[bd_]], [Re1])
                    tt("dve", qt[h][0][:], qt[h][0][:], e1[:], ALU.mult, [qt[h][1], Re1], [qt[h][1]])
                    e1v = e1[:].rearrange("p (cj n) -> p cj n", n=64)
                    lfv = logf[:].rearrange("p (cj n) -> p cj n", n=64)
                    cp("dve", E2s[:, h, :], e1v[:, :, 63], [Re1], [RE2s])
                    cp("dve", x8[:, 0:8], PSB[bd_][:].rearrange("p (cj n) -> p cj n", n=64)[:, :, 0], [RPS[bd_]], [Rx8])
                    tt("dve", x8[:, 0:8], lfv[:, :, 0], x8[:, 0:8], ALU.subtract, [Rlogf, Rx8], [Rx8])
                    act(E1s[:, h, :], x8[:, 0:8], AF.Exp, [Rx8], [RE1s])
            ev1 = P1.close()
            for u in (22, 23):
                for sub, b in proj_tm(l, u):
                    cp("act" if sub % 2 else "dve", itok[:, sub, (u - 22) * 256:(u - 21) * 256], PSB[b][:, 0:256], [RPS[b]], [Ritok])
            P2 = Phase(ev1)
            am32 = [P2.tile([128, 4, 128]) for _ in range(2)]
            gs = [P2.tile([128, T], BF16) for _ in range(4)]
            Am = [P2.tile([128, 4, 128], BF16) for _ in range(2)]
            ktok = [[P2.tile([128, 4, 128], BF16) for _ in range(2)] for _ in range(2)]
            Sbv = [[P2.tile([128, 4, 128], BF16) for _ in range(2)] for _ in range(2)]
            SE, RSE = P2.tile([128, 4, 128]); tmpS, RtmpS = P2.tile([128, 4, 128])
            sqb, Rsqb = P2.tile([128, T], BF16); rs, Rrs = P2.tile([128, T]); r2, Rr2 = P2.tile([128, T])
            for i2 in range(2):
                S.op("pool", "memset", ap=am32[i2][0][:], constant=0.0, writes=[am32[i2][1]])
            for u in (24, 25):
                for j, b in proj_fm(l, u):
                    h = (u - 24) * 2 + j
                    act(gs[h][0][:], PSB[b][:], AF.Silu, [RPS[b]], [gs[h][1]])
            S3 = S_hg[l][:].rearrange("p (h n) -> p h n", h=4)
            bo = [ps_get(hold=True) for _ in range(4)]
            for c in range(4):
                cs = slice(c * 128, (c + 1) * 128)
                am, Ram = Am[c % 2]
                b = ps_get()
                for h in range(4):
                    mm(PSB[b][:, h * 128:(h + 1) * 128], kt_[h][0][:, cs], qt[h][0][:, cs], True, True,
                       [kt_[h][1], qt[h][1]], [RPS[b]], inc=(h == 3))
                a32, Ra32 = am32[c % 2]
                for h in range(4):
                    S.op("dve", "copy_predicated", out=a32[:, h, :], mask=hgmask[:], data=PSB[b][:, h * 128:(h + 1) * 128],
                         reads=[RPS[b], Rc], writes=[Ra32])
                cp("act", am[:].rearrange("p h n -> p (h n)"), a32[:].rearrange("p h n -> p (h n)"), [Ra32], [Ram])
                b = ps_get()
                for h in range(4):
                    mm(PSB[b][:, h * 128:(h + 1) * 128], kt_[h][0][:, cs], identb[:], True, True, [kt_[h][1], Rc], [RPS[b]], inc=(h == 3))
                for j in range(2):
                    ktk, Rktk = ktok[c % 2][j]
                    ts("dve", ktk[:].rearrange("p h n -> p (h n)"), PSB[b][:], colmask[:, j:j + 1], None, ALU.mult, None,
                       [RPS[b], Rc], [Rktk])
                for j in range(2):
                    ktk, Rktk = ktok[c % 2][j]
                    sbv, Rsbv = Sbv[c % 2][j]
                    bd_ = ps_get()
                    for h in range(4):
                        mm(PSB[bd_][:, h * 128:(h + 1) * 128], ktk[:, h, :], itok[:, c, h * 128:(h + 1) * 128], True, True,
                           [Rktk, Ritok], [RPS[bd_]], inc=(h == 3))
                    cj = c * 2 + j
                    for h in range(4):
                        hs_ = slice(h * 128, (h + 1) * 128)
                        ts("dve", sbv[:, h, :], S_hg[l][:, hs_], E1s[:, h, cj:cj + 1], None, ALU.mult, None, [RS_hg[l], RE1s], [Rsbv])
                        stt("dve", tmpS[:, h, :], S_hg[l][:, hs_], E1s[:, h, cj:cj + 1], PSB[bd_][:, hs_], ALU.mult, ALU.add,
                            [RS_hg[l], RE1s, RPS[bd_]], [RtmpS])
                        ts("dve", S_hg[l][:, hs_], tmpS[:, h, :], E2s[:, h, cj:cj + 1], None, ALU.mult, None, [RtmpS, RE2s], [RS_hg[l]])
                for h in range(4):
                    hs = slice(h * 128, (h + 1) * 128)
                    mm(PSB[bo[h]][:, cs], itok[:, c, hs], am[:, h, :], True, False, [Ritok, Ram], [RPS[bo[h]]], inc=False)
                    for j in range(2):
                        js = slice(c * 128 + j * 64, c * 128 + (j + 1) * 64)
                        sbv, Rsbv = Sbv[c % 2][j]
                        mm(PSB[bo[h]][:, js], sbv[:, h, :], qt[h][0][:, js], False, j == 1, [Rsbv, qt[h][1]], [RPS[bo[h]]],
                           inc=(j == 1))
            bss = ps_get(hold=True)
            for h in range(4):
                b = bo[h]
                act(sqb[:], PSB[b][:], AF.Square, [RPS[b]], [Rsqb])
                mm(PSB[bss][:], onesb[:], sqb[:], h == 0, h == 3, [Rc, Rsqb], [RPS[bss]], inc=True)
            rstd_from(rs[:], PSB[bss][:], 1.0 / 512, [RPS[bss]], [Rrs], r2[:], Rr2)
            ps_release(bss)
            for h in range(4):
                col = l * 4 + h
                b = bo[h]
                stt("dve", r2[:], PSB[b][:], par[:, 2 + col:3 + col], rs[:], ALU.mult, ALU.mult, [RPS[b], Rrs, Rc], [Rr2])
                tt("dve", mixT[:, 12 + h, :], r2[:], gs[h][0][:], ALU.mult, [Rr2, gs[h][1]], [Rmix[12 + h]])
                ps_release(b)
            ev2 = P2.close()
            P.prev = ev2
            return P.close()

        def mix_out(l, prev):
            P = Phase(prev)
            sq2 = [P.tile([128, T], BF16) for _ in range(2)]
            lnst = ln_begin()
            for u in range(8):
                slot, Rs = w_next(l, "wmo", u)
                s3 = slot[:].rearrange("p (kc n) -> p kc n", kc=KC)
                for j in range(2):
                    m = u * 2 + j
                    b = ps_get()
                    for kc in range(KC):
                        mm(PSB[b][:], s3[:, kc, j * 128:(j + 1) * 128], mixT[:, kc, :], kc == 0, kc == KC - 1, [Rs, Rmix[kc]], [RPS[b]])
                    resid_chunk(lnst, m, PSB[b][:], [RPS[b]], 1.0 / ALPHA, [s[0] for s in sq2], [s[1] for s in sq2])
            ln_finish(lnst, l, 1, P)
            return P.close()

        def ple(l, t, prev):
            P = Phase(prev)
            sq2 = [P.tile([128, T], BF16) for _ in range(2)]
            pst, Rpst = P.tile([128, 4, PD])
            pT, RpT = P.tile([128, 2, T], BF16)
            gt2 = [P.tile([128, T]) for _ in range(2)]
            S.dma("sp" if t == 0 else "pool", pst[:], p_d[l, t * T:(t + 1) * T, :].rearrange("(s p) n -> p s n", p=128), writes=[Rpst])
            for kc2 in range(2):
                b = ps_get()
                for sub in range(4):
                    S.op("pe", "transpose", out=PSB[b][:, sub * 128:(sub + 1) * 128], in_=pst[:, sub, kc2 * 128:(kc2 + 1) * 128],
                         identity=ident[:], reads=[Rpst, Rc], writes=[RPS[b]], inc=(sub == 3))
                cp("dve", pT[:, kc2, :], PSB[b][:], [RPS[b]], [RpT])
            gates = []
            G16, RG16_ = P.tile([128, KC, T], BF16)
            RG16 = [Res() for _ in range(KC)]
            for r in RG16:
                r.r = dict(prev)
            P.res.extend(RG16)
            for u in range(8):
                slot, Rs = w_next(l, "wpg", u)
                s3 = slot[:].rearrange("p (kc n) -> p kc n", kc=KC)
                for j in range(2):
                    m = u * 2 + j
                    b = ps_get()
                    for kc in range(KC):
                        mm(PSB[b][:], s3[:, kc, j * 128:(j + 1) * 128], Xb[:, kc, :], kc == 0, kc == KC - 1, [Rs, RXb[kc]], [RPS[b]])
                    act(G16[:, m, :], PSB[b][:], AF.Sigmoid, [RPS[b]], [RG16[m]])
            slot, Rs = w_next(l, "wpp", 0)
            s3 = slot[:].rearrange("p (kc n) -> p kc n", kc=2)
            lnst = ln_begin()
            for m in range(KC):
                b = ps_get()
                for kc2 in range(2):
                    mm(PSB[b][:], s3[:, kc2, m * 128:(m + 1) * 128], pT[:, kc2, :], kc2 == 0, kc2 == 1, [Rs, RpT], [RPS[b]])
                g_, Rg_ = gt2[m % 2]
                tt("dve", g_[:], PSB[b][:], G16[:, m, :], ALU.mult, [RPS[b], RG16[m]], [Rg_])
                resid_chunk(lnst, m, g_[:], [Rg_], 1.0 / ALPHA, [s[0] for s in sq2], [s[1] for s in sq2])
            ln_finish(lnst, l, 3, P)
            return P.close()

        def load_x(t, prev):
            P = Phase(prev)
            stg2 = [P.tile([128, D]) for _ in range(2)]
            for sub in range(4):
                sg_, Rsg_ = stg2[sub % 2]
                S.dma("sp" if t == 0 else "pool", sg_[:], x_d[t * T + sub * 128:t * T + (sub + 1) * 128, :], writes=[Rsg_])
                for g4 in range(4):
                    b = ps_get()
                    for j in range(4):
                        kc = g4 * 4 + j
                        S.op("pe", "transpose", out=PSB[b][:, j * 128:(j + 1) * 128], in_=sg_[:, kc * 128:(kc + 1) * 128],
                             identity=ident[:], reads=[Rsg_, Rc], writes=[RPS[b]], inc=(j == 3))
                    rx = RX[g4 * 4:g4 * 4 + 4]
                    rxb = RXb[g4 * 4:g4 * 4 + 4]
                    cp("dve", X[:, g4 * 4:g4 * 4 + 4, sub * 128:(sub + 1) * 128], PSB[b][:].rearrange("p (a n) -> p a n", a=4), [RPS[b]], rx)
                    cp("pool", Xb[:, g4 * 4:g4 * 4 + 4, sub * 128:(sub + 1) * 128], X[:, g4 * 4:g4 * 4 + 4, sub * 128:(sub + 1) * 128], rx, rxb)
            return P.close()

        def store_y(t, prev):
            P = Phase(prev)
            stg2 = [P.tile([128, D]) for _ in range(2)]
            for sub in range(4):
                sg_, Rsg_ = stg2[sub % 2]
                for g4 in range(4):
                    b = ps_get()
                    for j in range(4):
                        kc = g4 * 4 + j
                        S.op("pe", "transpose", out=PSB[b][:, j * 128:(j + 1) * 128], in_=X[:, kc, sub * 128:(sub + 1) * 128],
                             identity=ident[:], reads=[RX[kc], Rc], writes=[RPS[b]], inc=(j == 3))
                    cp("act" if g4 % 2 else "dve", sg_[:, g4 * 512:(g4 + 1) * 512], PSB[b][:], [RPS[b]], [Rsg_])
                S.dma("pool", y_d[t * T + sub * 128:t * T + (sub + 1) * 128, :], sg_[:], reads=[Rsg_], writes=[Ry])
            return P.close()

        ev = prev_ev
        for t in range(NT):
            if 'rope' not in SKIP:
                ev = rope_tables(t, ev)
            if 'loadx' not in SKIP:
                ev = load_x(t, ev)
            for l in range(nlayers):
                if stop >= 1:
                    ev = ffn(l, "w1i", "w1o", 0, ev)
                else:
                    for _ in range(FC + 32):
                        st["next"] += 1
                stages = [(2, mixer_A, 6), (3, mixer_B, 8), (4, mixer_C, 4), (5, mixer_D, 8)]
                for sid, fn, nun in stages:
                    if stop >= sid:
                        ev = fn(l, t, ev)
                    else:
                        st["next"] += nun
                if dumpmix:
                    for m in range(KC):
                        cp("dve", X[:, m, :], mixT[:, m, :], [Rmix[m]], [RX[m]])
                if stop >= 6:
                    ev = mix_out(l, ev)
                else:
                    st["next"] += 8
                if stop >= 7:
                    ev = ffn(l, "w2i", "w2o", 2, ev)
                else:
                    st["next"] += FC + 32
                if stop >= 8:
                    ev = ple(l, t, ev)
                else:
                    st["next"] += 9
            for _ in range((L - nlayers) * NU):
                st["next"] += 1
            if 'storey' not in SKIP:
                ev = store_y(t, ev)
        S.finish("pool", [Ry])
        S.finish("sp", Rring)
        S.emit(block)
    return nc


def make_in_map(inputs, b, S_len):
    f32 = np.float32
    m = {}
    m["x"] = np.ascontiguousarray(inputs["x"][b, :S_len])
    m["p"] = np.ascontiguousarray(inputs["p"][:, b, :S_len])
    m["pos"] = np.ascontiguousarray(inputs["positions"][b:b + 1, :S_len]).astype(np.int32)
    m["w1i"] = inputs["ffn1_w_in"]; m["w1o"] = inputs["ffn1_w_out"]
    m["wmi"] = inputs["w_mix_in"]; m["wmo"] = inputs["w_mix_out"]
    m["w2i"] = inputs["ffn2_w_in"]; m["w2o"] = inputs["ffn2_w_out"]
    m["wpg"] = inputs["ple_w_gate"]; m["wpp"] = inputs["ple_w_proj"]
    m["relb"] = np.ascontiguousarray(inputs["rel_bias"]).reshape(1, 128)
    m["dlam"] = np.ascontiguousarray(inputs["diff_lambda"]).reshape(1, -1)
    par = np.zeros((32, 128), f32)
    par[0:2] = inputs["diff_norm_g"]
    par[2:10] = np.ascontiguousarray(inputs["hgrn_norm_g"]).reshape(8, 128)
    par[10:18] = np.ascontiguousarray(inputs["hgrn_lb_logits"]).reshape(8, 128)
    m["par"] = par
    m["gg"] = np.ascontiguousarray(inputs["gmlp_ln_g"]).reshape(1, -1)
    m["gb"] = np.ascontiguousarray(inputs["gmlp_ln_b"]).reshape(1, -1)
    m["gws"] = np.ascontiguousarray(inputs["gmlp_w_s"])
    m["gbs"] = np.ascontiguousarray(inputs["gmlp_b_s"]).reshape(1, -1)
    m["lng"] = np.ascontiguousarray(inputs["ln_g"]).reshape(128, 128)
    m["lnb"] = np.ascontiguousarray(inputs["ln_b"]).reshape(128, 128)
    m.update(host_consts())
    return {k: np.ascontiguousarray(v) for k, v in m.items()}


def kernel(**inputs):
    inputs = {k: np.asarray(v) for k, v in inputs.items()}
    B, S_len = inputs["x"].shape[:2]
    F = inputs["ffn1_w_out"].shape[1]
    nc = build(S_len, F)
    in_maps = [make_in_map(inputs, b, S_len) for b in range(B)]
    res = run_bass_kernel_spmd(nc, in_maps, core_ids=list(range(B)))
    out = np.stack([np.asarray(res.results[b]["y"]) for b in range(B)], axis=0)
    return out.astype(np.float32)
```
